# Optimizing a Trainium2 kernel written in Bass

```python
import jax, jax.numpy as jnp
from jax import lax
import numpy as np

D_MODEL = 1024
BATCH = 8
SEQ = 2048
DEPTH = 2
DEC_BATCH = 128
DEC_SEQ = 1
PAST_LEN = 16384
PAGE_SIZE = 128

GLA_HEADS = 4
GLA_DK = D_MODEL // 8
GLA_DV = D_MODEL // 4
GLA_RANK = 16
GLA_TAU = 16.0
GLA_CHUNK = 64
QK_TOT = GLA_HEADS * GLA_DK
V_TOT = GLA_HEADS * GLA_DV
CONV_WIDTH = D_MODEL // 2
CONV_K = 3
ALPHA = (2 * DEPTH) ** 0.25
BETA = (8 * DEPTH) ** -0.25
EPS = 1e-5
SPLITS = (QK_TOT, QK_TOT, V_TOT, V_TOT, GLA_RANK, CONV_WIDTH, CONV_WIDTH, CONV_WIDTH, CONV_WIDTH, D_MODEL, D_MODEL)
VALUE_SLOTS = (2, 7)
N_IN = sum(SPLITS)

kernel_name = "gla_shortconv_gated_merge_deepnorm_step"


def _split_in(z):
    out = []
    off = 0
    for n in SPLITS:
        out.append(z[..., off:off + n])
        off += n
    return out


def _layernorm(x, g, b):
    xf = x.astype(jnp.float32)
    mu = jnp.mean(xf, axis=-1, keepdims=True)
    var = jnp.mean(jnp.square(xf - mu), axis=-1, keepdims=True)
    return ((xf - mu) * lax.rsqrt(var + EPS) * g + b).astype(x.dtype)


def _head_rmsnorm(o, g):
    return o * lax.rsqrt(jnp.mean(jnp.square(o), axis=-1, keepdims=True) + EPS) * g


def _gla_chunked(q, k, v, log_a, s0):
    bsz, L = q.shape[0], q.shape[1]
    C = min(GLA_CHUNK, L)
    n = -(-L // C)
    pad = n * C - L

    def prep(t):
        t = jnp.pad(t.astype(jnp.float32), ((0, 0), (0, pad), (0, 0), (0, 0)))
        return t.reshape(bsz, n, C, t.shape[2], t.shape[3]).transpose(0, 3, 1, 2, 4)

    q, k, v, g = prep(q), prep(k), prep(v), prep(log_a)
    q = q * (GLA_DK ** -0.5)
    b = jnp.cumsum(g, axis=3)
    b_last = b[:, :, :, -1:, :]
    q_in = q * jnp.exp(b)
    k_in = k * jnp.exp(-b)
    k_st = k * jnp.exp(b_last - b)
    mask = jnp.tril(jnp.ones((C, C), dtype=bool))
    att = jnp.where(mask, jnp.einsum('bhnik,bhnjk->bhnij', q_in, k_in), 0.0)
    o_intra = jnp.einsum('bhnij,bhnjv->bhniv', att, v)
    d_state = jnp.einsum('bhnjk,bhnjv->bhnkv', k_st, v)
    decay = jnp.exp(b_last[:, :, :, 0, :])

    def step(S, inp):
        d, ds = inp
        return d[..., None] * S + ds, S

    s_fin, s_prev = lax.scan(step, s0.astype(jnp.float32),
                             (jnp.moveaxis(decay, 2, 0), jnp.moveaxis(d_state, 2, 0)))
    s_prev = jnp.moveaxis(s_prev, 0, 2)
    o = o_intra + jnp.einsum('bhnik,bhnkv->bhniv', q_in, s_prev)
    o = o.transpose(0, 2, 3, 1, 4).reshape(bsz, n * C, GLA_HEADS, GLA_DV)[:, :L]
    return o, s_fin


def _short_conv(z, buf, w):
    L = z.shape[1]
    zz = jnp.concatenate([buf.astype(z.dtype), z], axis=1)
    y = zz[:, 0:L] * w[0]
    for i in range(1, CONV_K):
        y = y + zz[:, i:i + L] * w[i]
    return y, zz[:, -(CONV_K - 1):]


def _layer(x, c, s_gla, s_conv, w_ada, b_ada, w_in, w_a2, b_a, gla_norm_g, conv_w, w_pa, w_pb, w_o, ln_g, ln_b):
    bsz, L, _ = x.shape
    shift, scale, gate = jnp.split(c @ w_ada + b_ada, 3, axis=-1)
    u = x * (1.0 + scale[:, None, :]) + shift[:, None, :]
    (q, k, v, g_gla, a_lr, cb, cc, ch, g_conv, m_gla, m_conv) = _split_in(u @ w_in)
    log_a = jax.nn.log_sigmoid((a_lr @ w_a2 + b_a).astype(jnp.float32)) / GLA_TAU
    hk = lambda t: t.reshape(bsz, L, GLA_HEADS, -1)
    o, s_gla_new = _gla_chunked(hk(q), hk(k), hk(v), hk(log_a), s_gla)
    o = _head_rmsnorm(o, gla_norm_g).reshape(bsz, L, V_TOT).astype(x.dtype) * jax.nn.silu(g_gla)
    yc, s_conv_new = _short_conv(cc * ch, s_conv, conv_w)
    yc = cb * yc * jax.nn.silu(g_conv)
    merged = jax.nn.sigmoid(m_gla) * (o @ w_pa) + jax.nn.sigmoid(m_conv) * (yc @ w_pb)
    out = merged @ w_o
    y = _layernorm(ALPHA * x + gate[:, None, :] * out, ln_g, ln_b)
    return y, s_gla_new, s_conv_new


def setup_inputs(seed: int = 0) -> dict:
    key = jax.random.key(seed)
    ks = jax.random.split(key, 20)
    f32 = jnp.float32
    nrm = lambda kk, shape, s: jax.random.normal(kk, shape, f32) * s
    col_scale = jnp.concatenate([jnp.full((n,), BETA if i in VALUE_SLOTS else 1.0, f32)
                                 for i, n in enumerate(SPLITS)])
    return {
        "x_prompt": nrm(ks[0], (BATCH, SEQ, D_MODEL), 1.0),
        "x_sample": nrm(ks[1], (DEC_BATCH, DEC_SEQ, D_MODEL), 1.0),
        "c_prompt": nrm(ks[2], (BATCH, D_MODEL), 1.0),
        "c_sample": nrm(ks[3], (DEC_BATCH, D_MODEL), 1.0),
        "state_gla": nrm(ks[4], (DEPTH, DEC_BATCH, GLA_HEADS, GLA_DK, GLA_DV), 0.5),
        "state_conv": nrm(ks[5], (DEPTH, DEC_BATCH, CONV_K - 1, CONV_WIDTH), 1.0),
        "w_ada": nrm(ks[6], (DEPTH, D_MODEL, 3 * D_MODEL), 0.5 * D_MODEL ** -0.5),
        "b_ada": nrm(ks[7], (DEPTH, 3 * D_MODEL), 0.02),
        "w_in": nrm(ks[8], (DEPTH, D_MODEL, N_IN), D_MODEL ** -0.5) * col_scale,
        "w_a2": nrm(ks[9], (DEPTH, GLA_RANK, QK_TOT), GLA_RANK ** -0.5),
        "b_a": nrm(ks[10], (DEPTH, QK_TOT), 0.1),
        "gla_norm_g": 1.0 + nrm(ks[11], (DEPTH, GLA_DV), 0.02),
        "conv_w": nrm(ks[12], (DEPTH, CONV_K, CONV_WIDTH), CONV_K ** -0.5),
        "w_pa": nrm(ks[13], (DEPTH, V_TOT, D_MODEL), BETA * V_TOT ** -0.5),
        "w_pb": nrm(ks[14], (DEPTH, CONV_WIDTH, D_MODEL), BETA * CONV_WIDTH ** -0.5),
        "w_o": nrm(ks[15], (DEPTH, D_MODEL, D_MODEL), BETA * D_MODEL ** -0.5),
        "ln_g": 1.0 + nrm(ks[16], (DEPTH, D_MODEL), 0.02),
        "ln_b": nrm(ks[17], (DEPTH, D_MODEL), 0.02),
    }


def reference(x_prompt, x_sample, c_prompt, c_sample, state_gla, state_conv, w_ada, b_ada, w_in, w_a2, b_a,
              gla_norm_g, conv_w, w_pa, w_pb, w_o, ln_g, ln_b):
    hp, hs = x_prompt, x_sample
    bp = x_prompt.shape[0]
    gla_p, conv_p, gla_s, conv_s = [], [], [], []
    for l in range(DEPTH):
        wl = (w_ada[l], b_ada[l], w_in[l], w_a2[l], b_a[l], gla_norm_g[l], conv_w[l],
              w_pa[l], w_pb[l], w_o[l], ln_g[l], ln_b[l])
        s0 = jnp.zeros((bp, GLA_HEADS, GLA_DK, GLA_DV), jnp.float32)
        buf0 = jnp.zeros((bp, CONV_K - 1, CONV_WIDTH), x_prompt.dtype)
        hp, sg, sc = _layer(hp, c_prompt, s0, buf0, *wl)
        gla_p.append(sg)
        conv_p.append(sc)
        hs, sg2, sc2 = _layer(hs, c_sample, state_gla[l], state_conv[l], *wl)
        gla_s.append(sg2)
        conv_s.append(sc2)
    return (hp, hs, jnp.stack(gla_p), jnp.stack(conv_p), jnp.stack(gla_s), jnp.stack(conv_s))
```

```python
import os
import numpy as np
from contextlib import ExitStack
import concourse.bass as bass
import concourse.mybir as mybir
from concourse.bass_utils import run_bass_kernel_spmd

F32 = mybir.dt.float32
BF16 = mybir.dt.bfloat16
AF = mybir.ActivationFunctionType
ALU = mybir.AluOpType

ENGS = ("pe", "act", "dve", "pool", "sp")

D = 1024
SEQ = 2048
DEPTH = 2
NS = 16
H = 4
DK = 128
DV = 256
NKC = 8
BT = 512
GT = 4
NG = SEQ // BT
TW = BT + NS
SG = 0
ALPHA = (2 * DEPTH) ** 0.25
EPS = 1e-5
QSCALE = DK ** -0.5
O_Q, O_K, O_V, O_GG, O_ALR, O_CB, O_CC, O_CH, O_GC, O_MG, O_MC = 0, 512, 1024, 2048, 3072, 3088, 3600, 4112, 4624, 5136, 6160
NW = 8
WC = 256


class Ins:
    __slots__ = ("eng", "fn", "deps", "dma", "sig", "semid", "semval", "pos")

    def __init__(self, eng, fn, dma, pos):
        self.eng = eng
        self.fn = fn
        self.deps = []
        self.dma = dma
        self.sig = False
        self.semid = None
        self.semval = None
        self.pos = pos


class Prog:
    def __init__(self, n_dma_sems=8):
        self.lists = {e: [] for e in ENGS}
        self.last_w = {}
        self.rd_eng = {}
        self.rd_dma = {}
        self.n_dma_sems = n_dma_sems

    def add(self, eng, fn, reads=(), writes=(), excl=(), dma=False):
        I = Ins(eng, fn, dma, len(self.lists[eng]))
        cand = []
        writes = list(writes) + list(excl)
        for k in reads:
            w = self.last_w.get(k)
            if w is not None:
                cand.append((w, True))
        for k in writes:
            w = self.last_w.get(k)
            if w is not None:
                cand.append((w, False))
            for r in self.rd_eng.get(k, {}).values():
                cand.append((r, False))
            for r in self.rd_dma.get(k, ()):
                cand.append((r, False))
        for k in writes:
            self.last_w[k] = I
            self.rd_eng[k] = {}
            self.rd_dma[k] = []
        for k in reads:
            if dma:
                self.rd_dma.setdefault(k, []).append(I)
            else:
                self.rd_eng.setdefault(k, {})[eng] = I
        best = {}
        seen = set()
        for d, raw in cand:
            if d is I:
                continue
            if d.dma:
                if id(d) not in seen:
                    seen.add(id(d))
                    I.deps.append(d)
                continue
            if d.eng == eng and not dma:
                if eng == "pe":
                    continue
            b = best.get(d.eng)
            if b is None or d.pos > b.pos:
                best[d.eng] = d
        for d in best.values():
            d.sig = True
            I.deps.append(d)
        self.lists[eng].append(I)
        return I

    def finalize(self):
        for e in ENGS:
            cnt = 0
            dcnt = {}
            k = 0
            for I in self.lists[e]:
                if I.dma:
                    I.semid = k % self.n_dma_sems
                    k += 1
                    dcnt[I.semid] = dcnt.get(I.semid, 0) + 16
                    I.semval = dcnt[I.semid]
                elif I.sig:
                    cnt += 1
                    I.semval = cnt

    def run_engine(self, e, engobj, sems, dma_sems):
        waited = {}
        for I in self.lists[e]:
            need = {}
            for d in I.deps:
                if d.dma:
                    key = ("d", d.eng, d.semid)
                    s = dma_sems[d.eng][d.semid]
                else:
                    key = ("e", d.eng)
                    s = sems[d.eng]
                if need.get(key, (None, 0))[1] < d.semval:
                    need[key] = (s, d.semval)
            if I.dma and I.semval > 16:
                key = ("d", e, I.semid)
                if need.get(key, (None, 0))[1] < I.semval - 16:
                    need[key] = (dma_sems[e][I.semid], I.semval - 16)
            for key, (s, v) in need.items():
                if waited.get(key, 0) < v:
                    engobj.wait_ge(s, v)
                    waited[key] = v
            ins = getattr(engobj, I.fn[0])(**I.fn[1])
            if I.dma:
                ins.then_inc(dma_sems[e][I.semid], 16)
            elif I.sig:
                ins.then_inc(sems[e], 1)

    def final_wait(self, engobj, dma_sems):
        for e in ENGS:
            last = {}
            for I in self.lists[e]:
                if I.dma:
                    last[I.semid] = I.semval
            for sid, v in last.items():
                engobj.wait_ge(dma_sems[e][sid], v)


def build_program(ng_run=NG, nl_run=DEPTH):
    nc = bass.Bass("TRN2", target_bir_lowering=False)

    def din(name, shape):
        return nc.dram_tensor(name, list(shape), F32, kind="ExternalInput").ap()

    def dout(name, shape):
        return nc.dram_tensor(name, list(shape), F32, kind="ExternalOutput").ap()

    xp = din("xp", [SEQ, D])
    xs = din("xs", [NS, D])
    c17d = din("c17", [NS + 1, D])
    sgl = din("sgl", [DEPTH, NS, H, DK, DV])
    scv = din("scv", [DEPTH, NS, 2, 512])
    w_ada = din("w_ada", [DEPTH, D, 3 * D])
    b_ada = din("b_ada", [DEPTH, 3 * D])
    w_in = din("w_in", [DEPTH, D, 7184])
    w_a2 = din("w_a2", [DEPTH, 16, 512])
    b_a = din("b_a", [DEPTH, 512])
    gng = din("gng", [DEPTH, DV])
    conv_w = din("conv_w", [DEPTH, 3, 512])
    w_pa = din("w_pa", [DEPTH, D, D])
    w_pb = din("w_pb", [DEPTH, 512, D])
    w_o = din("w_o", [DEPTH, D, D])
    ln_g = din("ln_g", [DEPTH, D])
    ln_b = din("ln_b", [DEPTH, D])
    yp = dout("yp", [SEQ, D])
    ys = dout("ys", [NS, D])
    gp = dout("gp", [DEPTH, H, DK, DV])
    cp = dout("cp", [DEPTH, 2, 512])
    gs = dout("gs", [DEPTH, NS, H, DK, DV])
    cs = dout("cs", [DEPTH, NS, 2, 512])

    with ExitStack() as es:
        def sb(name, shape, dt):
            return es.enter_context(nc.sbuf_tensor(name, list(shape), dt))

        x_sb = sb("x_sb", [128, GT, D], F32)
        xs_sb = sb("xs_sb", [NS, D], F32)
        uT = sb("uT", [128, NKC, TW], BF16)
        og = sb("og", [128, NKC, TW], BF16)
        yc = sb("yc", [128, 4, TW], BF16)
        wring = sb("wring", [128, NW, NKC, WC], BF16)
        gate_bc = sb("gate_bc", [128, DEPTH, D], F32)
        gate_s = sb("gate_s", [NS, DEPTH, D], F32)
        lng_bc = sb("lng_bc", [128, D], F32)
        lnb_bc = sb("lnb_bc", [128, D], F32)
        S = sb("S", [128, DEPTH, H * DV], F32)
        S_bf = sb("S_bf", [128, DEPTH, H * DV], BF16)
        Sbuf = sb("Sbuf", [128, 3, H * DV], F32)
        sbf = sb("sbf", [128, 2, H * DV], BF16)
        ident_f = sb("ident_f", [128, 128], F32)
        ident_b = sb("ident_b", [128, 128], BF16)
        tri = sb("tri", [128, 128], F32)
        mask4 = sb("mask4", [128, 4, 128], F32)
        ones_b = sb("ones_b", [128, 128], BF16)
        alr_aug = sb("alr_aug", [17, TW], F32)
        w_a2aug = sb("w_a2aug", [17, DEPTH, 512], F32)
        gnorm = sb("gnorm", [128, DEPTH, 2], F32)
        convw = sb("convw", [128, DEPTH, 4, 3], F32)
        ada_sb = sb("ada_sb", [128, DEPTH, 16, 17], F32)
        b_adaT = sb("b_adaT", [128, DEPTH, 24], F32)
        carry = sb("carry", [128, DEPTH, 4, 2], F32)
        cT = sb("cT", [128, NKC, 17], BF16)
        tmp_s = sb("tmp_s", [128, NKC, NS], F32)
        st = sb("st", [128, 2, 6], F32)
        mv = sb("mv", [128, 8], F32)
        st_s = sb("st_s", [NS, 2, 6], F32)
        mv_s = sb("mv_s", [NS, 8], F32)
        dummy = sb("dummy_t", [128, 8], F32)
        ARENA_F = 16384 + 256 + 1024
        arena = sb("arena", [128, ARENA_F], F32)

        ps = es.enter_context(nc.psum_tensor("ps", [128, 8, 512], F32))

        sems = {e: es.enter_context(nc.semaphore(f"s_{e}")) for e in ENGS}
        dma_sems = {e: [es.enter_context(nc.semaphore(f"d_{e}{i}")) for i in range(8)]
                    for e in ("sp", "pool")}
        block = es.enter_context(nc.Block())

        class Arena:
            def __init__(self):
                self.off = 0

            def f32(self, n, parts=128):
                a = arena[0:parts, self.off:self.off + n]
                self.off += n
                assert self.off <= ARENA_F, self.off
                return a

            def bf16(self, n, parts=128):
                nf = (n + 1) // 2
                a = arena[0:parts, self.off:self.off + nf].bitcast(BF16)
                self.off += nf
                assert self.off <= ARENA_F, self.off
                return a

        def v3(ap, c):
            return ap.rearrange("p (c t) -> p c t", c=c)

        A = Arena()
        E_q = v3(A.f32(4 * BT), 4)
        E_k = v3(A.f32(4 * BT), 4)
        q_in = v3(A.bf16(4 * BT), 4)
        k_in = v3(A.bf16(4 * BT), 4)
        sgt = v3(A.bf16(8 * TW), 8)
        v_tok = v3(A.bf16(GT * 1024), GT)
        k_tok = v3(A.bf16(2 * 512), 2)
        _o = A.off
        e1 = A.f32(512)
        A.off = _o
        ktok_s = A.f32(512, NS)
        sp = v3(A.f32(2 * 512), 2)
        v_s = A.bf16(1024, NS)
        attm = v3(A.bf16(2 * 512), 2)
        sq = v3(A.bf16(8 * 128), 8)
        lnv = A.f32(512)
        rstd = v3(A.f32(512), 4)
        t1 = v3(A.f32(1024), 8)
        q_s = v3(A.bf16(4 * NS), 4)
        a_s = v3(A.f32(4 * NS), 4)
        km = A.bf16(512, NS)
        sq_s = v3(A.bf16(8 * NS), 8)
        lnv_s = A.f32(4 * NS)
        rstd_s = v3(A.f32(4 * NS), 4)
        t1_s = v3(A.f32(8 * NS), 8)
        g_end = A.off
        A = Arena()
        c17 = A.f32(D, NS + 1)
        cTb = v3(A.bf16(NKC * 128), NKC)
        bgate = A.f32(D)
        A = Arena()
        ccy = v3(A.f32(4 * TW), 4)
        pbuf = v3(A.f32(4 * (BT + 2)), 4)
        tmp1 = A.f32(512)
        tmp2 = A.f32(512)
        tmp3 = A.f32(512)
        p_s = v3(A.f32(4 * NS), 4)
        scT = v3(A.f32(4 * 2 * NS), 4)
        sc_tok = A.f32(512, 2 * NS)
        tcs1 = A.f32(NS)
        tcs2 = A.f32(NS)
        tmp3s = v3(A.f32(4 * NS), 4)
        cst = A.f32(512, 2)
        pst = A.f32(512, NS)
        sgm = A.f32(512)
        scm = A.f32(512)
        tm1 = A.f32(512)
        tm2 = A.f32(512)
        sg_s = A.f32(2 * NS)
        t_s = A.f32(2 * NS)
        mg = v3(A.bf16(NKC * TW), NKC)
        to = A.f32(D)
        to_s = A.f32(D, NS)
        cmo_end = A.off

        def emit_all(P, record, wlist_in):
            def PE(meth, kw, r=(), x=()):
                return P.add("pe", (meth, kw), reads=r, excl=x)

            def ACT(meth, kw, r=(), w=(), x=()):
                return P.add("act", (meth, kw), reads=r, writes=w, excl=x)

            def DVE(meth, kw, r=(), w=(), x=()):
                return P.add("dve", (meth, kw), reads=r, writes=w, excl=x)

            def POOL(meth, kw, r=(), w=()):
                return P.add("pool", (meth, kw), reads=r, writes=w)

            def DMA(q, meth, kw, r=(), w=()):
                return P.add(q, (meth, kw), reads=r, writes=w, dma=True)

            EP = "EPOCH"

            def ar(*keys):
                return [EP] + ["A:" + k for k in keys]

            def aw(*keys):
                return ["A:" + k for k in keys]

            def epoch_barrier():
                DVE("memset", dict(ap=dummy[:, 0:1], constant=0.0), w=[EP])

            free_banks = list(range(8))

            def bget():
                assert free_banks, "PSUM banks exhausted"
                return free_banks.pop(0)

            def bget2():
                for i in range(len(free_banks)):
                    b = free_banks[i]
                    if b % 2 == 0 and (b + 1) in free_banks:
                        free_banks.remove(b)
                        free_banks.remove(b + 1)
                        return b
                raise AssertionError("no PSUM bank pair free")

            def bput(b, n=1):
                for i in range(n):
                    free_banks.append(b + i)

            def BK(b):
                return ("B", b)

            def mm(out, lhsT, rhs, start, stop, r, b):
                PE("matmul", dict(out=out, lhsT=lhsT, rhs=rhs, start=start, stop=stop), r=r, x=[BK(b)])

            def tp(out, in_, ident, r, b):
                PE("transpose", dict(out=out, in_=in_, identity=ident), r=r, x=[BK(b)])

            wlist = [] if record else wlist_in
            OFFS = {"q": O_Q, "k": O_K, "v": O_V, "gg": O_GG, "cc": O_CC, "ch": O_CH, "cb": O_CB, "gc": O_GC, "mg": O_MG, "mc": O_MC}

            def wspec(tag):
                nm = tag[0]
                if nm == "ada":
                    return w_ada[tag[1], :, tag[2] * WC:(tag[2] + 1) * WC], NKC, WC
                if nm == "alr":
                    return w_in[tag[2], :, O_ALR:O_ALR + 16], NKC, 16
                l, i = tag[2], tag[3]
                if nm in OFFS:
                    return w_in[l, :, OFFS[nm] + i * WC: OFFS[nm] + (i + 1) * WC], NKC, WC
                if nm == "pa":
                    return w_pa[l, :, i * WC:(i + 1) * WC], NKC, WC
                if nm == "pb":
                    return w_pb[l, :, i * WC:(i + 1) * WC], 4, WC
                if nm == "o":
                    return w_o[l, :, i * WC:(i + 1) * WC], NKC, WC
                raise KeyError(tag)

            wstate = {"issued": 0, "taken": 0}
            released = [False] * (100000 if record else len(wlist))

            def wpump():
                while wstate["issued"] < len(wlist):
                    j = wstate["issued"]
                    if j >= NW and not released[j - NW]:
                        break
                    src, nkc, ncols = wspec(wlist[j])
                    slot = j % NW
                    dst = wring[:, slot, 0:nkc, 0:ncols]
                    srcv = src.rearrange("(kc p) c -> p kc c", p=128)
                    DMA("pool", "dma_start", dict(out=dst, in_=srcv), w=[("W", slot)])
                    wstate["issued"] += 1

            def wtake(tag):
                i = wstate["taken"]
                if record:
                    wlist.append(tag)
                    wstate["taken"] += 1
                    return i, wring[:, i % NW], ("W", i % NW)
                assert wlist[i] == tag, (wlist[i], tag)
                wpump()
                assert wstate["issued"] > i, "weight ring deadlock"
                wstate["taken"] += 1
                return i, wring[:, i % NW], ("W", i % NW)

            def wrel(i):
                released[i] = True
                if not record:
                    wpump()

            POOL("memset", dict(ap=ident_f[:], constant=0.0), w=["ident_f"])
            POOL("affine_select", dict(out=ident_f[:], in_=ident_f[:], pattern=[[-1, 128]], compare_op=ALU.not_equal,
                                           fill=1.0, base=0, channel_multiplier=1), r=["ident_f"], w=["ident_f"])
            POOL("tensor_copy", dict(out=ident_b[:], in_=ident_f[:]), r=["ident_f"], w=["ident_b"])
            POOL("memset", dict(ap=tri[:], constant=1.0), w=["tri"])
            POOL("affine_select", dict(out=tri[:], in_=tri[:], pattern=[[1, 128]], compare_op=ALU.is_ge,
                                           fill=0.0, base=0, channel_multiplier=-1), r=["tri"], w=["tri"])
            for h in range(4):
                POOL("tensor_copy", dict(out=mask4[:, h, :], in_=tri[:]), r=["tri"], w=["mask4"])
            POOL("memset", dict(ap=ones_b[:], constant=1.0), w=["ones_b"])
            POOL("memset", dict(ap=alr_aug[:], constant=1.0), w=["alr_aug"])
            POOL("memset", dict(ap=carry[:], constant=0.0), w=["carry"])
            POOL("memset", dict(ap=S[:], constant=0.0), w=[("S", l_, h_) for l_ in range(DEPTH) for h_ in range(H)])
            POOL("memset", dict(ap=S_bf[:], constant=0.0), w=[("S_bf", l_, h_) for l_ in range(DEPTH) for h_ in range(H)])

            DMA("sp", "dma_start", dict(out=c17, in_=c17d), w=aw("c17"))
            DMA("sp", "dma_start", dict(out=xs_sb[:], in_=xs), w=["xs"])
            for l in range(DEPTH):
                DMA("sp", "dma_start", dict(out=w_a2aug[0:16, l, :], in_=w_a2[l]), w=["w_a2aug"])
                DMA("sp", "dma_start", dict(out=w_a2aug[16:17, l, :], in_=b_a[l:l + 1, :]), w=["w_a2aug"])
                DMA("sp", "dma_start", dict(out=gnorm[:, l, :], in_=gng[l].rearrange("(c p) -> p c", p=128),
                                                     allow_slow_non_contiguous=True), w=["gnorm"])
                for ti in range(3):
                    DMA("sp", "dma_start", dict(out=convw[:, l, :, ti], in_=conv_w[l, ti].rearrange("(c p) -> p c", p=128),
                                                allow_slow_non_contiguous=True), w=["convw"])
                DMA("sp", "dma_start", dict(out=b_adaT[:, l, :], in_=b_ada[l].rearrange("(f p) -> p f", p=128),
                                                     allow_slow_non_contiguous=True), w=["b_adaT"])

            b = bget()
            for c in range(NKC):
                tp(ps[:, b, c * 17:(c + 1) * 17], c17[:, c * 128:(c + 1) * 128], ident_f[0:17, 0:17], ["ident_f"] + ar("c17"), b)
            ACT("copy", dict(out=cT[:].rearrange("p c t -> p (c t)"), in_=ps[:, b, 0:NKC * 17]), w=["cT"], x=[BK(b)])
            bput(b)
            for c in range(NKC):
                DVE("tensor_scalar", dict(out=cTb[:, c, :], in0=ones_b[:], scalar1=cT[:, c, 0:1], scalar2=None, op0=ALU.mult),
                    r=["ones_b", "cT"] + ar(), w=aw("cTb"))
            for l in range(DEPTH):
                DVE("tensor_scalar", dict(out=b_adaT[:, l, 8:16], in0=b_adaT[:, l, 8:16], scalar1=1.0, scalar2=None, op0=ALU.add),
                    r=["b_adaT"], w=["b_adaT"])
                DMA("sp", "dma_start", dict(out=bgate, in_=b_ada[l, 2 * D:3 * D].partition_broadcast(128)), w=aw("bgate"))
                b = bget()
                for wi in range(8):
                    i, slot, wk = wtake(("ada", l, wi))
                    for sub in range(2):
                        fc = wi * 2 + sub
                        for kc in range(NKC):
                            mm(ps[:, b, fc * 17:(fc + 1) * 17], slot[:, kc, sub * 128:(sub + 1) * 128], cT[:, kc, :],
                               kc == 0, kc == NKC - 1, [wk, "cT"], b)
                    wrel(i)
                DVE("tensor_tensor", dict(out=ada_sb[:, l, :, :], in0=ps[:, b, 0:16 * 17].rearrange("p (f t) -> p f t", f=16),
                                                       in1=b_adaT[:, l, 0:16].unsqueeze(2).to_broadcast([128, 16, 17]), op=ALU.add),
                    r=["b_adaT"], w=["ada_sb"], x=[BK(b)])
                bput(b)
                for half in range(2):
                    bp = bget()
                    bs = bget()
                    for wj in range(2):
                        wi = 8 + half * 2 + wj
                        i, slot, wk = wtake(("ada", l, wi))
                        for kc in range(NKC):
                            mm(ps[:, bp, wj * WC:(wj + 1) * WC], cTb[:, kc, :], slot[:, kc, :], kc == 0, kc == NKC - 1, [wk] + ar("cTb"), bp)
                        for kc in range(NKC):
                            mm(ps[0:NS, bs, wj * WC:(wj + 1) * WC], cT[:, kc, 1:17], slot[:, kc, :], kc == 0, kc == NKC - 1, [wk, "cT"], bs)
                        wrel(i)
                    DVE("tensor_tensor", dict(out=gate_bc[:, l, half * 512:(half + 1) * 512], in0=ps[:, bp, :],
                                                                        in1=bgate[:, half * 512:(half + 1) * 512], op=ALU.add),
                        r=ar("bgate"), w=["gate_bc"], x=[BK(bp)])
                    DVE("tensor_tensor", dict(out=gate_s[:, l, half * 512:(half + 1) * 512], in0=ps[0:NS, bs, :],
                                                                        in1=bgate[0:NS, half * 512:(half + 1) * 512], op=ALU.add),
                        r=ar("bgate"), w=["gate_s"], x=[BK(bs)])
                    bput(bp)
                    bput(bs)

            for g in range(ng_run):
                has_s = (g == SG)
                for l in range(nl_run):
                    last_g = (g == ng_run - 1)
                    if l == 0:
                        DMA("sp", "dma_start", dict(out=x_sb[:], in_=xp[g * BT:(g + 1) * BT, :].rearrange("(j p) d -> p j d", p=128)),
                            w=[("x", j) for j in range(GT)])
                    DMA("sp", "dma_start", dict(out=lng_bc[:], in_=ln_g[l].partition_broadcast(128)), w=["lng"])
                    DMA("sp", "dma_start", dict(out=lnb_bc[:], in_=ln_b[l].partition_broadcast(128)), w=["lnb"])

                    for c in range(NKC):
                        b = bget()
                        for j in range(GT):
                            tp(ps[:, b, j * 128:(j + 1) * 128], x_sb[:, j, c * 128:(c + 1) * 128], ident_f[:], ["ident_f", ("x", j)], b)
                        ACT("activation", dict(out=uT[:, c, 0:BT], in_=ps[:, b, :], func=AF.Identity,
                                                                  bias=ada_sb[:, l, c, 0:1], scale=ada_sb[:, l, 8 + c, 0:1]),
                            r=["ada_sb"], w=[("uT", c)], x=[BK(b)])
                        bput(b)
                    if has_s:
                        b = bget()
                        for c in range(NKC):
                            tp(ps[:, b, c * NS:(c + 1) * NS], xs_sb[:, c * 128:(c + 1) * 128], ident_f[0:NS, 0:NS], ["ident_f", "xs"], b)
                        DVE("tensor_tensor", dict(out=tmp_s[:], in0=ps[:, b, 0:NKC * NS].rearrange("p (c t) -> p c t", c=NKC),
                                                               in1=ada_sb[:, l, 8:16, 1:17], op=ALU.mult),
                            r=["ada_sb"], w=["tmp_s"], x=[BK(b)])
                        bput(b)
                        DVE("tensor_tensor", dict(out=uT[:, :, BT:TW], in0=tmp_s[:], in1=ada_sb[:, l, 0:8, 1:17], op=ALU.add),
                            r=["ada_sb", "tmp_s"], w=["uTs"])

                    uTr = [("uT", c) for c in range(NKC)]

                    epoch_barrier()
                    i, slot, wk = wtake(("alr", g, l))
                    b = bget()
                    for kc in range(NKC):
                        mm(ps[0:16, b, :], slot[:, kc, 0:16], uT[:, kc, 0:BT], kc == 0, kc == NKC - 1, [wk, ("uT", kc)], b)
                    ACT("copy", dict(out=alr_aug[0:16, 0:BT], in_=ps[0:16, b, :]), w=["alr_aug"], x=[BK(b)])
                    bput(b)
                    if has_s:
                        b = bget()
                        for kc in range(NKC):
                            mm(ps[0:16, b, 0:NS], slot[:, kc, 0:16], uT[:, kc, BT:TW], kc == 0, kc == NKC - 1, [wk, "uTs"], b)
                        ACT("copy", dict(out=alr_aug[0:16, BT:TW], in_=ps[0:16, b, 0:NS]), w=["alr_aug_s"], x=[BK(b)])
                        bput(b)
                    wrel(i)
                    bvs_h = [None]
                    bgs_h = [None]

                    def do_v(h):
                        i, slot, wk = wtake(("v", g, l, h))
                        for jp in range(2):
                            b = bget()
                            for jj in range(2):
                                j = jp * 2 + jj
                                for kc in range(NKC):
                                    mm(ps[:, b, jj * WC:(jj + 1) * WC], uT[:, kc, j * 128:(j + 1) * 128], slot[:, kc, :], kc == 0, kc == NKC - 1,
                                       [wk, ("uT", kc)], b)
                            ACT("copy", dict(out=v_tok[:, jp * 2:jp * 2 + 2, h * DV:(h + 1) * DV],
                                             in_=ps[:, b, :].rearrange("p (j v) -> p j v", j=2)),
                                r=ar(), w=aw(f"v_tok{jp * 2}", f"v_tok{jp * 2 + 1}"), x=[BK(b)])
                            bput(b)
                        if has_s:
                            if h % 2 == 0:
                                bvs_h[0] = bget()
                            bvs = bvs_h[0]
                            for kc in range(NKC):
                                mm(ps[0:NS, bvs, (h % 2) * WC:(h % 2 + 1) * WC], uT[:, kc, BT:TW], slot[:, kc, :], kc == 0, kc == NKC - 1, [wk, "uTs"], bvs)
                            if h % 2 == 1:
                                ACT("copy", dict(out=v_s[:, (h - 1) * DV:(h + 1) * DV], in_=ps[0:NS, bvs, :]),
                                    r=ar(), w=aw("v_s"), x=[BK(bvs)])
                                bput(bvs)
                        wrel(i)

                    def do_gg(wi):
                        if has_s and wi == 0:
                            bgs_h[0] = bget()
                        bgs = bgs_h[0]
                        i, slot, wk = wtake(("gg", g, l, wi))
                        for sub in range(2):
                            fc = wi * 2 + sub
                            b = bget()
                            for kc in range(NKC):
                                mm(ps[:, b, :], slot[:, kc, sub * 128:(sub + 1) * 128], uT[:, kc, 0:BT], kc == 0, kc == NKC - 1, [wk, ("uT", kc)], b)
                            ACT("activation", dict(out=sgt[:, fc, 0:BT], in_=ps[:, b, :], func=AF.Silu), r=ar(), w=aw(f"sg{fc}"), x=[BK(b)])
                            bput(b)
                            if has_s:
                                for kc in range(NKC):
                                    mm(ps[:, bgs, fc * NS:(fc + 1) * NS], slot[:, kc, sub * 128:(sub + 1) * 128], uT[:, kc, BT:TW],
                                       kc == 0, kc == NKC - 1, [wk, "uTs"], bgs)
                        wrel(i)
                        if has_s and wi == 3:
                            ACT("activation", dict(out=sgt[:, :, BT:TW], in_=ps[:, bgs, 0:NKC * NS].rearrange("p (c t) -> p c t", c=NKC), func=AF.Silu),
                                r=ar(), w=aw("sg_s"), x=[BK(bgs)])
                            bput(bgs)

                    bulk = [(do_v, h) for h in range(H)] + [(do_gg, wi) for wi in range(4)]
                    EQK = aw(*[f"E_q{h}" for h in range(H)])
                    EKK = aw(*[f"E_k{h}" for h in range(H)])
                    for j in range(GT):
                        jj = j % 2
                        b = bget()
                        mm(ps[:, b, :], alr_aug[0:17, j * 128:(j + 1) * 128], w_a2aug[0:17, l, :], True, True, ["alr_aug", "w_a2aug"], b)
                        ACT("activation", dict(out=e1, in_=ps[:, b, :], func=AF.Exp, scale=-1.0), r=ar(), w=aw("e1"), x=[BK(b)])
                        bput(b)
                        ACT("activation", dict(out=sp[:, jj, :], in_=e1, func=AF.Ln, bias=1.0), r=ar("e1"), w=aw("sp"))
                        bc = bget()
                        for h in range(H):
                            mm(ps[:, bc, h * 128:(h + 1) * 128], sp[:, jj, h * 128:(h + 1) * 128], tri[:], True, True, ["tri"] + ar("sp"), bc)
                        csv = ps[:, bc, :].rearrange("p (h t) -> p h t", h=H)
                        ACT("activation", dict(out=E_q[:, :, j * 128:(j + 1) * 128], in_=csv, func=AF.Exp, scale=-1.0 / 16), r=ar(), w=EQK, x=[BK(bc)])
                        ACT("activation", dict(out=E_k[:, :, j * 128:(j + 1) * 128], in_=csv, func=AF.Exp, scale=1.0 / 16), r=ar(), w=EKK, x=[BK(bc)])
                        bput(bc)
                        f_, a_ = bulk.pop(0)
                        f_(a_)
                    if has_s:
                        b = bget()
                        mm(ps[0:NS, b, :], alr_aug[0:17, BT:TW], w_a2aug[0:17, l, :], True, True, ["alr_aug_s", "w_a2aug"], b)
                        ACT("activation", dict(out=e1[0:NS, :], in_=ps[0:NS, b, :], func=AF.Exp, scale=-1.0), r=ar(), w=aw("e1"), x=[BK(b)])
                        bput(b)
                        ACT("activation", dict(out=sp[0:NS, 0, :], in_=e1[0:NS, :], func=AF.Ln, bias=1.0), r=ar("e1"), w=aw("sp"))
                        b = bget()
                        for h in range(H):
                            tp(ps[:, b, h * NS:(h + 1) * NS], sp[0:NS, 0, h * 128:(h + 1) * 128], ident_f[0:NS, 0:NS], ["ident_f"] + ar("sp"), b)
                        ACT("activation", dict(out=a_s[:].rearrange("p h s -> p (h s)"), in_=ps[:, b, 0:H * NS], func=AF.Exp, scale=-1.0 / 16),
                            r=ar(), w=aw("a_s"), x=[BK(b)])
                        bput(b)
                    while bulk:
                        f_, a_ = bulk.pop(0)
                        f_(a_)
                    for nm, Eb, dst in (("q", E_q, q_in), ("k", E_k, k_in)):
                        bs = bget() if (has_s and nm == "q") else None
                        bkt = bget() if (has_s and nm == "k") else None
                        for wi in range(2):
                            i, slot, wk = wtake((nm, g, l, wi))
                            for sub in range(2):
                                h = wi * 2 + sub
                                b = bget()
                                for kc in range(NKC):
                                    mm(ps[:, b, :], slot[:, kc, sub * 128:(sub + 1) * 128], uT[:, kc, 0:BT], kc == 0, kc == NKC - 1, [wk, ("uT", kc)], b)
                                if nm == "q":
                                    DVE("scalar_tensor_tensor", dict(out=q_in[:, h, :], in0=ps[:, b, :], scalar=QSCALE, in1=E_q[:, h, :],
                                                                                   op0=ALU.mult, op1=ALU.mult),
                                        r=ar(f"E_q{h}"), w=aw(f"q_in{h}"), x=[BK(b)])
                                else:
                                    DVE("tensor_tensor", dict(out=k_in[:, h, :], in0=ps[:, b, :], in1=E_k[:, h, :], op=ALU.mult),
                                        r=ar(f"E_k{h}"), w=aw(f"k_in{h}"), x=[BK(b)])
                                bput(b)
                                if has_s and nm == "q":
                                    for kc in range(NKC):
                                        mm(ps[:, bs, h * NS:(h + 1) * NS], slot[:, kc, sub * 128:(sub + 1) * 128], uT[:, kc, BT:TW],
                                           kc == 0, kc == NKC - 1, [wk, "uTs"], bs)
                            if has_s and nm == "k":
                                for kc in range(NKC):
                                    mm(ps[0:NS, bkt, wi * WC:(wi + 1) * WC], uT[:, kc, BT:TW], slot[:, kc, :], kc == 0, kc == NKC - 1, [wk, "uTs"], bkt)
                            wrel(i)
                        if has_s and nm == "q":
                            ACT("activation", dict(out=q_s[:].rearrange("p h s -> p (h s)"), in_=ps[:, bs, 0:H * NS], func=AF.Identity, scale=QSCALE),
                                r=ar(), w=aw("q_s"), x=[BK(bs)])
                        if has_s and nm == "k":
                            ACT("copy", dict(out=ktok_s, in_=ps[0:NS, bkt, :]), r=ar(), w=aw("e1"), x=[BK(bkt)])
                            bput(bkt)
                        if has_s and nm == "q":
                            bput(bs)
                    bos = bget() if has_s else None

                    def sample_update(s):
                        buf = Sbuf[:, s % 3, :]
                        bk = ("Sbuf", s % 3)
                        sbv = sbf[:, s % 2, :]
                        sk = ("sbf", s % 2)
                        DMA("sp", "dma_start", dict(out=buf.rearrange("p (h v) -> p h v", h=H), in_=sgl[l, s].rearrange("h k v -> k h v")),
                            w=[bk])
                        DVE("tensor_scalar", dict(out=km, in0=ktok_s, scalar1=ident_f[0:NS, s:s + 1], scalar2=None, op0=ALU.mult),
                            r=["ident_f"] + ar("e1"), w=aw("km"))
                        b2 = bget2()
                        for h in range(H):
                            mm(ps[:, b2 + h // 2, (h % 2) * DV:(h % 2 + 1) * DV], km[:, h * 128:(h + 1) * 128], v_s[:, h * DV:(h + 1) * DV], True, True,
                               ar("km", "v_s"), b2 + h // 2)
                        for h in range(H):
                            ACT("activation", dict(out=buf[:, h * DV:(h + 1) * DV], in_=buf[:, h * DV:(h + 1) * DV], func=AF.Identity,
                                                   scale=a_s[:, h, s:s + 1]),
                                r=[bk] + ar("a_s"), w=[bk])
                        for half in range(2):
                            DVE("tensor_tensor", dict(out=buf[:, half * 512:(half + 1) * 512], in0=ps[:, b2 + half, :],
                                                      in1=buf[:, half * 512:(half + 1) * 512], op=ALU.add),
                                r=[bk], w=[bk], x=[BK(b2 + half)])
                        bput(b2, 2)
                        DMA("sp", "dma_start", dict(out=gs[l, s].rearrange("h k v -> k h v"), in_=buf.rearrange("p (h v) -> p h v", h=H)),
                            r=[bk], w=[("gs", l, s)])
                        POOL("tensor_copy", dict(out=sbv, in_=buf), r=[bk], w=[sk])
                        for h in range(H):
                            for vc in range(2):
                                idx = h * 2 + vc
                                mm(ps[:, bos, idx * NS + s: idx * NS + s + 1], sbv[:, idx * 128:(idx + 1) * 128], q_s[:, h, s:s + 1], True, True,
                                   [sk] + ar("q_s"), bos)

                    for j in range(GT):
                        jj = j % 2
                        jc0, jc1 = j * 128, (j + 1) * 128
                        b = bget()
                        psb = ps[:, b, :].bitcast(BF16)
                        for h in range(H):
                            tp(psb[:, h * 128:(h + 1) * 128], k_in[:, h, jc0:jc1], ident_b[:], ["ident_b"] + ar(f"k_in{h}"), b)
                        ACT("copy", dict(out=k_tok[:, jj, :], in_=psb[:, 0:512]), r=ar(), w=aw(f"k_tok{jj}"), x=[BK(b)])
                        bput(b)
                        b = bget()
                        for h in range(H):
                            mm(ps[:, b, h * 128:(h + 1) * 128], k_in[:, h, jc0:jc1], q_in[:, h, jc0:jc1], True, True, ar(f"k_in{h}", f"q_in{h}"), b)
                        DVE("tensor_tensor", dict(out=attm[:, jj, :], in0=ps[:, b, :], in1=mask4[:].rearrange("p h t -> p (h t)"), op=ALU.mult),
                            r=["mask4"] + ar(), w=aw(f"attm{jj}"), x=[BK(b)])
                        bput(b)
                        bo = bget2()
                        for h in range(H):
                            for vc in range(2):
                                idx = h * 2 + vc
                                out = ps[:, bo + idx // 4, (idx % 4) * 128:(idx % 4 + 1) * 128]
                                mm(out, v_tok[:, j, idx * 128:(idx + 1) * 128], attm[:, jj, h * 128:(h + 1) * 128], True, False,
                                   ar(f"v_tok{j}", f"attm{jj}"), bo + idx // 4)
                                mm(out, S_bf[:, l, idx * 128:(idx + 1) * 128], q_in[:, h, jc0:jc1], False, True, [("S_bf", l, h)] + ar(f"q_in{h}"), bo + idx // 4)
                        b2 = bget2()
                        for h in range(H):
                            mm(ps[:, b2 + h // 2, (h % 2) * DV:(h % 2 + 1) * DV], k_tok[:, jj, h * 128:(h + 1) * 128], v_tok[:, j, h * DV:(h + 1) * DV], True, True,
                               ar(f"k_tok{jj}", f"v_tok{j}"), b2 + h // 2)
                        Pst = t1[:].rearrange("p c t -> p (c t)")
                        for half in range(2):
                            DVE("tensor_tensor", dict(out=Pst[:, half * 512:(half + 1) * 512], in0=ps[:, b2 + half, :],
                                                      in1=S[:, l, half * 512:(half + 1) * 512], op=ALU.add),
                                r=[("S", l, 2 * half), ("S", l, 2 * half + 1)] + ar(), w=aw(f"t1{half}"), x=[BK(b2 + half)])
                        bput(b2, 2)
                        for h in range(H):
                            dcol = E_q[:, h, jc1 - 1:jc1]
                            ACT("activation", dict(out=S_bf[:, l, h * DV:(h + 1) * DV], in_=Pst[:, h * DV:(h + 1) * DV], func=AF.Identity, scale=dcol),
                                r=ar(f"E_q{h}", f"t1{h // 2}"), w=[("S_bf", l, h)])
                            POOL("tensor_scalar", dict(out=S[:, l, h * DV:(h + 1) * DV], in0=Pst[:, h * DV:(h + 1) * DV], scalar1=dcol, scalar2=0.0,
                                                       op0=ALU.mult, op1=ALU.add),
                                 r=ar(f"E_q{h}", f"t1{h // 2}"), w=[("S", l, h)])
                        ov = ps[:, bo:bo + 2, :].rearrange("p b (c t) -> p (b c) t", c=4)
                        ACT("activation", dict(out=sq[:], in_=ov, func=AF.Square), r=ar(), w=aw("sq"), x=[BK(bo), BK(bo + 1)])
                        bss = bget()
                        for h in range(H):
                            for vc in range(2):
                                mm(ps[:, bss, h * 128:(h + 1) * 128], ones_b[:], sq[:, h * 2 + vc, :], vc == 0, vc == 1, ["ones_b"] + ar("sq"), bss)
                        ACT("activation", dict(out=lnv, in_=ps[:, bss, :], func=AF.Ln, bias=EPS, scale=1.0 / DV), r=ar(), w=aw("lnv"), x=[BK(bss)])
                        bput(bss)
                        ACT("activation", dict(out=rstd[:].rearrange("p h t -> p (h t)"), in_=lnv, func=AF.Exp, scale=-0.5), r=ar("lnv"), w=aw("rstd"))
                        for vc in range(2):
                            DVE("scalar_tensor_tensor", dict(out=t1[:, vc::2, :], in0=ov[:, vc::2, :], scalar=gnorm[:, l, vc:vc + 1], in1=rstd[:],
                                                                               op0=ALU.mult, op1=ALU.mult),
                                r=["gnorm"] + ar("rstd"), w=aw("t10", "t11"), x=[BK(bo), BK(bo + 1)])
                        bput(bo, 2)
                        DVE("tensor_tensor", dict(out=og[:, :, jc0:jc1], in0=t1[:], in1=sgt[:, :, jc0:jc1], op=ALU.mult),
                            r=ar("t10", "t11", *[f"sg{fc}" for fc in range(8)]), w=[("og", j)])
                        if has_s:
                            for s in range(j * 4, (j + 1) * 4):
                                sample_update(s)
                    if has_s:
                        ovs = ps[:, bos, 0:NKC * NS].rearrange("p (c t) -> p c t", c=NKC)
                        ACT("activation", dict(out=sq_s[:], in_=ovs, func=AF.Square), r=ar(), w=aw("sq_s"), x=[BK(bos)])
                        bss = bget()
                        for h in range(H):
                            for vc in range(2):
                                mm(ps[:, bss, h * NS:(h + 1) * NS], ones_b[:], sq_s[:, h * 2 + vc, :], vc == 0, vc == 1, ["ones_b"] + ar("sq_s"), bss)
                        ACT("activation", dict(out=lnv_s, in_=ps[:, bss, 0:H * NS], func=AF.Ln, bias=EPS, scale=1.0 / DV), r=ar(), w=aw("lnv_s"), x=[BK(bss)])
                        bput(bss)
                        ACT("activation", dict(out=rstd_s[:].rearrange("p h t -> p (h t)"), in_=lnv_s, func=AF.Exp, scale=-0.5), r=ar("lnv_s"), w=aw("rstd_s"))
                        for vc in range(2):
                            DVE("scalar_tensor_tensor", dict(out=t1_s[:, vc::2, :], in0=ovs[:, vc::2, :], scalar=gnorm[:, l, vc:vc + 1], in1=rstd_s[:],
                                                                                 op0=ALU.mult, op1=ALU.mult),
                                r=["gnorm"] + ar("rstd_s"), w=aw("t1_s"), x=[BK(bos)])
                        bput(bos)
                        DVE("tensor_tensor", dict(out=og[:, :, BT:TW], in0=t1_s[:], in1=sgt[:, :, BT:TW], op=ALU.mult), r=ar("t1_s", "sg_s"), w=["og_s"])
                    if last_g:
                        DMA("sp", "dma_start", dict(out=gp[l].rearrange("h k v -> k h v"), in_=S[:, l, :].rearrange("p (h v) -> p h v", h=H)),
                            r=[("S", l, h_) for h_ in range(H)], w=[("gp", l)])

                    epoch_barrier()
                    DVE("tensor_copy", dict(out=pbuf[:, :, 0:2], in_=carry[:, l, :, :]), r=["carry"] + ar(), w=aw("pb0", "pb1", "pb2", "pb3"))
                    if has_s:
                        DMA("sp", "dma_start", dict(out=sc_tok, in_=scv[l].rearrange("s r d -> (s r) d")), r=[EP], w=aw("sc_tok"))
                        b = bget()
                        for c in range(4):
                            tp(ps[:, b, c * 2 * NS:(c + 1) * 2 * NS], sc_tok[:, c * 128:(c + 1) * 128], ident_f[0:2 * NS, 0:2 * NS], ["ident_f"] + ar("sc_tok"), b)
                        ACT("copy", dict(out=scT[:].rearrange("p c t -> p (c t)"), in_=ps[:, b, 0:4 * 2 * NS]), r=ar(), w=aw("scT"), x=[BK(b)])
                        bput(b)
                        DMA("sp", "dma_start", dict(out=cs[l, :, 0, :], in_=scv[l, :, 1, :]), w=[("cs0", l)])

                    def conv_mm(nm, post, post_s):
                        bs = bget() if has_s else None
                        for wi in range(2):
                            i, slot, wk = wtake((nm, g, l, wi))
                            for sub in range(2):
                                c = wi * 2 + sub
                                b = bget()
                                for kc in range(NKC):
                                    mm(ps[:, b, :], slot[:, kc, sub * 128:(sub + 1) * 128], uT[:, kc, 0:BT], kc == 0, kc == NKC - 1, [wk, ("uT", kc)], b)
                                post(c, b)
                                bput(b)
                                if has_s:
                                    for kc in range(NKC):
                                        mm(ps[:, bs, c * NS:(c + 1) * NS], slot[:, kc, sub * 128:(sub + 1) * 128], uT[:, kc, BT:TW],
                                           kc == 0, kc == NKC - 1, [wk, "uTs"], bs)
                            wrel(i)
                        if has_s:
                            post_s(ps[:, bs, 0:4 * NS].rearrange("p (c t) -> p c t", c=4), bs)
                            bput(bs)

                    conv_mm("cc",
                            lambda c, b: ACT("copy", dict(out=ccy[:, c, 0:BT], in_=ps[:, b, :]), r=ar(), w=aw(f"ccy{c}"), x=[BK(b)]),
                            lambda pv, bs: ACT("copy", dict(out=ccy[:, :, BT:TW], in_=pv), r=ar(), w=aw("ccy_s"), x=[BK(bs)]))

                    def post_ch(c, b):
                        DVE("tensor_tensor", dict(out=pbuf[:, c, 2:BT + 2], in0=ps[:, b, :], in1=ccy[:, c, 0:BT], op=ALU.mult),
                            r=ar(f"ccy{c}"), w=aw(f"pb{c}"), x=[BK(b)])
                        ACT("activation", dict(out=tmp1, in_=pbuf[:, c, 0:BT], func=AF.Identity, scale=convw[:, l, c, 0:1]),
                            r=["convw"] + ar(f"pb{c}"), w=aw("tmp1"))
                        DVE("scalar_tensor_tensor", dict(out=tmp2, in0=pbuf[:, c, 1:BT + 1], scalar=convw[:, l, c, 1:2], in1=tmp1, op0=ALU.mult, op1=ALU.add),
                            r=["convw"] + ar(f"pb{c}", "tmp1"), w=aw("tmp2"))
                        DVE("scalar_tensor_tensor", dict(out=ccy[:, c, 0:BT], in0=pbuf[:, c, 2:BT + 2], scalar=convw[:, l, c, 2:3], in1=tmp2, op0=ALU.mult, op1=ALU.add),
                            r=["convw"] + ar(f"pb{c}", "tmp2"), w=aw(f"ccy{c}"))

                    def post_ch_s(pv, bs):
                        DVE("tensor_tensor", dict(out=p_s[:], in0=pv, in1=ccy[:, :, BT:TW], op=ALU.mult), r=ar("ccy_s"), w=aw("p_s"), x=[BK(bs)])
                        for c in range(4):
                            ACT("activation", dict(out=tcs1, in_=scT[:, c, 0:2 * NS:2], func=AF.Identity, scale=convw[:, l, c, 0:1]),
                                r=["convw"] + ar("scT"), w=aw("tcs1"))
                            DVE("scalar_tensor_tensor", dict(out=tcs2, in0=scT[:, c, 1:2 * NS:2], scalar=convw[:, l, c, 1:2], in1=tcs1, op0=ALU.mult, op1=ALU.add),
                                r=["convw"] + ar("scT", "tcs1"), w=aw("tcs2"))
                            DVE("scalar_tensor_tensor", dict(out=ccy[:, c, BT:TW], in0=p_s[:, c, :], scalar=convw[:, l, c, 2:3], in1=tcs2, op0=ALU.mult, op1=ALU.add),
                                r=["convw"] + ar("p_s", "tcs2"), w=aw("ccy_s"))

                    conv_mm("ch", post_ch, post_ch_s)
                    DVE("tensor_copy", dict(out=carry[:, l, :, :], in_=pbuf[:, :, BT:BT + 2]), r=ar("pb0", "pb1", "pb2", "pb3"), w=["carry"])
                    if last_g:
                        b = bget()
                        for c in range(4):
                            tp(ps[0:2, b, c * 128:(c + 1) * 128], carry[:, l, c, :], ident_f[:], ["ident_f", "carry"], b)
                        ACT("copy", dict(out=cst, in_=ps[0:2, b, :]), r=ar(), w=aw("cst"), x=[BK(b)])
                        bput(b)
                        DMA("sp", "dma_start", dict(out=cp[l], in_=cst), r=ar("cst"), w=[("cp", l)])
                    if has_s:
                        b = bget()
                        for c in range(4):
                            tp(ps[0:NS, b, c * 128:(c + 1) * 128], p_s[:, c, :], ident_f[:], ["ident_f"] + ar("p_s"), b)
                        ACT("copy", dict(out=pst, in_=ps[0:NS, b, :]), r=ar(), w=aw("pst"), x=[BK(b)])
                        bput(b)
                        DMA("sp", "dma_start", dict(out=cs[l, :, 1, :], in_=pst), r=ar("pst"), w=[("cs1", l)])
                    conv_mm("cb",
                            lambda c, b: DVE("tensor_tensor", dict(out=ccy[:, c, 0:BT], in0=ps[:, b, :], in1=ccy[:, c, 0:BT], op=ALU.mult),
                                             r=ar(f"ccy{c}"), w=aw(f"ccy{c}"), x=[BK(b)]),
                            lambda pv, bs: DVE("tensor_tensor", dict(out=ccy[:, :, BT:TW], in0=pv, in1=ccy[:, :, BT:TW], op=ALU.mult),
                                               r=ar("ccy_s"), w=aw("ccy_s"), x=[BK(bs)]))

                    def post_gc(c, b):
                        ACT("activation", dict(out=tmp3, in_=ps[:, b, :], func=AF.Silu), r=ar(), w=aw("tmp3"), x=[BK(b)])
                        DVE("tensor_tensor", dict(out=yc[:, c, 0:BT], in0=ccy[:, c, 0:BT], in1=tmp3, op=ALU.mult), r=ar(f"ccy{c}", "tmp3"), w=[("yc", c)])

                    def post_gc_s(pv, bs):
                        ACT("activation", dict(out=tmp3s[:], in_=pv, func=AF.Silu), r=ar(), w=aw("tmp3s"), x=[BK(bs)])
                        DVE("tensor_tensor", dict(out=yc[:, :, BT:TW], in0=ccy[:, :, BT:TW], in1=tmp3s[:], op=ALU.mult), r=ar("ccy_s", "tmp3s"), w=["yc_s"])

                    conv_mm("gc", post_gc, post_gc_s)

                    ogr = [("og", j) for j in range(GT)]
                    ycr = [("yc", c) for c in range(4)]
                    for cpi in range(4):
                        ia, sa, ka = wtake(("pa", g, l, cpi))
                        ib, sbb, kb = wtake(("pb", g, l, cpi))
                        ig, sgw, kg = wtake(("mg", g, l, cpi))
                        ic, scw, kcw = wtake(("mc", g, l, cpi))
                        for sub in range(2):
                            c = cpi * 2 + sub
                            cs0, cs1 = sub * 128, (sub + 1) * 128
                            bA = bget()
                            for kc in range(NKC):
                                mm(ps[:, bA, :], sa[:, kc, cs0:cs1], og[:, kc, 0:BT], kc == 0, kc == NKC - 1, [ka] + ogr, bA)
                            bB = bget()
                            for kc in range(4):
                                mm(ps[:, bB, :], sbb[:, kc, cs0:cs1], yc[:, kc, 0:BT], kc == 0, kc == 3, [kb, ("yc", kc)], bB)
                            bG = bget()
                            for kc in range(NKC):
                                mm(ps[:, bG, :], sgw[:, kc, cs0:cs1], uT[:, kc, 0:BT], kc == 0, kc == NKC - 1, [kg, ("uT", kc)], bG)
                            bC = bget()
                            for kc in range(NKC):
                                mm(ps[:, bC, :], scw[:, kc, cs0:cs1], uT[:, kc, 0:BT], kc == 0, kc == NKC - 1, [kcw, ("uT", kc)], bC)
                            ACT("activation", dict(out=sgm, in_=ps[:, bG, :], func=AF.Sigmoid), r=ar(), w=aw("sgm"), x=[BK(bG)])
                            bput(bG)
                            ACT("activation", dict(out=scm, in_=ps[:, bC, :], func=AF.Sigmoid), r=ar(), w=aw("scm"), x=[BK(bC)])
                            bput(bC)
                            DVE("tensor_tensor", dict(out=tm1, in0=ps[:, bA, :], in1=sgm, op=ALU.mult), r=ar("sgm"), w=aw("tm1"), x=[BK(bA)])
                            bput(bA)
                            DVE("tensor_tensor", dict(out=tm2, in0=ps[:, bB, :], in1=scm, op=ALU.mult), r=ar("scm"), w=aw("tm2"), x=[BK(bB)])
                            bput(bB)
                            DVE("tensor_tensor", dict(out=mg[:, c, 0:BT], in0=tm1, in1=tm2, op=ALU.add), r=ar("tm1", "tm2"), w=aw(f"mg{c}"))
                            if has_s:
                                bS = bget()
                                for kc in range(NKC):
                                    mm(ps[:, bS, 0:NS], sa[:, kc, cs0:cs1], og[:, kc, BT:TW], kc == 0, kc == NKC - 1, [ka, "og_s"], bS)
                                for kc in range(4):
                                    mm(ps[:, bS, NS:2 * NS], sbb[:, kc, cs0:cs1], yc[:, kc, BT:TW], kc == 0, kc == 3, [kb, "yc_s"], bS)
                                for kc in range(NKC):
                                    mm(ps[:, bS, 2 * NS:3 * NS], sgw[:, kc, cs0:cs1], uT[:, kc, BT:TW], kc == 0, kc == NKC - 1, [kg, "uTs"], bS)
                                for kc in range(NKC):
                                    mm(ps[:, bS, 3 * NS:4 * NS], scw[:, kc, cs0:cs1], uT[:, kc, BT:TW], kc == 0, kc == NKC - 1, [kcw, "uTs"], bS)
                                ACT("activation", dict(out=sg_s, in_=ps[:, bS, 2 * NS:4 * NS], func=AF.Sigmoid), r=ar(), w=aw("sg_s2"), x=[BK(bS)])
                                DVE("tensor_tensor", dict(out=t_s, in0=ps[:, bS, 0:2 * NS], in1=sg_s, op=ALU.mult), r=ar("sg_s2"), w=aw("t_s"), x=[BK(bS)])
                                bput(bS)
                                DVE("tensor_tensor", dict(out=mg[:, c, BT:TW], in0=t_s[:, 0:NS], in1=t_s[:, NS:2 * NS], op=ALU.add), r=ar("t_s"), w=aw("mg_s"))
                        wrel(ia)
                        wrel(ib)
                        wrel(ig)
                        wrel(ic)

                    so = [wtake(("o", g, l, wi)) for wi in range(4)]
                    mgr = ar(*[f"mg{c}" for c in range(NKC)])

                    def o_tail(xv, gv, tov, stv, mvv, lg, lb, bo, npart, xkey, mkeys):
                        if gv is not None:
                            DVE("tensor_tensor", dict(out=tov.rearrange("p (b n) -> p b n", b=2), in0=ps[0:npart, bo:bo + 2, :],
                                                      in1=gv.rearrange("p (b n) -> p b n", b=2), op=ALU.mult),
                                r=["gate_bc", "gate_s"] + ar(), w=aw(mkeys + "to"), x=[BK(bo), BK(bo + 1)])
                            bput(bo, 2)
                            DVE("scalar_tensor_tensor", dict(out=xv, in0=xv, scalar=ALPHA, in1=tov, op0=ALU.mult, op1=ALU.add),
                                r=[xkey] + ar(mkeys + "to"), w=[xkey])
                        else:
                            DVE("scalar_tensor_tensor", dict(out=xv.rearrange("p (b n) -> p b n", b=2), in0=xv.rearrange("p (b n) -> p b n", b=2), scalar=ALPHA,
                                                             in1=ps[0:npart, bo:bo + 2, :], op0=ALU.mult, op1=ALU.add),
                                r=[xkey], w=[xkey], x=[BK(bo), BK(bo + 1)])
                            bput(bo, 2)
                        for hh in range(2):
                            DVE("bn_stats", dict(out=stv[:, hh, :], in_=xv[:, hh * 512:(hh + 1) * 512]), r=[xkey], w=[mkeys + "st"])
                        DVE("bn_aggr", dict(out=mvv[:, 0:2], in_=stv.rearrange("p a b -> p (a b)")), r=[mkeys + "st"], w=[mkeys + "mv"])
                        ACT("activation", dict(out=mvv[:, 2:3], in_=mvv[:, 1:2], func=AF.Ln, bias=EPS), r=[mkeys + "mv"], w=[mkeys + "mv2"])
                        ACT("activation", dict(out=mvv[:, 3:4], in_=mvv[:, 2:3], func=AF.Exp, scale=-0.5), r=[mkeys + "mv2"], w=[mkeys + "mv3"])
                        DVE("tensor_scalar", dict(out=mvv[:, 4:5], in0=mvv[:, 0:1], scalar1=mvv[:, 3:4], scalar2=-1.0, op0=ALU.mult, op1=ALU.mult),
                            r=[mkeys + "mv", mkeys + "mv3"], w=[mkeys + "mv4"])
                        ACT("activation", dict(out=xv, in_=xv, func=AF.Identity, bias=mvv[:, 4:5], scale=mvv[:, 3:4]),
                            r=[xkey, mkeys + "mv3", mkeys + "mv4"], w=[xkey])
                        POOL("tensor_tensor", dict(out=xv, in0=xv, in1=lg, op=ALU.mult), r=[xkey, "lng"], w=[xkey])
                        DVE("tensor_tensor", dict(out=xv, in0=xv, in1=lb, op=ALU.add), r=[xkey, "lnb"], w=[xkey])

                    if has_s:
                        bo = bget2()
                        for wi in range(4):
                            for kc in range(NKC):
                                mm(ps[0:NS, bo + wi // 2, (wi % 2) * WC:(wi % 2 + 1) * WC], mg[:, kc, BT:TW], so[wi][1][:, kc, :],
                                   kc == 0, kc == NKC - 1, [so[wi][2]] + ar("mg_s"), bo + wi // 2)
                        o_tail(xs_sb[:], gate_s[:, l, :], to_s, st_s[:], mv_s[:], lng_bc[0:NS, :], lnb_bc[0:NS, :], bo, NS, "xs", "s")
                    for wi in range(4):
                        POOL("tensor_tensor", dict(out=so[wi][1], in0=so[wi][1],
                                                   in1=gate_bc[:, l, wi * WC:(wi + 1) * WC].unsqueeze(1).to_broadcast([128, NKC, WC]), op=ALU.mult),
                             r=[so[wi][2], "gate_bc"], w=[so[wi][2]])
                    for j in range(GT):
                        bo = bget2()
                        for wi in range(4):
                            for kc in range(NKC):
                                mm(ps[:, bo + wi // 2, (wi % 2) * WC:(wi % 2 + 1) * WC], mg[:, kc, j * 128:(j + 1) * 128], so[wi][1][:, kc, :],
                                   kc == 0, kc == NKC - 1, [so[wi][2]] + mgr, bo + wi // 2)
                        o_tail(x_sb[:, j, :], None, to, st[:], mv[:], lng_bc[:], lnb_bc[:], bo, 128, ("x", j), "p")
                    for wi in range(4):
                        wrel(so[wi][0])
                    if l == nl_run - 1:
                        DMA("sp", "dma_start", dict(out=yp[g * BT:(g + 1) * BT, :].rearrange("(j p) d -> p j d", p=128), in_=x_sb[:]),
                            r=[("x", j) for j in range(GT)], w=[("yp", g)])
                        if has_s:
                            DMA("sp", "dma_start", dict(out=ys, in_=xs_sb[:]), r=["xs"], w=["ys"])

            assert wstate["taken"] == len(wlist), (wstate, len(wlist))
            return wlist

        wl = emit_all(Prog(), True, None)
        P = Prog()
        emit_all(P, False, wl)
        P.finalize()

        @block.tensor
        def _(e):
            P.run_engine("pe", e, sems, dma_sems)

        @block.scalar
        def _(e):
            P.run_engine("act", e, sems, dma_sems)

        @block.vector
        def _(e):
            P.run_engine("dve", e, sems, dma_sems)

        @block.gpsimd
        def _(e):
            P.run_engine("pool", e, sems, dma_sems)

        @block.sync
        def _(e):
            P.run_engine("sp", e, sems, dma_sems)
            P.final_wait(e, dma_sems)

    return nc


_NC_CACHE = {}


def kernel(x_prompt, x_sample, c_prompt, c_sample, state_gla, state_conv, w_ada, b_ada, w_in, w_a2, b_a,
           gla_norm_g, conv_w, w_pa, w_pb, w_o, ln_g, ln_b):
    f = lambda a: np.ascontiguousarray(np.asarray(a, dtype=np.float32))
    x_prompt, x_sample, c_prompt, c_sample = f(x_prompt), f(x_sample), f(c_prompt), f(c_sample)
    state_gla, state_conv = f(state_gla), f(state_conv)
    shared = dict(w_ada=f(w_ada), b_ada=f(b_ada), w_in=f(w_in), w_a2=f(w_a2), b_a=f(b_a), gng=f(gla_norm_g),
                  conv_w=f(conv_w), w_pa=f(w_pa), w_pb=f(w_pb), w_o=f(w_o), ln_g=f(ln_g), ln_b=f(ln_b))
    ncores = 8
    in_maps = []
    for i in range(ncores):
        sl = slice(i * NS, (i + 1) * NS)
        m = dict(shared)
        m["xp"] = x_prompt[i]
        m["xs"] = np.ascontiguousarray(x_sample[sl, 0, :])
        m["c17"] = np.ascontiguousarray(np.concatenate([c_prompt[i:i + 1], c_sample[sl]], axis=0))
        m["sgl"] = np.ascontiguousarray(state_gla[:, sl])
        m["scv"] = np.ascontiguousarray(state_conv[:, sl])
        in_maps.append(m)
    if "nc" not in _NC_CACHE:
        _NC_CACHE["nc"] = build_program()
    res = run_bass_kernel_spmd(_NC_CACHE["nc"], in_maps, core_ids=list(range(ncores)))
    R = res.results
    y_prompt = np.stack([R[i]["yp"] for i in range(ncores)], axis=0)
    y_sample = np.concatenate([R[i]["ys"] for i in range(ncores)], axis=0)[:, None, :]
    gla_p = np.stack([R[i]["gp"] for i in range(ncores)], axis=1)
    conv_p = np.stack([R[i]["cp"] for i in range(ncores)], axis=1)
    gla_s = np.concatenate([R[i]["gs"] for i in range(ncores)], axis=1)
    conv_s = np.concatenate([R[i]["cs"] for i in range(ncores)], axis=1)
    return (y_prompt.astype(np.float32), y_sample.astype(np.float32), gla_p.astype(np.float32),
            conv_p.astype(np.float32), gla_s.astype(np.float32), conv_s.astype(np.float32))
```

```python
import os
import numpy as np
from contextlib import ExitStack
import concourse.bass as bass
import concourse.mybir as mybir
from concourse.bass_utils import run_bass_kernel_spmd

F32 = mybir.dt.float32
BF16 = mybir.dt.bfloat16
AF = mybir.ActivationFunctionType
ALU = mybir.AluOpType

ENGS = ("pe", "act", "dve", "pool", "sp")

D = 1024
SEQ = 2048
DEPTH = 2
NS = 16
H = 4
DK = 128
DV = 256
NKC = 8
BT = 512
GT = 4
NG = SEQ // BT
TW = BT + NS
SG = 0
ALPHA = (2 * DEPTH) ** 0.25
EPS = 1e-5
QSCALE = DK ** -0.5
O_Q, O_K, O_V, O_GG, O_ALR, O_CB, O_CC, O_CH, O_GC, O_MG, O_MC = 0, 512, 1024, 2048, 3072, 3088, 3600, 4112, 4624, 5136, 6160
NW = 8
WC = 256


class Ins:
    __slots__ = ("eng", "fn", "deps", "dma", "sig", "semid", "semval", "pos")

    def __init__(self, eng, fn, dma, pos):
        self.eng = eng
        self.fn = fn
        self.deps = []
        self.dma = dma
        self.sig = False
        self.semid = None
        self.semval = None
        self.pos = pos


class Prog:
    def __init__(self, n_dma_sems=8):
        self.lists = {e: [] for e in ENGS}
        self.last_w = {}
        self.rd_eng = {}
        self.rd_dma = {}
        self.n_dma_sems = n_dma_sems

    def add(self, eng, fn, reads=(), writes=(), excl=(), dma=False):
        I = Ins(eng, fn, dma, len(self.lists[eng]))
        cand = []
        writes = list(writes) + list(excl)
        for k in reads:
            w = self.last_w.get(k)
            if w is not None:
                cand.append((w, True))
        for k in writes:
            w = self.last_w.get(k)
            if w is not None:
                cand.append((w, False))
            for r in self.rd_eng.get(k, {}).values():
                cand.append((r, False))
            for r in self.rd_dma.get(k, ()):
                cand.append((r, False))
        for k in writes:
            self.last_w[k] = I
            self.rd_eng[k] = {}
            self.rd_dma[k] = []
        for k in reads:
            if dma:
                self.rd_dma.setdefault(k, []).append(I)
            else:
                self.rd_eng.setdefault(k, {})[eng] = I
        best = {}
        seen = set()
        for d, raw in cand:
            if d is I:
                continue
            if d.dma:
                if id(d) not in seen:
                    seen.add(id(d))
                    I.deps.append(d)
                continue
            if d.eng == eng and not dma:
                if eng == "pe":
                    continue
            b = best.get(d.eng)
            if b is None or d.pos > b.pos:
                best[d.eng] = d
        for d in best.values():
            d.sig = True
            I.deps.append(d)
        self.lists[eng].append(I)
        return I

    def finalize(self):
        for e in ENGS:
            cnt = 0
            dcnt = {}
            k = 0
            for I in self.lists[e]:
                if I.dma:
                    I.semid = k % self.n_dma_sems
                    k += 1
                    dcnt[I.semid] = dcnt.get(I.semid, 0) + 16
                    I.semval = dcnt[I.semid]
                elif I.sig:
                    cnt += 1
                    I.semval = cnt

    def run_engine(self, e, engobj, sems, dma_sems):
        waited = {}
        for I in self.lists[e]:
            need = {}
            for d in I.deps:
                if d.dma:
                    key = ("d", d.eng, d.semid)
                    s = dma_sems[d.eng][d.semid]
                else:
                    key = ("e", d.eng)
                    s = sems[d.eng]
                if need.get(key, (None, 0))[1] < d.semval:
                    need[key] = (s, d.semval)
            if I.dma and I.semval > 16:
                key = ("d", e, I.semid)
                if need.get(key, (None, 0))[1] < I.semval - 16:
                    need[key] = (dma_sems[e][I.semid], I.semval - 16)
            for key, (s, v) in need.items():
                if waited.get(key, 0) < v:
                    engobj.wait_ge(s, v)
                    waited[key] = v
            ins = getattr(engobj, I.fn[0])(**I.fn[1])
            if I.dma:
                ins.then_inc(dma_sems[e][I.semid], 16)
            elif I.sig:
                ins.then_inc(sems[e], 1)

    def final_wait(self, engobj, dma_sems):
        for e in ENGS:
            last = {}
            for I in self.lists[e]:
                if I.dma:
                    last[I.semid] = I.semval
            for sid, v in last.items():
                engobj.wait_ge(dma_sems[e][sid], v)


def build_program(ng_run=NG, nl_run=DEPTH):
    nc = bass.Bass("TRN2", target_bir_lowering=False)

    def din(name, shape):
        return nc.dram_tensor(name, list(shape), F32, kind="ExternalInput").ap()

    def dout(name, shape):
        return nc.dram_tensor(name, list(shape), F32, kind="ExternalOutput").ap()

    xp = din("xp", [SEQ, D])
    xs = din("xs", [NS, D])
    c17d = din("c17", [NS + 1, D])
    sgl = din("sgl", [DEPTH, NS, H, DK, DV])
    scv = din("scv", [DEPTH, NS, 2, 512])
    w_ada = din("w_ada", [DEPTH, D, 3 * D])
    b_ada = din("b_ada", [DEPTH, 3 * D])
    w_in = din("w_in", [DEPTH, D, 7184])
    w_a2 = din("w_a2", [DEPTH, 16, 512])
    b_a = din("b_a", [DEPTH, 512])
    gng = din("gng", [DEPTH, DV])
    conv_w = din("conv_w", [DEPTH, 3, 512])
    w_pa = din("w_pa", [DEPTH, D, D])
    w_pb = din("w_pb", [DEPTH, 512, D])
    w_o = din("w_o", [DEPTH, D, D])
    ln_g = din("ln_g", [DEPTH, D])
    ln_b = din("ln_b", [DEPTH, D])
    yp = dout("yp", [SEQ, D])
    ys = dout("ys", [NS, D])
    gp = dout("gp", [DEPTH, H, DK, DV])
    cp = dout("cp", [DEPTH, 2, 512])
    gs = dout("gs", [DEPTH, NS, H, DK, DV])
    cs = dout("cs", [DEPTH, NS, 2, 512])

    with ExitStack() as es:
        def sb(name, shape, dt):
            return es.enter_context(nc.sbuf_tensor(name, list(shape), dt))

        x_sb = sb("x_sb", [128, GT, D], F32)
        xs_sb = sb("xs_sb", [NS, D], F32)
        uT = sb("uT", [128, NKC, TW], BF16)
        og = sb("og", [128, NKC, TW], BF16)
        yc = sb("yc", [128, 4, TW], BF16)
        wring = sb("wring", [128, NW, NKC, WC], BF16)
        gate_bc = sb("gate_bc", [128, DEPTH, D], F32)
        gate_s = sb("gate_s", [NS, DEPTH, D], F32)
        lng_bc = sb("lng_bc", [128, D], F32)
        lnb_bc = sb("lnb_bc", [128, D], F32)
        S = sb("S", [128, DEPTH, H * DV], F32)
        S_bf = sb("S_bf", [128, DEPTH, H * DV], BF16)
        Sbuf = sb("Sbuf", [128, 3, H * DV], F32)
        sbf = sb("sbf", [128, 2, H * DV], BF16)
        ident_f = sb("ident_f", [128, 128], F32)
        ident_b = sb("ident_b", [128, 128], BF16)
        tri = sb("tri", [128, 128], F32)
        mask4 = sb("mask4", [128, 4, 128], F32)
        ones_b = sb("ones_b", [128, 128], BF16)
        alr_aug = sb("alr_aug", [17, TW], F32)
        w_a2aug = sb("w_a2aug", [17, DEPTH, 512], F32)
        gnorm = sb("gnorm", [128, DEPTH, 2], F32)
        convw = sb("convw", [128, DEPTH, 4, 3], F32)
        ada_sb = sb("ada_sb", [128, DEPTH, 16, 17], F32)
        b_adaT = sb("b_adaT", [128, DEPTH, 24], F32)
        carry = sb("carry", [128, DEPTH, 4, 2], F32)
        cT = sb("cT", [128, NKC, 17], BF16)
        tmp_s = sb("tmp_s", [128, NKC, NS], F32)
        st = sb("st", [128, 2, 6], F32)
        mv = sb("mv", [128, 8], F32)
        st_s = sb("st_s", [NS, 2, 6], F32)
        mv_s = sb("mv_s", [NS, 8], F32)
        dummy = sb("dummy_t", [128, 8], F32)
        ARENA_F = 16384 + 256 + 1024
        arena = sb("arena", [128, ARENA_F], F32)

        ps = es.enter_context(nc.psum_tensor("ps", [128, 8, 512], F32))

        sems = {e: es.enter_context(nc.semaphore(f"s_{e}")) for e in ENGS}
        dma_sems = {e: [es.enter_context(nc.semaphore(f"d_{e}{i}")) for i in range(8)]
                    for e in ("sp", "pool")}
        block = es.enter_context(nc.Block())

        class Arena:
            def __init__(self):
                self.off = 0

            def f32(self, n, parts=128):
                a = arena[0:parts, self.off:self.off + n]
                self.off += n
                assert self.off <= ARENA_F, self.off
                return a

            def bf16(self, n, parts=128):
                nf = (n + 1) // 2
                a = arena[0:parts, self.off:self.off + nf].bitcast(BF16)
                self.off += nf
                assert self.off <= ARENA_F, self.off
                return a

        def v3(ap, c):
            return ap.rearrange("p (c t) -> p c t", c=c)

        A = Arena()
        E_q = v3(A.f32(4 * BT), 4)
        E_k = v3(A.f32(4 * BT), 4)
        q_in = v3(A.bf16(4 * BT), 4)
        k_in = v3(A.bf16(4 * BT), 4)
        sgt = v3(A.bf16(8 * TW), 8)
        v_tok = v3(A.bf16(GT * 1024), GT)
        k_tok = v3(A.bf16(2 * 512), 2)
        _o = A.off
        e1 = A.f32(512)
        A.off = _o
        ktok_s = A.f32(512, NS)
        sp = v3(A.f32(2 * 512), 2)
        v_s = A.bf16(1024, NS)
        attm = v3(A.bf16(2 * 512), 2)
        sq = v3(A.bf16(8 * 128), 8)
        lnv = A.f32(512)
        rstd = v3(A.f32(512), 4)
        t1 = v3(A.f32(1024), 8)
        q_s = v3(A.bf16(4 * NS), 4)
        a_s = v3(A.f32(4 * NS), 4)
        km = A.bf16(512, NS)
        sq_s = v3(A.bf16(8 * NS), 8)
        lnv_s = A.f32(4 * NS)
        rstd_s = v3(A.f32(4 * NS), 4)
        t1_s = v3(A.f32(8 * NS), 8)
        g_end = A.off
        A = Arena()
        c17 = A.f32(D, NS + 1)
        cTb = v3(A.bf16(NKC * 128), NKC)
        bgate = A.f32(D)
        A = Arena()
        ccy = v3(A.f32(4 * TW), 4)
        pbuf = v3(A.f32(4 * (BT + 2)), 4)
        tmp1 = A.f32(512)
        tmp2 = A.f32(512)
        tmp3 = A.f32(512)
        p_s = v3(A.f32(4 * NS), 4)
        scT = v3(A.f32(4 * 2 * NS), 4)
        sc_tok = A.f32(512, 2 * NS)
        tcs1 = A.f32(NS)
        tcs2 = A.f32(NS)
        tmp3s = v3(A.f32(4 * NS), 4)
        cst = A.f32(512, 2)
        pst = A.f32(512, NS)
        sgm = A.f32(512)
        scm = A.f32(512)
        tm1 = A.f32(512)
        tm2 = A.f32(512)
        sg_s = A.f32(2 * NS)
        t_s = A.f32(2 * NS)
        mg = v3(A.bf16(NKC * TW), NKC)
        to = A.f32(D)
        to_s = A.f32(D, NS)
        cmo_end = A.off

        def emit_all(P, record, wlist_in):
            def PE(meth, kw, r=(), x=()):
                return P.add("pe", (meth, kw), reads=r, excl=x)

            def ACT(meth, kw, r=(), w=(), x=()):
                return P.add("act", (meth, kw), reads=r, writes=w, excl=x)

            def DVE(meth, kw, r=(), w=(), x=()):
                return P.add("dve", (meth, kw), reads=r, writes=w, excl=x)

            def POOL(meth, kw, r=(), w=()):
                return P.add("pool", (meth, kw), reads=r, writes=w)

            def DMA(q, meth, kw, r=(), w=()):
                return P.add(q, (meth, kw), reads=r, writes=w, dma=True)

            EP = "EPOCH"

            def ar(*keys):
                return [EP] + ["A:" + k for k in keys]

            def aw(*keys):
                return ["A:" + k for k in keys]

            def epoch_barrier():
                DVE("memset", dict(ap=dummy[:, 0:1], constant=0.0), w=[EP])

            free_banks = list(range(8))

            def bget():
                assert free_banks, "PSUM banks exhausted"
                return free_banks.pop(0)

            def bget2():
                for i in range(len(free_banks)):
                    b = free_banks[i]
                    if b % 2 == 0 and (b + 1) in free_banks:
                        free_banks.remove(b)
                        free_banks.remove(b + 1)
                        return b
                raise AssertionError("no PSUM bank pair free")

            def bput(b, n=1):
                for i in range(n):
                    free_banks.append(b + i)

            def BK(b):
                return ("B", b)

            def mm(out, lhsT, rhs, start, stop, r, b):
                PE("matmul", dict(out=out, lhsT=lhsT, rhs=rhs, start=start, stop=stop), r=r, x=[BK(b)])

            def tp(out, in_, ident, r, b):
                PE("transpose", dict(out=out, in_=in_, identity=ident), r=r, x=[BK(b)])

            wlist = [] if record else wlist_in
            OFFS = {"q": O_Q, "k": O_K, "v": O_V, "gg": O_GG, "cc": O_CC, "ch": O_CH, "cb": O_CB, "gc": O_GC, "mg": O_MG, "mc": O_MC}

            def wspec(tag):
                nm = tag[0]
                if nm == "ada":
                    return w_ada[tag[1], :, tag[2] * WC:(tag[2] + 1) * WC], NKC, WC
                if nm == "alr":
                    return w_in[tag[2], :, O_ALR:O_ALR + 16], NKC, 16
                l, i = tag[2], tag[3]
                if nm in OFFS:
                    return w_in[l, :, OFFS[nm] + i * WC: OFFS[nm] + (i + 1) * WC], NKC, WC
                if nm == "pa":
                    return w_pa[l, :, i * WC:(i + 1) * WC], NKC, WC
                if nm == "pb":
                    return w_pb[l, :, i * WC:(i + 1) * WC], 4, WC
                if nm == "o":
                    return w_o[l, :, i * WC:(i + 1) * WC], NKC, WC
                raise KeyError(tag)

            wstate = {"issued": 0, "taken": 0}
            released = [False] * (100000 if record else len(wlist))

            def wpump():
                while wstate["issued"] < len(wlist):
                    j = wstate["issued"]
                    if j >= NW and not released[j - NW]:
                        break
                    src, nkc, ncols = wspec(wlist[j])
                    slot = j % NW
                    dst = wring[:, slot, 0:nkc, 0:ncols]
                    srcv = src.rearrange("(kc p) c -> p kc c", p=128)
                    DMA("pool", "dma_start", dict(out=dst, in_=srcv), w=[("W", slot)])
                    wstate["issued"] += 1

            def wtake(tag):
                i = wstate["taken"]
                if record:
                    wlist.append(tag)
                    wstate["taken"] += 1
                    return i, wring[:, i % NW], ("W", i % NW)
                assert wlist[i] == tag, (wlist[i], tag)
                wpump()
                assert wstate["issued"] > i, "weight ring deadlock"
                wstate["taken"] += 1
                return i, wring[:, i % NW], ("W", i % NW)

            def wrel(i):
                released[i] = True
                if not record:
                    wpump()

            POOL("memset", dict(ap=ident_f[:], constant=0.0), w=["ident_f"])
            POOL("affine_select", dict(out=ident_f[:], in_=ident_f[:], pattern=[[-1, 128]], compare_op=ALU.not_equal,
                                           fill=1.0, base=0, channel_multiplier=1), r=["ident_f"], w=["ident_f"])
            POOL("tensor_copy", dict(out=ident_b[:], in_=ident_f[:]), r=["ident_f"], w=["ident_b"])
            POOL("memset", dict(ap=tri[:], constant=1.0), w=["tri"])
            POOL("affine_select", dict(out=tri[:], in_=tri[:], pattern=[[1, 128]], compare_op=ALU.is_ge,
                                           fill=0.0, base=0, channel_multiplier=-1), r=["tri"], w=["tri"])
            for h in range(4):
                POOL("tensor_copy", dict(out=mask4[:, h, :], in_=tri[:]), r=["tri"], w=["mask4"])
            POOL("memset", dict(ap=ones_b[:], constant=1.0), w=["ones_b"])
            POOL("memset", dict(ap=alr_aug[:], constant=1.0), w=["alr_aug"])
            POOL("memset", dict(ap=carry[:], constant=0.0), w=["carry"])
            POOL("memset", dict(ap=S[:], constant=0.0), w=[("S", l_, h_) for l_ in range(DEPTH) for h_ in range(H)])
            POOL("memset", dict(ap=S_bf[:], constant=0.0), w=[("S_bf", l_, h_) for l_ in range(DEPTH) for h_ in range(H)])

            DMA("sp", "dma_start", dict(out=c17, in_=c17d), w=aw("c17"))
            DMA("sp", "dma_start", dict(out=xs_sb[:], in_=xs), w=["xs"])
            for l in range(DEPTH):
                DMA("sp", "dma_start", dict(out=w_a2aug[0:16, l, :], in_=w_a2[l]), w=["w_a2aug"])
                DMA("sp", "dma_start", dict(out=w_a2aug[16:17, l, :], in_=b_a[l:l + 1, :]), w=["w_a2aug"])
                DMA("sp", "dma_start", dict(out=gnorm[:, l, :], in_=gng[l].rearrange("(c p) -> p c", p=128),
                                                     allow_slow_non_contiguous=True), w=["gnorm"])
                for ti in range(3):
                    DMA("sp", "dma_start", dict(out=convw[:, l, :, ti], in_=conv_w[l, ti].rearrange("(c p) -> p c", p=128),
                                                allow_slow_non_contiguous=True), w=["convw"])
                DMA("sp", "dma_start", dict(out=b_adaT[:, l, :], in_=b_ada[l].rearrange("(f p) -> p f", p=128),
                                                     allow_slow_non_contiguous=True), w=["b_adaT"])

            b = bget()
            for c in range(NKC):
                tp(ps[:, b, c * 17:(c + 1) * 17], c17[:, c * 128:(c + 1) * 128], ident_f[0:17, 0:17], ["ident_f"] + ar("c17"), b)
            ACT("copy", dict(out=cT[:].rearrange("p c t -> p (c t)"), in_=ps[:, b, 0:NKC * 17]), w=["cT"], x=[BK(b)])
            bput(b)
            for c in range(NKC):
                DVE("tensor_scalar", dict(out=cTb[:, c, :], in0=ones_b[:], scalar1=cT[:, c, 0:1], scalar2=None, op0=ALU.mult),
                    r=["ones_b", "cT"] + ar(), w=aw("cTb"))
            for l in range(DEPTH):
                DVE("tensor_scalar", dict(out=b_adaT[:, l, 8:16], in0=b_adaT[:, l, 8:16], scalar1=1.0, scalar2=None, op0=ALU.add),
                    r=["b_adaT"], w=["b_adaT"])
                DMA("sp", "dma_start", dict(out=bgate, in_=b_ada[l, 2 * D:3 * D].partition_broadcast(128)), w=aw("bgate"))
                b = bget()
                for wi in range(8):
                    i, slot, wk = wtake(("ada", l, wi))
                    for sub in range(2):
                        fc = wi * 2 + sub
                        for kc in range(NKC):
                            mm(ps[:, b, fc * 17:(fc + 1) * 17], slot[:, kc, sub * 128:(sub + 1) * 128], cT[:, kc, :],
                               kc == 0, kc == NKC - 1, [wk, "cT"], b)
                    wrel(i)
                DVE("tensor_tensor", dict(out=ada_sb[:, l, :, :], in0=ps[:, b, 0:16 * 17].rearrange("p (f t) -> p f t", f=16),
                                                       in1=b_adaT[:, l, 0:16].unsqueeze(2).to_broadcast([128, 16, 17]), op=ALU.add),
                    r=["b_adaT"], w=["ada_sb"], x=[BK(b)])
                bput(b)
                for half in range(2):
                    bp = bget()
                    bs = bget()
                    for wj in range(2):
                        wi = 8 + half * 2 + wj
                        i, slot, wk = wtake(("ada", l, wi))
                        for kc in range(NKC):
                            mm(ps[:, bp, wj * WC:(wj + 1) * WC], cTb[:, kc, :], slot[:, kc, :], kc == 0, kc == NKC - 1, [wk] + ar("cTb"), bp)
                        for kc in range(NKC):
                            mm(ps[0:NS, bs, wj * WC:(wj + 1) * WC], cT[:, kc, 1:17], slot[:, kc, :], kc == 0, kc == NKC - 1, [wk, "cT"], bs)
                        wrel(i)
                    DVE("tensor_tensor", dict(out=gate_bc[:, l, half * 512:(half + 1) * 512], in0=ps[:, bp, :],
                                                                        in1=bgate[:, half * 512:(half + 1) * 512], op=ALU.add),
                        r=ar("bgate"), w=["gate_bc"], x=[BK(bp)])
                    DVE("tensor_tensor", dict(out=gate_s[:, l, half * 512:(half + 1) * 512], in0=ps[0:NS, bs, :],
                                                                        in1=bgate[0:NS, half * 512:(half + 1) * 512], op=ALU.add),
                        r=ar("bgate"), w=["gate_s"], x=[BK(bs)])
                    bput(bp)
                    bput(bs)

            for g in range(ng_run):
                has_s = (g == SG)
                for l in range(nl_run):
                    last_g = (g == ng_run - 1)
                    if l == 0:
                        for j in range(GT):
                            DMA("sp", "dma_start", dict(out=x_sb[:, j, :], in_=xp[g * BT + j * 128: g * BT + (j + 1) * 128, :]), w=[("x", j)])
                    DMA("sp", "dma_start", dict(out=lng_bc[:], in_=ln_g[l].partition_broadcast(128)), w=["lng"])
                    DMA("sp", "dma_start", dict(out=lnb_bc[:], in_=ln_b[l].partition_broadcast(128)), w=["lnb"])

                    for c in range(NKC):
                        b = bget()
                        for j in range(GT):
                            tp(ps[:, b, j * 128:(j + 1) * 128], x_sb[:, j, c * 128:(c + 1) * 128], ident_f[:], ["ident_f", ("x", j)], b)
                        ACT("activation", dict(out=uT[:, c, 0:BT], in_=ps[:, b, :], func=AF.Identity,
                                                                  bias=ada_sb[:, l, c, 0:1], scale=ada_sb[:, l, 8 + c, 0:1]),
                            r=["ada_sb"], w=[("uT", c)], x=[BK(b)])
                        bput(b)
                    if has_s:
                        b = bget()
                        for c in range(NKC):
                            tp(ps[:, b, c * NS:(c + 1) * NS], xs_sb[:, c * 128:(c + 1) * 128], ident_f[0:NS, 0:NS], ["ident_f", "xs"], b)
                        DVE("tensor_tensor", dict(out=tmp_s[:], in0=ps[:, b, 0:NKC * NS].rearrange("p (c t) -> p c t", c=NKC),
                                                               in1=ada_sb[:, l, 8:16, 1:17], op=ALU.mult),
                            r=["ada_sb"], w=["tmp_s"], x=[BK(b)])
                        bput(b)
                        DVE("tensor_tensor", dict(out=uT[:, :, BT:TW], in0=tmp_s[:], in1=ada_sb[:, l, 0:8, 1:17], op=ALU.add),
                            r=["ada_sb", "tmp_s"], w=["uTs"])

                    uTr = [("uT", c) for c in range(NKC)]

                    epoch_barrier()
                    i, slot, wk = wtake(("alr", g, l))
                    b = bget()
                    for kc in range(NKC):
                        mm(ps[0:16, b, :], slot[:, kc, 0:16], uT[:, kc, 0:BT], kc == 0, kc == NKC - 1, [wk, ("uT", kc)], b)
                    ACT("copy", dict(out=alr_aug[0:16, 0:BT], in_=ps[0:16, b, :]), w=["alr_aug"], x=[BK(b)])
                    bput(b)
                    if has_s:
                        b = bget()
                        for kc in range(NKC):
                            mm(ps[0:16, b, 0:NS], slot[:, kc, 0:16], uT[:, kc, BT:TW], kc == 0, kc == NKC - 1, [wk, "uTs"], b)
                        ACT("copy", dict(out=alr_aug[0:16, BT:TW], in_=ps[0:16, b, 0:NS]), w=["alr_aug_s"], x=[BK(b)])
                        bput(b)
                    wrel(i)
                    bvs_h = [None]
                    bgs_h = [None]

                    def do_v(h):
                        i, slot, wk = wtake(("v", g, l, h))
                        for jp in range(2):
                            b = bget()
                            for jj in range(2):
                                j = jp * 2 + jj
                                for kc in range(NKC):
                                    mm(ps[:, b, jj * WC:(jj + 1) * WC], uT[:, kc, j * 128:(j + 1) * 128], slot[:, kc, :], kc == 0, kc == NKC - 1,
                                       [wk, ("uT", kc)], b)
                            ACT("copy", dict(out=v_tok[:, jp * 2:jp * 2 + 2, h * DV:(h + 1) * DV],
                                             in_=ps[:, b, :].rearrange("p (j v) -> p j v", j=2)),
                                r=ar(), w=aw(f"v_tok{jp * 2}", f"v_tok{jp * 2 + 1}"), x=[BK(b)])
                            bput(b)
                        if has_s:
                            if h % 2 == 0:
                                bvs_h[0] = bget()
                            bvs = bvs_h[0]
                            for kc in range(NKC):
                                mm(ps[0:NS, bvs, (h % 2) * WC:(h % 2 + 1) * WC], uT[:, kc, BT:TW], slot[:, kc, :], kc == 0, kc == NKC - 1, [wk, "uTs"], bvs)
                            if h % 2 == 1:
                                ACT("copy", dict(out=v_s[:, (h - 1) * DV:(h + 1) * DV], in_=ps[0:NS, bvs, :]),
                                    r=ar(), w=aw("v_s"), x=[BK(bvs)])
                                bput(bvs)
                        wrel(i)

                    def do_gg(wi):
                        if has_s and wi == 0:
                            bgs_h[0] = bget()
                        bgs = bgs_h[0]
                        i, slot, wk = wtake(("gg", g, l, wi))
                        for sub in range(2):
                            fc = wi * 2 + sub
                            b = bget()
                            for kc in range(NKC):
                                mm(ps[:, b, :], slot[:, kc, sub * 128:(sub + 1) * 128], uT[:, kc, 0:BT], kc == 0, kc == NKC - 1, [wk, ("uT", kc)], b)
                            ACT("activation", dict(out=sgt[:, fc, 0:BT], in_=ps[:, b, :], func=AF.Silu), r=ar(), w=aw(f"sg{fc}"), x=[BK(b)])
                            bput(b)
                            if has_s:
                                for kc in range(NKC):
                                    mm(ps[:, bgs, fc * NS:(fc + 1) * NS], slot[:, kc, sub * 128:(sub + 1) * 128], uT[:, kc, BT:TW],
                                       kc == 0, kc == NKC - 1, [wk, "uTs"], bgs)
                        wrel(i)
                        if has_s and wi == 3:
                            ACT("activation", dict(out=sgt[:, :, BT:TW], in_=ps[:, bgs, 0:NKC * NS].rearrange("p (c t) -> p c t", c=NKC), func=AF.Silu),
                                r=ar(), w=aw("sg_s"), x=[BK(bgs)])
                            bput(bgs)

                    bulk = [(do_v, h) for h in range(H)] + [(do_gg, wi) for wi in range(4)]
                    EQK = aw(*[f"E_q{h}" for h in range(H)])
                    EKK = aw(*[f"E_k{h}" for h in range(H)])
                    for j in range(GT):
                        jj = j % 2
                        b = bget()
                        mm(ps[:, b, :], alr_aug[0:17, j * 128:(j + 1) * 128], w_a2aug[0:17, l, :], True, True, ["alr_aug", "w_a2aug"], b)
                        ACT("activation", dict(out=e1, in_=ps[:, b, :], func=AF.Exp, scale=-1.0), r=ar(), w=aw("e1"), x=[BK(b)])
                        bput(b)
                        ACT("activation", dict(out=sp[:, jj, :], in_=e1, func=AF.Ln, bias=1.0), r=ar("e1"), w=aw("sp"))
                        bc = bget()
                        for h in range(H):
                            mm(ps[:, bc, h * 128:(h + 1) * 128], sp[:, jj, h * 128:(h + 1) * 128], tri[:], True, True, ["tri"] + ar("sp"), bc)
                        csv = ps[:, bc, :].rearrange("p (h t) -> p h t", h=H)
                        ACT("activation", dict(out=E_q[:, :, j * 128:(j + 1) * 128], in_=csv, func=AF.Exp, scale=-1.0 / 16), r=ar(), w=EQK, x=[BK(bc)])
                        ACT("activation", dict(out=E_k[:, :, j * 128:(j + 1) * 128], in_=csv, func=AF.Exp, scale=1.0 / 16), r=ar(), w=EKK, x=[BK(bc)])
                        bput(bc)
                        f_, a_ = bulk.pop(0)
                        f_(a_)
                    if has_s:
                        b = bget()
                        mm(ps[0:NS, b, :], alr_aug[0:17, BT:TW], w_a2aug[0:17, l, :], True, True, ["alr_aug_s", "w_a2aug"], b)
                        ACT("activation", dict(out=e1[0:NS, :], in_=ps[0:NS, b, :], func=AF.Exp, scale=-1.0), r=ar(), w=aw("e1"), x=[BK(b)])
                        bput(b)
                        ACT("activation", dict(out=sp[0:NS, 0, :], in_=e1[0:NS, :], func=AF.Ln, bias=1.0), r=ar("e1"), w=aw("sp"))
                        b = bget()
                        for h in range(H):
                            tp(ps[:, b, h * NS:(h + 1) * NS], sp[0:NS, 0, h * 128:(h + 1) * 128], ident_f[0:NS, 0:NS], ["ident_f"] + ar("sp"), b)
                        ACT("activation", dict(out=a_s[:].rearrange("p h s -> p (h s)"), in_=ps[:, b, 0:H * NS], func=AF.Exp, scale=-1.0 / 16),
                            r=ar(), w=aw("a_s"), x=[BK(b)])
                        bput(b)
                    while bulk:
                        f_, a_ = bulk.pop(0)
                        f_(a_)
                    for nm, Eb, dst in (("q", E_q, q_in), ("k", E_k, k_in)):
                        bs = bget() if (has_s and nm == "q") else None
                        bkt = bget() if (has_s and nm == "k") else None
                        for wi in range(2):
                            i, slot, wk = wtake((nm, g, l, wi))
                            for sub in range(2):
                                h = wi * 2 + sub
                                b = bget()
                                for kc in range(NKC):
                                    mm(ps[:, b, :], slot[:, kc, sub * 128:(sub + 1) * 128], uT[:, kc, 0:BT], kc == 0, kc == NKC - 1, [wk, ("uT", kc)], b)
                                if nm == "q":
                                    DVE("scalar_tensor_tensor", dict(out=q_in[:, h, :], in0=ps[:, b, :], scalar=QSCALE, in1=E_q[:, h, :],
                                                                                   op0=ALU.mult, op1=ALU.mult),
                                        r=ar(f"E_q{h}"), w=aw(f"q_in{h}"), x=[BK(b)])
                                else:
                                    DVE("tensor_tensor", dict(out=k_in[:, h, :], in0=ps[:, b, :], in1=E_k[:, h, :], op=ALU.mult),
                                        r=ar(f"E_k{h}"), w=aw(f"k_in{h}"), x=[BK(b)])
                                bput(b)
                                if has_s and nm == "q":
                                    for kc in range(NKC):
                                        mm(ps[:, bs, h * NS:(h + 1) * NS], slot[:, kc, sub * 128:(sub + 1) * 128], uT[:, kc, BT:TW],
                                           kc == 0, kc == NKC - 1, [wk, "uTs"], bs)
                            if has_s and nm == "k":
                                for kc in range(NKC):
                                    mm(ps[0:NS, bkt, wi * WC:(wi + 1) * WC], uT[:, kc, BT:TW], slot[:, kc, :], kc == 0, kc == NKC - 1, [wk, "uTs"], bkt)
                            wrel(i)
                        if has_s and nm == "q":
                            ACT("activation", dict(out=q_s[:].rearrange("p h s -> p (h s)"), in_=ps[:, bs, 0:H * NS], func=AF.Identity, scale=QSCALE),
                                r=ar(), w=aw("q_s"), x=[BK(bs)])
                        if has_s and nm == "k":
                            ACT("copy", dict(out=ktok_s, in_=ps[0:NS, bkt, :]), r=ar(), w=aw("e1"), x=[BK(bkt)])
                            bput(bkt)
                        if has_s and nm == "q":
                            bput(bs)
                    bos = bget() if has_s else None

                    def sample_load(s):
                        DMA("sp", "dma_start", dict(out=Sbuf[:, s % 3, :].rearrange("p (h v) -> p h v", h=H), in_=sgl[l, s].rearrange("h k v -> k h v")),
                            w=[("Sbuf", s % 3)])

                    if has_s:
                        for s_ in range(3):
                            sample_load(s_)

                    def sample_update(s):
                        buf = Sbuf[:, s % 3, :]
                        bk = ("Sbuf", s % 3)
                        sbv = sbf[:, s % 2, :]
                        sk = ("sbf", s % 2)
                        DVE("tensor_scalar", dict(out=km, in0=ktok_s, scalar1=ident_f[0:NS, s:s + 1], scalar2=None, op0=ALU.mult),
                            r=["ident_f"] + ar("e1"), w=aw("km"))
                        b2 = bget2()
                        for h in range(H):
                            mm(ps[:, b2 + h // 2, (h % 2) * DV:(h % 2 + 1) * DV], km[:, h * 128:(h + 1) * 128], v_s[:, h * DV:(h + 1) * DV], True, True,
                               ar("km", "v_s"), b2 + h // 2)
                        for h in range(H):
                            ACT("activation", dict(out=buf[:, h * DV:(h + 1) * DV], in_=buf[:, h * DV:(h + 1) * DV], func=AF.Identity,
                                                   scale=a_s[:, h, s:s + 1]),
                                r=[bk] + ar("a_s"), w=[bk])
                        for half in range(2):
                            DVE("tensor_tensor", dict(out=buf[:, half * 512:(half + 1) * 512], in0=ps[:, b2 + half, :],
                                                      in1=buf[:, half * 512:(half + 1) * 512], op=ALU.add),
                                r=[bk], w=[bk], x=[BK(b2 + half)])
                        bput(b2, 2)
                        DMA("sp", "dma_start", dict(out=gs[l, s].rearrange("h k v -> k h v"), in_=buf.rearrange("p (h v) -> p h v", h=H)),
                            r=[bk], w=[("gs", l, s)])
                        DVE("tensor_copy", dict(out=sbv, in_=buf), r=[bk], w=[sk])
                        if s + 3 < NS:
                            sample_load(s + 3)
                        for h in range(H):
                            for vc in range(2):
                                idx = h * 2 + vc
                                mm(ps[:, bos, idx * NS + s: idx * NS + s + 1], sbv[:, idx * 128:(idx + 1) * 128], q_s[:, h, s:s + 1], True, True,
                                   [sk] + ar("q_s"), bos)

                    for j in range(GT):
                        jj = j % 2
                        jc0, jc1 = j * 128, (j + 1) * 128
                        b = bget()
                        psb = ps[:, b, :].bitcast(BF16)
                        for h in range(H):
                            tp(psb[:, h * 128:(h + 1) * 128], k_in[:, h, jc0:jc1], ident_b[:], ["ident_b"] + ar(f"k_in{h}"), b)
                        ACT("copy", dict(out=k_tok[:, jj, :], in_=psb[:, 0:512]), r=ar(), w=aw(f"k_tok{jj}"), x=[BK(b)])
                        bput(b)
                        b = bget()
                        for h in range(H):
                            mm(ps[:, b, h * 128:(h + 1) * 128], k_in[:, h, jc0:jc1], q_in[:, h, jc0:jc1], True, True, ar(f"k_in{h}", f"q_in{h}"), b)
                        DVE("tensor_tensor", dict(out=attm[:, jj, :], in0=ps[:, b, :], in1=mask4[:].rearrange("p h t -> p (h t)"), op=ALU.mult),
                            r=["mask4"] + ar(), w=aw(f"attm{jj}"), x=[BK(b)])
                        bput(b)
                        bo = bget2()
                        for h in range(H):
                            for vc in range(2):
                                idx = h * 2 + vc
                                out = ps[:, bo + idx // 4, (idx % 4) * 128:(idx % 4 + 1) * 128]
                                mm(out, v_tok[:, j, idx * 128:(idx + 1) * 128], attm[:, jj, h * 128:(h + 1) * 128], True, False,
                                   ar(f"v_tok{j}", f"attm{jj}"), bo + idx // 4)
                                mm(out, S_bf[:, l, idx * 128:(idx + 1) * 128], q_in[:, h, jc0:jc1], False, True, [("S_bf", l, h)] + ar(f"q_in{h}"), bo + idx // 4)
                        b2 = bget2()
                        for h in range(H):
                            mm(ps[:, b2 + h // 2, (h % 2) * DV:(h % 2 + 1) * DV], k_tok[:, jj, h * 128:(h + 1) * 128], v_tok[:, j, h * DV:(h + 1) * DV], True, True,
                               ar(f"k_tok{jj}", f"v_tok{j}"), b2 + h // 2)
                        Pst = t1[:].rearrange("p c t -> p (c t)")
                        for half in range(2):
                            DVE("tensor_tensor", dict(out=Pst[:, half * 512:(half + 1) * 512], in0=ps[:, b2 + half, :],
                                                      in1=S[:, l, half * 512:(half + 1) * 512], op=ALU.add),
                                r=[("S", l, 2 * half), ("S", l, 2 * half + 1)] + ar(), w=aw(f"t1{half}"), x=[BK(b2 + half)])
                        bput(b2, 2)
                        for h in range(H):
                            dcol = E_q[:, h, jc1 - 1:jc1]
                            ACT("activation", dict(out=S_bf[:, l, h * DV:(h + 1) * DV], in_=Pst[:, h * DV:(h + 1) * DV], func=AF.Identity, scale=dcol),
                                r=ar(f"E_q{h}", f"t1{h // 2}"), w=[("S_bf", l, h)])
                            DVE("tensor_scalar", dict(out=S[:, l, h * DV:(h + 1) * DV], in0=Pst[:, h * DV:(h + 1) * DV], scalar1=dcol, scalar2=None,
                                                      op0=ALU.mult),
                                r=ar(f"E_q{h}", f"t1{h // 2}"), w=[("S", l, h)])
                        ov = ps[:, bo:bo + 2, :].rearrange("p b (c t) -> p (b c) t", c=4)
                        ACT("activation", dict(out=sq[:], in_=ov, func=AF.Square), r=ar(), w=aw("sq"), x=[BK(bo), BK(bo + 1)])
                        bss = bget()
                        for h in range(H):
                            for vc in range(2):
                                mm(ps[:, bss, h * 128:(h + 1) * 128], ones_b[:], sq[:, h * 2 + vc, :], vc == 0, vc == 1, ["ones_b"] + ar("sq"), bss)
                        ACT("activation", dict(out=lnv, in_=ps[:, bss, :], func=AF.Ln, bias=EPS, scale=1.0 / DV), r=ar(), w=aw("lnv"), x=[BK(bss)])
                        bput(bss)
                        ACT("activation", dict(out=rstd[:].rearrange("p h t -> p (h t)"), in_=lnv, func=AF.Exp, scale=-0.5), r=ar("lnv"), w=aw("rstd"))
                        for vc in range(2):
                            DVE("scalar_tensor_tensor", dict(out=t1[:, vc::2, :], in0=ov[:, vc::2, :], scalar=gnorm[:, l, vc:vc + 1], in1=rstd[:],
                                                                               op0=ALU.mult, op1=ALU.mult),
                                r=["gnorm"] + ar("rstd"), w=aw("t10", "t11"), x=[BK(bo), BK(bo + 1)])
                        bput(bo, 2)
                        DVE("tensor_tensor", dict(out=og[:, :, jc0:jc1], in0=t1[:], in1=sgt[:, :, jc0:jc1], op=ALU.mult),
                            r=ar("t10", "t11", *[f"sg{fc}" for fc in range(8)]), w=[("og", j)])
                        if has_s:
                            for s in range(j * 4, (j + 1) * 4):
                                sample_update(s)
                    if has_s:
                        ovs = ps[:, bos, 0:NKC * NS].rearrange("p (c t) -> p c t", c=NKC)
                        ACT("activation", dict(out=sq_s[:], in_=ovs, func=AF.Square), r=ar(), w=aw("sq_s"), x=[BK(bos)])
                        bss = bget()
                        for h in range(H):
                            for vc in range(2):
                                mm(ps[:, bss, h * NS:(h + 1) * NS], ones_b[:], sq_s[:, h * 2 + vc, :], vc == 0, vc == 1, ["ones_b"] + ar("sq_s"), bss)
                        ACT("activation", dict(out=lnv_s, in_=ps[:, bss, 0:H * NS], func=AF.Ln, bias=EPS, scale=1.0 / DV), r=ar(), w=aw("lnv_s"), x=[BK(bss)])
                        bput(bss)
                        ACT("activation", dict(out=rstd_s[:].rearrange("p h t -> p (h t)"), in_=lnv_s, func=AF.Exp, scale=-0.5), r=ar("lnv_s"), w=aw("rstd_s"))
                        for vc in range(2):
                            DVE("scalar_tensor_tensor", dict(out=t1_s[:, vc::2, :], in0=ovs[:, vc::2, :], scalar=gnorm[:, l, vc:vc + 1], in1=rstd_s[:],
                                                                                 op0=ALU.mult, op1=ALU.mult),
                                r=["gnorm"] + ar("rstd_s"), w=aw("t1_s"), x=[BK(bos)])
                        bput(bos)
                        DVE("tensor_tensor", dict(out=og[:, :, BT:TW], in0=t1_s[:], in1=sgt[:, :, BT:TW], op=ALU.mult), r=ar("t1_s", "sg_s"), w=["og_s"])
                    if last_g:
                        DMA("sp", "dma_start", dict(out=gp[l].rearrange("h k v -> k h v"), in_=S[:, l, :].rearrange("p (h v) -> p h v", h=H)),
                            r=[("S", l, h_) for h_ in range(H)], w=[("gp", l)])

                    epoch_barrier()
                    DVE("tensor_copy", dict(out=pbuf[:, :, 0:2], in_=carry[:, l, :, :]), r=["carry"] + ar(), w=aw("pb0", "pb1", "pb2", "pb3"))
                    if has_s:
                        DMA("sp", "dma_start", dict(out=sc_tok, in_=scv[l].rearrange("s r d -> (s r) d")), r=[EP], w=aw("sc_tok"))
                        b = bget()
                        for c in range(4):
                            tp(ps[:, b, c * 2 * NS:(c + 1) * 2 * NS], sc_tok[:, c * 128:(c + 1) * 128], ident_f[0:2 * NS, 0:2 * NS], ["ident_f"] + ar("sc_tok"), b)
                        ACT("copy", dict(out=scT[:].rearrange("p c t -> p (c t)"), in_=ps[:, b, 0:4 * 2 * NS]), r=ar(), w=aw("scT"), x=[BK(b)])
                        bput(b)
                        DMA("sp", "dma_start", dict(out=cs[l, :, 0, :], in_=scv[l, :, 1, :]), w=[("cs0", l)])

                    def conv_mm(nm, post, post_s):
                        bs = bget() if has_s else None
                        for wi in range(2):
                            i, slot, wk = wtake((nm, g, l, wi))
                            for sub in range(2):
                                c = wi * 2 + sub
                                b = bget()
                                for kc in range(NKC):
                                    mm(ps[:, b, :], slot[:, kc, sub * 128:(sub + 1) * 128], uT[:, kc, 0:BT], kc == 0, kc == NKC - 1, [wk, ("uT", kc)], b)
                                post(c, b)
                                bput(b)
                                if has_s:
                                    for kc in range(NKC):
                                        mm(ps[:, bs, c * NS:(c + 1) * NS], slot[:, kc, sub * 128:(sub + 1) * 128], uT[:, kc, BT:TW],
                                           kc == 0, kc == NKC - 1, [wk, "uTs"], bs)
                            wrel(i)
                        if has_s:
                            post_s(ps[:, bs, 0:4 * NS].rearrange("p (c t) -> p c t", c=4), bs)
                            bput(bs)

                    conv_mm("cc",
                            lambda c, b: ACT("copy", dict(out=ccy[:, c, 0:BT], in_=ps[:, b, :]), r=ar(), w=aw(f"ccy{c}"), x=[BK(b)]),
                            lambda pv, bs: ACT("copy", dict(out=ccy[:, :, BT:TW], in_=pv), r=ar(), w=aw("ccy_s"), x=[BK(bs)]))

                    def post_ch(c, b):
                        DVE("tensor_tensor", dict(out=pbuf[:, c, 2:BT + 2], in0=ps[:, b, :], in1=ccy[:, c, 0:BT], op=ALU.mult),
                            r=ar(f"ccy{c}"), w=aw(f"pb{c}"), x=[BK(b)])
                        ACT("activation", dict(out=tmp1, in_=pbuf[:, c, 0:BT], func=AF.Identity, scale=convw[:, l, c, 0:1]),
                            r=["convw"] + ar(f"pb{c}"), w=aw("tmp1"))
                        DVE("scalar_tensor_tensor", dict(out=tmp2, in0=pbuf[:, c, 1:BT + 1], scalar=convw[:, l, c, 1:2], in1=tmp1, op0=ALU.mult, op1=ALU.add),
                            r=["convw"] + ar(f"pb{c}", "tmp1"), w=aw("tmp2"))
                        DVE("scalar_tensor_tensor", dict(out=ccy[:, c, 0:BT], in0=pbuf[:, c, 2:BT + 2], scalar=convw[:, l, c, 2:3], in1=tmp2, op0=ALU.mult, op1=ALU.add),
                            r=["convw"] + ar(f"pb{c}", "tmp2"), w=aw(f"ccy{c}"))

                    def post_ch_s(pv, bs):
                        DVE("tensor_tensor", dict(out=p_s[:], in0=pv, in1=ccy[:, :, BT:TW], op=ALU.mult), r=ar("ccy_s"), w=aw("p_s"), x=[BK(bs)])
                        for c in range(4):
                            ACT("activation", dict(out=tcs1, in_=scT[:, c, 0:2 * NS:2], func=AF.Identity, scale=convw[:, l, c, 0:1]),
                                r=["convw"] + ar("scT"), w=aw("tcs1"))
                            DVE("scalar_tensor_tensor", dict(out=tcs2, in0=scT[:, c, 1:2 * NS:2], scalar=convw[:, l, c, 1:2], in1=tcs1, op0=ALU.mult, op1=ALU.add),
                                r=["convw"] + ar("scT", "tcs1"), w=aw("tcs2"))
                            DVE("scalar_tensor_tensor", dict(out=ccy[:, c, BT:TW], in0=p_s[:, c, :], scalar=convw[:, l, c, 2:3], in1=tcs2, op0=ALU.mult, op1=ALU.add),
                                r=["convw"] + ar("p_s", "tcs2"), w=aw("ccy_s"))

                    conv_mm("ch", post_ch, post_ch_s)
                    DVE("tensor_copy", dict(out=carry[:, l, :, :], in_=pbuf[:, :, BT:BT + 2]), r=ar("pb0", "pb1", "pb2", "pb3"), w=["carry"])
                    if last_g:
                        b = bget()
                        for c in range(4):
                            tp(ps[0:2, b, c * 128:(c + 1) * 128], carry[:, l, c, :], ident_f[:], ["ident_f", "carry"], b)
                        ACT("copy", dict(out=cst, in_=ps[0:2, b, :]), r=ar(), w=aw("cst"), x=[BK(b)])
                        bput(b)
                        DMA("sp", "dma_start", dict(out=cp[l], in_=cst), r=ar("cst"), w=[("cp", l)])
                    if has_s:
                        b = bget()
                        for c in range(4):
                            tp(ps[0:NS, b, c * 128:(c + 1) * 128], p_s[:, c, :], ident_f[:], ["ident_f"] + ar("p_s"), b)
                        ACT("copy", dict(out=pst, in_=ps[0:NS, b, :]), r=ar(), w=aw("pst"), x=[BK(b)])
                        bput(b)
                        DMA("sp", "dma_start", dict(out=cs[l, :, 1, :], in_=pst), r=ar("pst"), w=[("cs1", l)])
                    conv_mm("cb",
                            lambda c, b: DVE("tensor_tensor", dict(out=ccy[:, c, 0:BT], in0=ps[:, b, :], in1=ccy[:, c, 0:BT], op=ALU.mult),
                                             r=ar(f"ccy{c}"), w=aw(f"ccy{c}"), x=[BK(b)]),
                            lambda pv, bs: DVE("tensor_tensor", dict(out=ccy[:, :, BT:TW], in0=pv, in1=ccy[:, :, BT:TW], op=ALU.mult),
                                               r=ar("ccy_s"), w=aw("ccy_s"), x=[BK(bs)]))

                    def post_gc(c, b):
                        ACT("activation", dict(out=tmp3, in_=ps[:, b, :], func=AF.Silu), r=ar(), w=aw("tmp3"), x=[BK(b)])
                        DVE("tensor_tensor", dict(out=yc[:, c, 0:BT], in0=ccy[:, c, 0:BT], in1=tmp3, op=ALU.mult), r=ar(f"ccy{c}", "tmp3"), w=[("yc", c)])

                    def post_gc_s(pv, bs):
                        ACT("activation", dict(out=tmp3s[:], in_=pv, func=AF.Silu), r=ar(), w=aw("tmp3s"), x=[BK(bs)])
                        DVE("tensor_tensor", dict(out=yc[:, :, BT:TW], in0=ccy[:, :, BT:TW], in1=tmp3s[:], op=ALU.mult), r=ar("ccy_s", "tmp3s"), w=["yc_s"])

                    conv_mm("gc", post_gc, post_gc_s)

                    ogr = [("og", j) for j in range(GT)]
                    ycr = [("yc", c) for c in range(4)]
                    for cpi in range(4):
                        ia, sa, ka = wtake(("pa", g, l, cpi))
                        ib, sbb, kb = wtake(("pb", g, l, cpi))
                        ig, sgw, kg = wtake(("mg", g, l, cpi))
                        ic, scw, kcw = wtake(("mc", g, l, cpi))
                        for sub in range(2):
                            c = cpi * 2 + sub
                            cs0, cs1 = sub * 128, (sub + 1) * 128
                            bA = bget()
                            for kc in range(NKC):
                                mm(ps[:, bA, :], sa[:, kc, cs0:cs1], og[:, kc, 0:BT], kc == 0, kc == NKC - 1, [ka] + ogr, bA)
                            bB = bget()
                            for kc in range(4):
                                mm(ps[:, bB, :], sbb[:, kc, cs0:cs1], yc[:, kc, 0:BT], kc == 0, kc == 3, [kb, ("yc", kc)], bB)
                            bG = bget()
                            for kc in range(NKC):
                                mm(ps[:, bG, :], sgw[:, kc, cs0:cs1], uT[:, kc, 0:BT], kc == 0, kc == NKC - 1, [kg, ("uT", kc)], bG)
                            bC = bget()
                            for kc in range(NKC):
                                mm(ps[:, bC, :], scw[:, kc, cs0:cs1], uT[:, kc, 0:BT], kc == 0, kc == NKC - 1, [kcw, ("uT", kc)], bC)
                            ACT("activation", dict(out=sgm, in_=ps[:, bG, :], func=AF.Sigmoid), r=ar(), w=aw("sgm"), x=[BK(bG)])
                            bput(bG)
                            ACT("activation", dict(out=scm, in_=ps[:, bC, :], func=AF.Sigmoid), r=ar(), w=aw("scm"), x=[BK(bC)])
                            bput(bC)
                            DVE("tensor_tensor", dict(out=tm1, in0=ps[:, bA, :], in1=sgm, op=ALU.mult), r=ar("sgm"), w=aw("tm1"), x=[BK(bA)])
                            bput(bA)
                            DVE("tensor_tensor", dict(out=tm2, in0=ps[:, bB, :], in1=scm, op=ALU.mult), r=ar("scm"), w=aw("tm2"), x=[BK(bB)])
                            bput(bB)
                            DVE("tensor_tensor", dict(out=mg[:, c, 0:BT], in0=tm1, in1=tm2, op=ALU.add), r=ar("tm1", "tm2"), w=aw(f"mg{c}"))
                            if has_s:
                                bS = bget()
                                for kc in range(NKC):
                                    mm(ps[:, bS, 0:NS], sa[:, kc, cs0:cs1], og[:, kc, BT:TW], kc == 0, kc == NKC - 1, [ka, "og_s"], bS)
                                for kc in range(4):
                                    mm(ps[:, bS, NS:2 * NS], sbb[:, kc, cs0:cs1], yc[:, kc, BT:TW], kc == 0, kc == 3, [kb, "yc_s"], bS)
                                for kc in range(NKC):
                                    mm(ps[:, bS, 2 * NS:3 * NS], sgw[:, kc, cs0:cs1], uT[:, kc, BT:TW], kc == 0, kc == NKC - 1, [kg, "uTs"], bS)
                                for kc in range(NKC):
                                    mm(ps[:, bS, 3 * NS:4 * NS], scw[:, kc, cs0:cs1], uT[:, kc, BT:TW], kc == 0, kc == NKC - 1, [kcw, "uTs"], bS)
                                ACT("activation", dict(out=sg_s, in_=ps[:, bS, 2 * NS:4 * NS], func=AF.Sigmoid), r=ar(), w=aw("sg_s2"), x=[BK(bS)])
                                DVE("tensor_tensor", dict(out=t_s, in0=ps[:, bS, 0:2 * NS], in1=sg_s, op=ALU.mult), r=ar("sg_s2"), w=aw("t_s"), x=[BK(bS)])
                                bput(bS)
                                DVE("tensor_tensor", dict(out=mg[:, c, BT:TW], in0=t_s[:, 0:NS], in1=t_s[:, NS:2 * NS], op=ALU.add), r=ar("t_s"), w=aw("mg_s"))
                        wrel(ia)
                        wrel(ib)
                        wrel(ig)
                        wrel(ic)

                    so = [wtake(("o", g, l, wi)) for wi in range(4)]
                    mgr = ar(*[f"mg{c}" for c in range(NKC)])

                    def o_tail(xv, gv, tov, stv, mvv, lg, lb, bo, npart, xkey, mkeys):
                        if gv is not None:
                            DVE("tensor_tensor", dict(out=tov.rearrange("p (b n) -> p b n", b=2), in0=ps[0:npart, bo:bo + 2, :],
                                                      in1=gv.rearrange("p (b n) -> p b n", b=2), op=ALU.mult),
                                r=["gate_bc", "gate_s"] + ar(), w=aw(mkeys + "to"), x=[BK(bo), BK(bo + 1)])
                            bput(bo, 2)
                            DVE("scalar_tensor_tensor", dict(out=xv, in0=xv, scalar=ALPHA, in1=tov, op0=ALU.mult, op1=ALU.add),
                                r=[xkey] + ar(mkeys + "to"), w=[xkey])
                        else:
                            DVE("scalar_tensor_tensor", dict(out=xv.rearrange("p (b n) -> p b n", b=2), in0=xv.rearrange("p (b n) -> p b n", b=2), scalar=ALPHA,
                                                             in1=ps[0:npart, bo:bo + 2, :], op0=ALU.mult, op1=ALU.add),
                                r=[xkey], w=[xkey], x=[BK(bo), BK(bo + 1)])
                            bput(bo, 2)
                        for hh in range(2):
                            DVE("bn_stats", dict(out=stv[:, hh, :], in_=xv[:, hh * 512:(hh + 1) * 512]), r=[xkey], w=[mkeys + "st"])
                        DVE("bn_aggr", dict(out=mvv[:, 0:2], in_=stv.rearrange("p a b -> p (a b)")), r=[mkeys + "st"], w=[mkeys + "mv"])
                        ACT("activation", dict(out=mvv[:, 2:3], in_=mvv[:, 1:2], func=AF.Ln, bias=EPS), r=[mkeys + "mv"], w=[mkeys + "mv2"])
                        ACT("activation", dict(out=mvv[:, 3:4], in_=mvv[:, 2:3], func=AF.Exp, scale=-0.5), r=[mkeys + "mv2"], w=[mkeys + "mv3"])
                        DVE("tensor_scalar", dict(out=mvv[:, 4:5], in0=mvv[:, 0:1], scalar1=mvv[:, 3:4], scalar2=-1.0, op0=ALU.mult, op1=ALU.mult),
                            r=[mkeys + "mv", mkeys + "mv3"], w=[mkeys + "mv4"])
                        ACT("activation", dict(out=xv, in_=xv, func=AF.Identity, bias=mvv[:, 4:5], scale=mvv[:, 3:4]),
                            r=[xkey, mkeys + "mv3", mkeys + "mv4"], w=[xkey])
                        DVE("tensor_tensor", dict(out=xv, in0=xv, in1=lg, op=ALU.mult), r=[xkey, "lng"], w=[xkey])
                        DVE("tensor_tensor", dict(out=xv, in0=xv, in1=lb, op=ALU.add), r=[xkey, "lnb"], w=[xkey])

                    if has_s:
                        bo = bget2()
                        for wi in range(4):
                            for kc in range(NKC):
                                mm(ps[0:NS, bo + wi // 2, (wi % 2) * WC:(wi % 2 + 1) * WC], mg[:, kc, BT:TW], so[wi][1][:, kc, :],
                                   kc == 0, kc == NKC - 1, [so[wi][2]] + ar("mg_s"), bo + wi // 2)
                        o_tail(xs_sb[:], gate_s[:, l, :], to_s, st_s[:], mv_s[:], lng_bc[0:NS, :], lnb_bc[0:NS, :], bo, NS, "xs", "s")
                    for j in range(GT):
                        bo = bget2()
                        for wi in range(4):
                            for kc in range(NKC):
                                mm(ps[:, bo + wi // 2, (wi % 2) * WC:(wi % 2 + 1) * WC], mg[:, kc, j * 128:(j + 1) * 128], so[wi][1][:, kc, :],
                                   kc == 0, kc == NKC - 1, [so[wi][2]] + mgr, bo + wi // 2)
                        o_tail(x_sb[:, j, :], gate_bc[:, l, :], to, st[:], mv[:], lng_bc[:], lnb_bc[:], bo, 128, ("x", j), "p")
                        if l == nl_run - 1:
                            DMA("sp", "dma_start", dict(out=yp[g * BT + j * 128: g * BT + (j + 1) * 128, :], in_=x_sb[:, j, :]),
                                r=[("x", j)], w=[("yp", g, j)])
                    for wi in range(4):
                        wrel(so[wi][0])
                    if l == nl_run - 1:
                        pass
                        if has_s:
                            DMA("sp", "dma_start", dict(out=ys, in_=xs_sb[:]), r=["xs"], w=["ys"])

            assert wstate["taken"] == len(wlist), (wstate, len(wlist))
            return wlist

        wl = emit_all(Prog(), True, None)
        P = Prog()
        emit_all(P, False, wl)
        P.finalize()

        @block.tensor
        def _(e):
            P.run_engine("pe", e, sems, dma_sems)

        @block.scalar
        def _(e):
            P.run_engine("act", e, sems, dma_sems)

        @block.vector
        def _(e):
            P.run_engine("dve", e, sems, dma_sems)

        @block.gpsimd
        def _(e):
            P.run_engine("pool", e, sems, dma_sems)

        @block.sync
        def _(e):
            P.run_engine("sp", e, sems, dma_sems)
            P.final_wait(e, dma_sems)

    return nc


_NC_CACHE = {}


def kernel(x_prompt, x_sample, c_prompt, c_sample, state_gla, state_conv, w_ada, b_ada, w_in, w_a2, b_a,
           gla_norm_g, conv_w, w_pa, w_pb, w_o, ln_g, ln_b):
    f = lambda a: np.ascontiguousarray(np.asarray(a, dtype=np.float32))
    x_prompt, x_sample, c_prompt, c_sample = f(x_prompt), f(x_sample), f(c_prompt), f(c_sample)
    state_gla, state_conv = f(state_gla), f(state_conv)
    shared = dict(w_ada=f(w_ada), b_ada=f(b_ada), w_in=f(w_in), w_a2=f(w_a2), b_a=f(b_a), gng=f(gla_norm_g),
                  conv_w=f(conv_w), w_pa=f(w_pa), w_pb=f(w_pb), w_o=f(w_o), ln_g=f(ln_g), ln_b=f(ln_b))
    ncores = 8
    in_maps = []
    for i in range(ncores):
        sl = slice(i * NS, (i + 1) * NS)
        m = dict(shared)
        m["xp"] = x_prompt[i]
        m["xs"] = np.ascontiguousarray(x_sample[sl, 0, :])
        m["c17"] = np.ascontiguousarray(np.concatenate([c_prompt[i:i + 1], c_sample[sl]], axis=0))
        m["sgl"] = np.ascontiguousarray(state_gla[:, sl])
        m["scv"] = np.ascontiguousarray(state_conv[:, sl])
        in_maps.append(m)
    if "nc" not in _NC_CACHE:
        _NC_CACHE["nc"] = build_program()
    res = run_bass_kernel_spmd(_NC_CACHE["nc"], in_maps, core_ids=list(range(ncores)))
    R = res.results
    y_prompt = np.stack([R[i]["yp"] for i in range(ncores)], axis=0)
    y_sample = np.concatenate([R[i]["ys"] for i in range(ncores)], axis=0)[:, None, :]
    gla_p = np.stack([R[i]["gp"] for i in range(ncores)], axis=1)
    conv_p = np.stack([R[i]["cp"] for i in range(ncores)], axis=1)
    gla_s = np.concatenate([R[i]["gs"] for i in range(ncores)], axis=1)
    conv_s = np.concatenate([R[i]["cs"] for i in range(ncores)], axis=1)
    return (y_prompt.astype(np.float32), y_sample.astype(np.float32), gla_p.astype(np.float32),
            conv_p.astype(np.float32), gla_s.astype(np.float32), conv_s.astype(np.float32))
```

```python
import os
import numpy as np
from contextlib import ExitStack
import concourse.bass as bass
import concourse.mybir as mybir
from concourse.bass_utils import run_bass_kernel_spmd

F32 = mybir.dt.float32
BF16 = mybir.dt.bfloat16
AF = mybir.ActivationFunctionType
ALU = mybir.AluOpType

ENGS = ("pe", "act", "dve", "pool", "sp")

D = 1024
SEQ = 2048
DEPTH = 2
NS = 16
H = 4
DK = 128
DV = 256
NKC = 8
BT = 512
GT = 4
NG = SEQ // BT
TW = BT + NS
SG = 0
ALPHA = (2 * DEPTH) ** 0.25
EPS = 1e-5
QSCALE = DK ** -0.5
O_Q, O_K, O_V, O_GG, O_ALR, O_CB, O_CC, O_CH, O_GC, O_MG, O_MC = 0, 512, 1024, 2048, 3072, 3088, 3600, 4112, 4624, 5136, 6160
NW = 8
WC = 256


class Ins:
    __slots__ = ("eng", "fn", "deps", "dma", "sig", "semid", "semval", "pos")

    def __init__(self, eng, fn, dma, pos):
        self.eng = eng
        self.fn = fn
        self.deps = []
        self.dma = dma
        self.sig = False
        self.semid = None
        self.semval = None
        self.pos = pos


class Prog:
    def __init__(self, n_dma_sems=8):
        self.lists = {e: [] for e in ENGS}
        self.last_w = {}
        self.rd_eng = {}
        self.rd_dma = {}
        self.n_dma_sems = n_dma_sems

    def add(self, eng, fn, reads=(), writes=(), excl=(), dma=False):
        I = Ins(eng, fn, dma, len(self.lists[eng]))
        cand = []
        writes = list(writes) + list(excl)
        for k in reads:
            w = self.last_w.get(k)
            if w is not None:
                cand.append((w, True))
        for k in writes:
            w = self.last_w.get(k)
            if w is not None:
                cand.append((w, False))
            for r in self.rd_eng.get(k, {}).values():
                cand.append((r, False))
            for r in self.rd_dma.get(k, ()):
                cand.append((r, False))
        for k in writes:
            self.last_w[k] = I
            self.rd_eng[k] = {}
            self.rd_dma[k] = []
        for k in reads:
            if dma:
                self.rd_dma.setdefault(k, []).append(I)
            else:
                self.rd_eng.setdefault(k, {})[eng] = I
        best = {}
        seen = set()
        for d, raw in cand:
            if d is I:
                continue
            if d.dma:
                if id(d) not in seen:
                    seen.add(id(d))
                    I.deps.append(d)
                continue
            if d.eng == eng and not dma:
                if eng == "pe":
                    continue
            b = best.get(d.eng)
            if b is None or d.pos > b.pos:
                best[d.eng] = d
        for d in best.values():
            d.sig = True
            I.deps.append(d)
        self.lists[eng].append(I)
        return I

    def finalize(self):
        for e in ENGS:
            cnt = 0
            dcnt = {}
            k = 0
            for I in self.lists[e]:
                if I.dma:
                    I.semid = k % self.n_dma_sems
                    k += 1
                    dcnt[I.semid] = dcnt.get(I.semid, 0) + 16
                    I.semval = dcnt[I.semid]
                elif I.sig:
                    cnt += 1
                    I.semval = cnt

    def run_engine(self, e, engobj, sems, dma_sems):
        waited = {}
        for I in self.lists[e]:
            need = {}
            for d in I.deps:
                if d.dma:
                    key = ("d", d.eng, d.semid)
                    s = dma_sems[d.eng][d.semid]
                else:
                    key = ("e", d.eng)
                    s = sems[d.eng]
                if need.get(key, (None, 0))[1] < d.semval:
                    need[key] = (s, d.semval)
            if I.dma and I.semval > 16:
                key = ("d", e, I.semid)
                if need.get(key, (None, 0))[1] < I.semval - 16:
                    need[key] = (dma_sems[e][I.semid], I.semval - 16)
            for key, (s, v) in need.items():
                if waited.get(key, 0) < v:
                    engobj.wait_ge(s, v)
                    waited[key] = v
            ins = getattr(engobj, I.fn[0])(**I.fn[1])
            if I.dma:
                ins.then_inc(dma_sems[e][I.semid], 16)
            elif I.sig:
                ins.then_inc(sems[e], 1)

    def final_wait(self, engobj, dma_sems):
        for e in ENGS:
            last = {}
            for I in self.lists[e]:
                if I.dma:
                    last[I.semid] = I.semval
            for sid, v in last.items():
                engobj.wait_ge(dma_sems[e][sid], v)


def build_program(ng_run=NG, nl_run=DEPTH):
    nc = bass.Bass("TRN2", target_bir_lowering=False)

    def din(name, shape):
        return nc.dram_tensor(name, list(shape), F32, kind="ExternalInput").ap()

    def dout(name, shape):
        return nc.dram_tensor(name, list(shape), F32, kind="ExternalOutput").ap()

    xp = din("xp", [SEQ, D])
    xs = din("xs", [NS, D])
    c17d = din("c17", [NS + 1, D])
    sgl = din("sgl", [DEPTH, NS, H, DK, DV])
    scv = din("scv", [DEPTH, NS, 2, 512])
    w_ada = din("w_ada", [DEPTH, D, 3 * D])
    b_ada = din("b_ada", [DEPTH, 3 * D])
    w_in = din("w_in", [DEPTH, D, 7184])
    w_a2 = din("w_a2", [DEPTH, 16, 512])
    b_a = din("b_a", [DEPTH, 512])
    gng = din("gng", [DEPTH, DV])
    conv_w = din("conv_w", [DEPTH, 3, 512])
    w_pa = din("w_pa", [DEPTH, D, D])
    w_pb = din("w_pb", [DEPTH, 512, D])
    w_o = din("w_o", [DEPTH, D, D])
    ln_g = din("ln_g", [DEPTH, D])
    ln_b = din("ln_b", [DEPTH, D])
    yp = dout("yp", [SEQ, D])
    ys = dout("ys", [NS, D])
    gp = dout("gp", [DEPTH, H, DK, DV])
    cp = dout("cp", [DEPTH, 2, 512])
    gs = dout("gs", [DEPTH, NS, H, DK, DV])
    cs = dout("cs", [DEPTH, NS, 2, 512])

    with ExitStack() as es:
        def sb(name, shape, dt):
            return es.enter_context(nc.sbuf_tensor(name, list(shape), dt))

        x_sb = sb("x_sb", [128, GT, D], F32)
        xs_sb = sb("xs_sb", [NS, D], F32)
        uT = sb("uT", [128, NKC, TW], BF16)
        og = sb("og", [128, NKC, TW], BF16)
        yc = sb("yc", [128, 4, TW], BF16)
        wring = sb("wring", [128, NW, NKC, WC], BF16)
        gate_bc = sb("gate_bc", [128, DEPTH, D], F32)
        gate_s = sb("gate_s", [NS, DEPTH, D], F32)
        lng_bc = sb("lng_bc", [128, D], F32)
        lnb_bc = sb("lnb_bc", [128, D], F32)
        S = sb("S", [128, DEPTH, H * DV], F32)
        S_bf = sb("S_bf", [128, DEPTH, H * DV], BF16)
        Sbuf = sb("Sbuf", [128, 3, H * DV], F32)
        sbf = sb("sbf", [128, 2, H * DV], BF16)
        ident_f = sb("ident_f", [128, 128], F32)
        ident_b = sb("ident_b", [128, 128], BF16)
        tri = sb("tri", [128, 128], F32)
        ones_b = sb("ones_b", [128, 128], BF16)
        alr_aug = sb("alr_aug", [17, TW], F32)
        w_a2aug = sb("w_a2aug", [17, DEPTH, 512], F32)
        gnorm = sb("gnorm", [128, DEPTH, 2], F32)
        convw = sb("convw", [128, DEPTH, 4, 3], F32)
        ada_sb = sb("ada_sb", [128, DEPTH, 16, 17], F32)
        b_adaT = sb("b_adaT", [128, DEPTH, 24], F32)
        carry = sb("carry", [128, DEPTH, 4, 2], F32)
        cT = sb("cT", [128, NKC, 17], BF16)
        tmp_s = sb("tmp_s", [128, NKC, NS], F32)
        st = sb("st", [128, 2, 6], F32)
        mv = sb("mv", [128, 8], F32)
        st_s = sb("st_s", [NS, 2, 6], F32)
        mv_s = sb("mv_s", [NS, 8], F32)
        dummy = sb("dummy_t", [128, 8], F32)
        ARENA_F = 16384 + 256 + 1024 + 256
        arena = sb("arena", [128, ARENA_F], F32)

        ps = es.enter_context(nc.psum_tensor("ps", [128, 8, 512], F32))

        sems = {e: es.enter_context(nc.semaphore(f"s_{e}")) for e in ENGS}
        dma_sems = {e: [es.enter_context(nc.semaphore(f"d_{e}{i}")) for i in range(8)]
                    for e in ("sp", "pool")}
        block = es.enter_context(nc.Block())

        class Arena:
            def __init__(self):
                self.off = 0

            def f32(self, n, parts=128):
                a = arena[0:parts, self.off:self.off + n]
                self.off += n
                assert self.off <= ARENA_F, self.off
                return a

            def bf16(self, n, parts=128):
                nf = (n + 1) // 2
                a = arena[0:parts, self.off:self.off + nf].bitcast(BF16)
                self.off += nf
                assert self.off <= ARENA_F, self.off
                return a

        def v3(ap, c):
            return ap.rearrange("p (c t) -> p c t", c=c)

        A = Arena()
        E_q = v3(A.f32(4 * BT), 4)
        E_k = v3(A.f32(4 * BT), 4)
        q_in = v3(A.bf16(4 * BT), 4)
        k_in = v3(A.bf16(4 * BT), 4)
        sgt = v3(A.bf16(8 * TW), 8)
        v_tok = v3(A.bf16(GT * 1024), GT)
        k_tok = v3(A.bf16(2 * 512), 2)
        _o = A.off
        e1 = A.f32(512)
        A.off = _o
        ktok_s = A.f32(512, NS)
        sp = v3(A.f32(2 * 512), 2)
        v_s = A.bf16(1024, NS)
        attm = v3(A.bf16(2 * 512), 2)
        sq = v3(A.bf16(8 * 128), 8)
        lnv = A.f32(512)
        rstd = v3(A.f32(512), 4)
        t1 = v3(A.f32(1024), 8)
        Pst = A.f32(1024)
        q_s = v3(A.bf16(4 * NS), 4)
        a_s = v3(A.f32(4 * NS), 4)
        km2 = v3(A.bf16(2 * 512, NS), 2)
        sq_s = v3(A.bf16(8 * NS), 8)
        lnv_s = A.f32(4 * NS)
        rstd_s = v3(A.f32(4 * NS), 4)
        t1_s = v3(A.f32(8 * NS), 8)
        g_end = A.off
        A = Arena()
        c17 = A.f32(D, NS + 1)
        cTb = v3(A.bf16(NKC * 128), NKC)
        bgate = A.f32(D)
        A = Arena()
        ccy = v3(A.f32(4 * TW), 4)
        pbuf = v3(A.f32(4 * (BT + 2)), 4)
        tmp1 = A.f32(512)
        tmp2 = A.f32(512)
        tmp3 = A.f32(512)
        p_s = v3(A.f32(4 * NS), 4)
        scT = v3(A.f32(4 * 2 * NS), 4)
        sc_tok = A.f32(512, 2 * NS)
        tcs1 = A.f32(NS)
        tcs2 = A.f32(NS)
        tmp3s = v3(A.f32(4 * NS), 4)
        cst = A.f32(512, 2)
        pst = A.f32(512, NS)
        sgm = A.f32(512)
        scm = A.f32(512)
        tm1 = A.f32(512)
        tm2 = A.f32(512)
        sg_s = A.f32(2 * NS)
        t_s = A.f32(2 * NS)
        mg = v3(A.bf16(NKC * TW), NKC)
        to = A.f32(D)
        to_s = A.f32(D, NS)
        cmo_end = A.off

        def emit_all(P, record, wlist_in):
            def PE(meth, kw, r=(), x=()):
                return P.add("pe", (meth, kw), reads=r, excl=x)

            def ACT(meth, kw, r=(), w=(), x=()):
                return P.add("act", (meth, kw), reads=r, writes=w, excl=x)

            def DVE(meth, kw, r=(), w=(), x=()):
                return P.add("dve", (meth, kw), reads=r, writes=w, excl=x)

            def POOL(meth, kw, r=(), w=()):
                return P.add("pool", (meth, kw), reads=r, writes=w)

            def DMA(q, meth, kw, r=(), w=()):
                return P.add(q, (meth, kw), reads=r, writes=w, dma=True)

            EP = "EPOCH"

            def ar(*keys):
                return [EP] + ["A:" + k for k in keys]

            def aw(*keys):
                return ["A:" + k for k in keys]

            def epoch_barrier():
                DVE("memset", dict(ap=dummy[:, 0:1], constant=0.0), w=[EP])

            free_banks = list(range(8))

            def bget():
                assert free_banks, "PSUM banks exhausted"
                return free_banks.pop(0)

            def bget2():
                for i in range(len(free_banks)):
                    b = free_banks[i]
                    if b % 2 == 0 and (b + 1) in free_banks:
                        free_banks.remove(b)
                        free_banks.remove(b + 1)
                        return b
                raise AssertionError("no PSUM bank pair free")

            def bput(b, n=1):
                for i in range(n):
                    free_banks.append(b + i)

            def BK(b):
                return ("B", b)

            def mm(out, lhsT, rhs, start, stop, r, b):
                PE("matmul", dict(out=out, lhsT=lhsT, rhs=rhs, start=start, stop=stop), r=r, x=[BK(b)])

            def tp(out, in_, ident, r, b):
                PE("transpose", dict(out=out, in_=in_, identity=ident), r=r, x=[BK(b)])

            wlist = [] if record else wlist_in
            OFFS = {"q": O_Q, "k": O_K, "v": O_V, "gg": O_GG, "cc": O_CC, "ch": O_CH, "cb": O_CB, "gc": O_GC, "mg": O_MG, "mc": O_MC}

            def wspec(tag):
                nm = tag[0]
                if nm == "ada":
                    return w_ada[tag[1], :, tag[2] * WC:(tag[2] + 1) * WC], NKC, WC
                if nm == "alr":
                    return w_in[tag[2], :, O_ALR:O_ALR + 16], NKC, 16
                l, i = tag[2], tag[3]
                if nm in OFFS:
                    return w_in[l, :, OFFS[nm] + i * WC: OFFS[nm] + (i + 1) * WC], NKC, WC
                if nm == "pa":
                    return w_pa[l, :, i * WC:(i + 1) * WC], NKC, WC
                if nm == "pb":
                    return w_pb[l, :, i * WC:(i + 1) * WC], 4, WC
                if nm == "o":
                    return w_o[l, :, i * WC:(i + 1) * WC], NKC, WC
                raise KeyError(tag)

            wstate = {"issued": 0, "taken": 0}
            released = [False] * (100000 if record else len(wlist))

            def wpump():
                while wstate["issued"] < len(wlist):
                    j = wstate["issued"]
                    if j >= NW and not released[j - NW]:
                        break
                    src, nkc, ncols = wspec(wlist[j])
                    slot = j % NW
                    dst = wring[:, slot, 0:nkc, 0:ncols]
                    srcv = src.rearrange("(kc p) c -> p kc c", p=128)
                    DMA("pool", "dma_start", dict(out=dst, in_=srcv), w=[("W", slot)])
                    wstate["issued"] += 1

            def wtake(tag):
                i = wstate["taken"]
                if record:
                    wlist.append(tag)
                    wstate["taken"] += 1
                    return i, wring[:, i % NW], ("W", i % NW)
                assert wlist[i] == tag, (wlist[i], tag)
                wpump()
                assert wstate["issued"] > i, "weight ring deadlock"
                wstate["taken"] += 1
                return i, wring[:, i % NW], ("W", i % NW)

            def wrel(i):
                released[i] = True
                if not record:
                    wpump()

            POOL("memset", dict(ap=ident_f[:], constant=0.0), w=["ident_f"])
            POOL("affine_select", dict(out=ident_f[:], in_=ident_f[:], pattern=[[-1, 128]], compare_op=ALU.not_equal,
                                           fill=1.0, base=0, channel_multiplier=1), r=["ident_f"], w=["ident_f"])
            POOL("tensor_copy", dict(out=ident_b[:], in_=ident_f[:]), r=["ident_f"], w=["ident_b"])
            POOL("memset", dict(ap=tri[:], constant=1.0), w=["tri"])
            POOL("affine_select", dict(out=tri[:], in_=tri[:], pattern=[[1, 128]], compare_op=ALU.is_ge,
                                           fill=0.0, base=0, channel_multiplier=-1), r=["tri"], w=["tri"])
            POOL("memset", dict(ap=ones_b[:], constant=1.0), w=["ones_b"])
            POOL("memset", dict(ap=alr_aug[:], constant=1.0), w=["alr_aug"])
            POOL("memset", dict(ap=carry[:], constant=0.0), w=["carry"])
            POOL("memset", dict(ap=S[:], constant=0.0), w=[("S", l_, h_) for l_ in range(DEPTH) for h_ in range(H)])
            POOL("memset", dict(ap=S_bf[:], constant=0.0), w=[("S_bf", l_, h_) for l_ in range(DEPTH) for h_ in range(H)])

            DMA("sp", "dma_start", dict(out=c17, in_=c17d), w=aw("c17"))
            DMA("sp", "dma_start", dict(out=xs_sb[:], in_=xs), w=["xs"])
            for l in range(DEPTH):
                DMA("sp", "dma_start", dict(out=w_a2aug[0:16, l, :], in_=w_a2[l]), w=["w_a2aug"])
                DMA("sp", "dma_start", dict(out=w_a2aug[16:17, l, :], in_=b_a[l:l + 1, :]), w=["w_a2aug"])
                DMA("sp", "dma_start", dict(out=gnorm[:, l, :], in_=gng[l].rearrange("(c p) -> p c", p=128),
                                                     allow_slow_non_contiguous=True), w=["gnorm"])
                for ti in range(3):
                    DMA("sp", "dma_start", dict(out=convw[:, l, :, ti], in_=conv_w[l, ti].rearrange("(c p) -> p c", p=128),
                                                allow_slow_non_contiguous=True), w=["convw"])
                DMA("sp", "dma_start", dict(out=b_adaT[:, l, :], in_=b_ada[l].rearrange("(f p) -> p f", p=128),
                                                     allow_slow_non_contiguous=True), w=["b_adaT"])

            b = bget()
            for c in range(NKC):
                tp(ps[:, b, c * 17:(c + 1) * 17], c17[:, c * 128:(c + 1) * 128], ident_f[0:17, 0:17], ["ident_f"] + ar("c17"), b)
            ACT("copy", dict(out=cT[:].rearrange("p c t -> p (c t)"), in_=ps[:, b, 0:NKC * 17]), w=["cT"], x=[BK(b)])
            bput(b)
            for c in range(NKC):
                DVE("tensor_scalar", dict(out=cTb[:, c, :], in0=ones_b[:], scalar1=cT[:, c, 0:1], scalar2=None, op0=ALU.mult),
                    r=["ones_b", "cT"] + ar(), w=aw("cTb"))
            for l in range(DEPTH):
                DVE("tensor_scalar", dict(out=b_adaT[:, l, 8:16], in0=b_adaT[:, l, 8:16], scalar1=1.0, scalar2=None, op0=ALU.add),
                    r=["b_adaT"], w=["b_adaT"])
                DMA("sp", "dma_start", dict(out=bgate, in_=b_ada[l, 2 * D:3 * D].partition_broadcast(128)), w=aw("bgate"))
                b = bget()
                for wi in range(8):
                    i, slot, wk = wtake(("ada", l, wi))
                    for sub in range(2):
                        fc = wi * 2 + sub
                        for kc in range(NKC):
                            mm(ps[:, b, fc * 17:(fc + 1) * 17], slot[:, kc, sub * 128:(sub + 1) * 128], cT[:, kc, :],
                               kc == 0, kc == NKC - 1, [wk, "cT"], b)
                    wrel(i)
                DVE("tensor_tensor", dict(out=ada_sb[:, l, :, :], in0=ps[:, b, 0:16 * 17].rearrange("p (f t) -> p f t", f=16),
                                                       in1=b_adaT[:, l, 0:16].unsqueeze(2).to_broadcast([128, 16, 17]), op=ALU.add),
                    r=["b_adaT"], w=["ada_sb"], x=[BK(b)])
                bput(b)
                for half in range(2):
                    bp = bget()
                    bs = bget()
                    for wj in range(2):
                        wi = 8 + half * 2 + wj
                        i, slot, wk = wtake(("ada", l, wi))
                        for kc in range(NKC):
                            mm(ps[:, bp, wj * WC:(wj + 1) * WC], cTb[:, kc, :], slot[:, kc, :], kc == 0, kc == NKC - 1, [wk] + ar("cTb"), bp)
                        for kc in range(NKC):
                            mm(ps[0:NS, bs, wj * WC:(wj + 1) * WC], cT[:, kc, 1:17], slot[:, kc, :], kc == 0, kc == NKC - 1, [wk, "cT"], bs)
                        wrel(i)
                    DVE("tensor_tensor", dict(out=gate_bc[:, l, half * 512:(half + 1) * 512], in0=ps[:, bp, :],
                                                                        in1=bgate[:, half * 512:(half + 1) * 512], op=ALU.add),
                        r=ar("bgate"), w=["gate_bc"], x=[BK(bp)])
                    DVE("tensor_tensor", dict(out=gate_s[:, l, half * 512:(half + 1) * 512], in0=ps[0:NS, bs, :],
                                                                        in1=bgate[0:NS, half * 512:(half + 1) * 512], op=ALU.add),
                        r=ar("bgate"), w=["gate_s"], x=[BK(bs)])
                    bput(bp)
                    bput(bs)

            for g in range(ng_run):
                has_s = (g == SG)
                for l in range(nl_run):
                    last_g = (g == ng_run - 1)
                    if l == 0:
                        for j in range(GT):
                            DMA("sp", "dma_start", dict(out=x_sb[:, j, :], in_=xp[g * BT + j * 128: g * BT + (j + 1) * 128, :]), w=[("x", j)])
                    DMA("sp", "dma_start", dict(out=lng_bc[:], in_=ln_g[l].partition_broadcast(128)), w=["lng"])
                    DMA("sp", "dma_start", dict(out=lnb_bc[:], in_=ln_b[l].partition_broadcast(128)), w=["lnb"])

                    for c in range(NKC):
                        b = bget()
                        for j in range(GT):
                            tp(ps[:, b, j * 128:(j + 1) * 128], x_sb[:, j, c * 128:(c + 1) * 128], ident_f[:], ["ident_f", ("x", j)], b)
                        ACT("activation", dict(out=uT[:, c, 0:BT], in_=ps[:, b, :], func=AF.Identity,
                                                                  bias=ada_sb[:, l, c, 0:1], scale=ada_sb[:, l, 8 + c, 0:1]),
                            r=["ada_sb"], w=[("uT", c)], x=[BK(b)])
                        bput(b)
                    if has_s:
                        b = bget()
                        for c in range(NKC):
                            tp(ps[:, b, c * NS:(c + 1) * NS], xs_sb[:, c * 128:(c + 1) * 128], ident_f[0:NS, 0:NS], ["ident_f", "xs"], b)
                        DVE("tensor_tensor", dict(out=tmp_s[:], in0=ps[:, b, 0:NKC * NS].rearrange("p (c t) -> p c t", c=NKC),
                                                               in1=ada_sb[:, l, 8:16, 1:17], op=ALU.mult),
                            r=["ada_sb"], w=["tmp_s"], x=[BK(b)])
                        bput(b)
                        DVE("tensor_tensor", dict(out=uT[:, :, BT:TW], in0=tmp_s[:], in1=ada_sb[:, l, 0:8, 1:17], op=ALU.add),
                            r=["ada_sb", "tmp_s"], w=["uTs"])

                    uTr = [("uT", c) for c in range(NKC)]

                    epoch_barrier()
                    i, slot, wk = wtake(("alr", g, l))
                    b = bget()
                    for kc in range(NKC):
                        mm(ps[0:16, b, :], slot[:, kc, 0:16], uT[:, kc, 0:BT], kc == 0, kc == NKC - 1, [wk, ("uT", kc)], b)
                    ACT("copy", dict(out=alr_aug[0:16, 0:BT], in_=ps[0:16, b, :]), w=["alr_aug"], x=[BK(b)])
                    bput(b)
                    if has_s:
                        b = bget()
                        for kc in range(NKC):
                            mm(ps[0:16, b, 0:NS], slot[:, kc, 0:16], uT[:, kc, BT:TW], kc == 0, kc == NKC - 1, [wk, "uTs"], b)
                        ACT("copy", dict(out=alr_aug[0:16, BT:TW], in_=ps[0:16, b, 0:NS]), w=["alr_aug_s"], x=[BK(b)])
                        bput(b)
                    wrel(i)
                    bvs_h = [None]
                    bgs_h = [None]

                    def do_v(h):
                        i, slot, wk = wtake(("v", g, l, h))
                        for jp in range(2):
                            b = bget()
                            for jj in range(2):
                                j = jp * 2 + jj
                                for kc in range(NKC):
                                    mm(ps[:, b, jj * WC:(jj + 1) * WC], uT[:, kc, j * 128:(j + 1) * 128], slot[:, kc, :], kc == 0, kc == NKC - 1,
                                       [wk, ("uT", kc)], b)
                            DVE("tensor_copy", dict(out=v_tok[:, jp * 2:jp * 2 + 2, h * DV:(h + 1) * DV],
                                                    in_=ps[:, b, :].rearrange("p (j v) -> p j v", j=2)),
                                r=ar(), w=aw(f"v_tok{jp * 2}", f"v_tok{jp * 2 + 1}"), x=[BK(b)])
                            bput(b)
                        if has_s:
                            if h % 2 == 0:
                                bvs_h[0] = bget()
                            bvs = bvs_h[0]
                            for kc in range(NKC):
                                mm(ps[0:NS, bvs, (h % 2) * WC:(h % 2 + 1) * WC], uT[:, kc, BT:TW], slot[:, kc, :], kc == 0, kc == NKC - 1, [wk, "uTs"], bvs)
                            if h % 2 == 1:
                                ACT("copy", dict(out=v_s[:, (h - 1) * DV:(h + 1) * DV], in_=ps[0:NS, bvs, :]),
                                    r=ar(), w=aw("v_s"), x=[BK(bvs)])
                                bput(bvs)
                        wrel(i)

                    def do_gg(wi):
                        if has_s and wi == 0:
                            bgs_h[0] = bget()
                        bgs = bgs_h[0]
                        i, slot, wk = wtake(("gg", g, l, wi))
                        for sub in range(2):
                            fc = wi * 2 + sub
                            b = bget()
                            for kc in range(NKC):
                                mm(ps[:, b, :], slot[:, kc, sub * 128:(sub + 1) * 128], uT[:, kc, 0:BT], kc == 0, kc == NKC - 1, [wk, ("uT", kc)], b)
                            ACT("activation", dict(out=sgt[:, fc, 0:BT], in_=ps[:, b, :], func=AF.Silu), r=ar(), w=aw(f"sg{fc}"), x=[BK(b)])
                            bput(b)
                            if has_s:
                                for kc in range(NKC):
                                    mm(ps[:, bgs, fc * NS:(fc + 1) * NS], slot[:, kc, sub * 128:(sub + 1) * 128], uT[:, kc, BT:TW],
                                       kc == 0, kc == NKC - 1, [wk, "uTs"], bgs)
                        wrel(i)
                        if has_s and wi == 3:
                            ACT("activation", dict(out=sgt[:, :, BT:TW], in_=ps[:, bgs, 0:NKC * NS].rearrange("p (c t) -> p c t", c=NKC), func=AF.Silu),
                                r=ar(), w=aw("sg_s"), x=[BK(bgs)])
                            bput(bgs)

                    bulk = [(do_v, h) for h in range(H)] + [(do_gg, wi) for wi in range(4)]
                    EQK = aw(*[f"E_q{h}" for h in range(H)])
                    EKK = aw(*[f"E_k{h}" for h in range(H)])
                    for j in range(GT):
                        jj = j % 2
                        b = bget()
                        mm(ps[:, b, :], alr_aug[0:17, j * 128:(j + 1) * 128], w_a2aug[0:17, l, :], True, True, ["alr_aug", "w_a2aug"], b)
                        ACT("activation", dict(out=e1, in_=ps[:, b, :], func=AF.Exp, scale=-1.0), r=ar(), w=aw("e1"), x=[BK(b)])
                        bput(b)
                        ACT("activation", dict(out=sp[:, jj, :], in_=e1, func=AF.Ln, bias=1.0), r=ar("e1"), w=aw("sp"))
                        bc = bget()
                        for h in range(H):
                            mm(ps[:, bc, h * 128:(h + 1) * 128], sp[:, jj, h * 128:(h + 1) * 128], tri[:], True, True, ["tri"] + ar("sp"), bc)
                        csv = ps[:, bc, :].rearrange("p (h t) -> p h t", h=H)
                        ACT("activation", dict(out=E_q[:, :, j * 128:(j + 1) * 128], in_=csv, func=AF.Exp, scale=-1.0 / 16), r=ar(), w=EQK, x=[BK(bc)])
                        ACT("activation", dict(out=E_k[:, :, j * 128:(j + 1) * 128], in_=csv, func=AF.Exp, scale=1.0 / 16), r=ar(), w=EKK, x=[BK(bc)])
                        bput(bc)
                        f_, a_ = bulk.pop(0)
                        f_(a_)
                    if has_s:
                        b = bget()
                        mm(ps[0:NS, b, :], alr_aug[0:17, BT:TW], w_a2aug[0:17, l, :], True, True, ["alr_aug_s", "w_a2aug"], b)
                        ACT("activation", dict(out=e1[0:NS, :], in_=ps[0:NS, b, :], func=AF.Exp, scale=-1.0), r=ar(), w=aw("e1"), x=[BK(b)])
                        bput(b)
                        ACT("activation", dict(out=sp[0:NS, 0, :], in_=e1[0:NS, :], func=AF.Ln, bias=1.0), r=ar("e1"), w=aw("sp"))
                        b = bget()
                        for h in range(H):
                            tp(ps[:, b, h * NS:(h + 1) * NS], sp[0:NS, 0, h * 128:(h + 1) * 128], ident_f[0:NS, 0:NS], ["ident_f"] + ar("sp"), b)
                        ACT("activation", dict(out=a_s[:].rearrange("p h s -> p (h s)"), in_=ps[:, b, 0:H * NS], func=AF.Exp, scale=-1.0 / 16),
                            r=ar(), w=aw("a_s"), x=[BK(b)])
                        bput(b)
                    while bulk:
                        f_, a_ = bulk.pop(0)
                        f_(a_)
                    for nm, Eb, dst in (("q", E_q, q_in), ("k", E_k, k_in)):
                        bs = bget() if (has_s and nm == "q") else None
                        bkt = bget() if (has_s and nm == "k") else None
                        for wi in range(2):
                            i, slot, wk = wtake((nm, g, l, wi))
                            for sub in range(2):
                                h = wi * 2 + sub
                                b = bget()
                                for kc in range(NKC):
                                    mm(ps[:, b, :], slot[:, kc, sub * 128:(sub + 1) * 128], uT[:, kc, 0:BT], kc == 0, kc == NKC - 1, [wk, ("uT", kc)], b)
                                if nm == "q":
                                    DVE("scalar_tensor_tensor", dict(out=q_in[:, h, :], in0=ps[:, b, :], scalar=QSCALE, in1=E_q[:, h, :],
                                                                                   op0=ALU.mult, op1=ALU.mult),
                                        r=ar(f"E_q{h}"), w=aw(f"q_in{h}"), x=[BK(b)])
                                else:
                                    DVE("tensor_tensor", dict(out=k_in[:, h, :], in0=ps[:, b, :], in1=E_k[:, h, :], op=ALU.mult),
                                        r=ar(f"E_k{h}"), w=aw(f"k_in{h}"), x=[BK(b)])
                                bput(b)
                                if has_s and nm == "q":
                                    for kc in range(NKC):
                                        mm(ps[:, bs, h * NS:(h + 1) * NS], slot[:, kc, sub * 128:(sub + 1) * 128], uT[:, kc, BT:TW],
                                           kc == 0, kc == NKC - 1, [wk, "uTs"], bs)
                            if has_s and nm == "k":
                                for kc in range(NKC):
                                    mm(ps[0:NS, bkt, wi * WC:(wi + 1) * WC], uT[:, kc, BT:TW], slot[:, kc, :], kc == 0, kc == NKC - 1, [wk, "uTs"], bkt)
                            wrel(i)
                        if has_s and nm == "q":
                            ACT("activation", dict(out=q_s[:].rearrange("p h s -> p (h s)"), in_=ps[:, bs, 0:H * NS], func=AF.Identity, scale=QSCALE),
                                r=ar(), w=aw("q_s"), x=[BK(bs)])
                        if has_s and nm == "k":
                            ACT("copy", dict(out=ktok_s, in_=ps[0:NS, bkt, :]), r=ar(), w=aw("e1"), x=[BK(bkt)])
                            bput(bkt)
                        if has_s and nm == "q":
                            bput(bs)
                    bos = bget() if has_s else None

                    def sample_load(s):
                        DMA("sp", "dma_start", dict(out=Sbuf[:, s % 3, :].rearrange("p (h v) -> p h v", h=H), in_=sgl[l, s].rearrange("h k v -> k h v")),
                            w=[("Sbuf", s % 3)])

                    if has_s:
                        for s_ in range(3):
                            sample_load(s_)

                    def stage_A(j):
                        jj = j % 2
                        jc0, jc1 = j * 128, (j + 1) * 128
                        b = bget()
                        psb = ps[:, b, :].bitcast(BF16)
                        for h in range(H):
                            tp(psb[:, h * 128:(h + 1) * 128], k_in[:, h, jc0:jc1], ident_b[:], ["ident_b"] + ar(f"k_in{h}"), b)
                        ACT("copy", dict(out=k_tok[:, jj, :], in_=psb[:, 0:512]), r=ar(), w=aw(f"k_tok{jj}"), x=[BK(b)])
                        bput(b)
                        b = bget()
                        for h in range(H):
                            mm(ps[:, b, h * 128:(h + 1) * 128], k_in[:, h, jc0:jc1], q_in[:, h, jc0:jc1], True, True, ar(f"k_in{h}", f"q_in{h}"), b)
                        DVE("tensor_tensor", dict(out=attm[:, jj, :].rearrange("p (h t) -> p h t", h=H), in0=ps[:, b, :].rearrange("p (h t) -> p h t", h=H),
                                                  in1=tri[:].unsqueeze(1).to_broadcast([128, H, 128]), op=ALU.mult),
                            r=["tri"] + ar(), w=aw(f"attm{jj}"), x=[BK(b)])
                        bput(b)

                    def samp_A(s):
                        kmv = km2[:, s % 2, :]
                        DVE("tensor_scalar", dict(out=kmv, in0=ktok_s, scalar1=ident_f[0:NS, s:s + 1], scalar2=None, op0=ALU.mult),
                            r=["ident_f"] + ar("e1"), w=aw(f"km{s % 2}"))
                        b2 = bget2()
                        for h in range(H):
                            mm(ps[:, b2 + h // 2, (h % 2) * DV:(h % 2 + 1) * DV], kmv[:, h * 128:(h + 1) * 128], v_s[:, h * DV:(h + 1) * DV], True, True,
                               ar(f"km{s % 2}", "v_s"), b2 + h // 2)
                        return b2

                    def samp_B(s, b2):
                        buf = Sbuf[:, s % 3, :]
                        bk = ("Sbuf", s % 3)
                        sbv = sbf[:, s % 2, :]
                        sk = ("sbf", s % 2)
                        for h in range(H):
                            ACT("activation", dict(out=buf[:, h * DV:(h + 1) * DV], in_=buf[:, h * DV:(h + 1) * DV], func=AF.Identity,
                                                   scale=a_s[:, h, s:s + 1]),
                                r=[bk] + ar("a_s"), w=[bk])
                        for half in range(2):
                            DVE("tensor_tensor", dict(out=buf[:, half * 512:(half + 1) * 512], in0=ps[:, b2 + half, :],
                                                      in1=buf[:, half * 512:(half + 1) * 512], op=ALU.add),
                                r=[bk], w=[bk], x=[BK(b2 + half)])
                        bput(b2, 2)
                        DMA("sp", "dma_start", dict(out=gs[l, s].rearrange("h k v -> k h v"), in_=buf.rearrange("p (h v) -> p h v", h=H)),
                            r=[bk], w=[("gs", l, s)])
                        DVE("tensor_copy", dict(out=sbv, in_=buf), r=[bk], w=[sk])
                        if s + 3 < NS:
                            sample_load(s + 3)
                        for h in range(H):
                            for vc in range(2):
                                idx = h * 2 + vc
                                mm(ps[:, bos, idx * NS + s: idx * NS + s + 1], sbv[:, idx * 128:(idx + 1) * 128], q_s[:, h, s:s + 1], True, True,
                                   [sk] + ar("q_s"), bos)

                    stage_A(0)
                    for j in range(GT):
                        jj = j % 2
                        jc0, jc1 = j * 128, (j + 1) * 128
                        bo = bget2()
                        for h in range(H):
                            for vc in range(2):
                                idx = h * 2 + vc
                                out = ps[:, bo + idx // 4, (idx % 4) * 128:(idx % 4 + 1) * 128]
                                mm(out, v_tok[:, j, idx * 128:(idx + 1) * 128], attm[:, jj, h * 128:(h + 1) * 128], True, False,
                                   ar(f"v_tok{j}", f"attm{jj}"), bo + idx // 4)
                                mm(out, S_bf[:, l, idx * 128:(idx + 1) * 128], q_in[:, h, jc0:jc1], False, True, [("S_bf", l, h)] + ar(f"q_in{h}"), bo + idx // 4)
                        b2 = bget2()
                        for h in range(H):
                            mm(ps[:, b2 + h // 2, (h % 2) * DV:(h % 2 + 1) * DV], k_tok[:, jj, h * 128:(h + 1) * 128], v_tok[:, j, h * DV:(h + 1) * DV], True, True,
                               ar(f"k_tok{jj}", f"v_tok{j}"), b2 + h // 2)
                        for half in range(2):
                            DVE("tensor_tensor", dict(out=Pst[:, half * 512:(half + 1) * 512], in0=ps[:, b2 + half, :],
                                                      in1=S[:, l, half * 512:(half + 1) * 512], op=ALU.add),
                                r=[("S", l, 2 * half), ("S", l, 2 * half + 1)] + ar(), w=aw(f"P{half}"), x=[BK(b2 + half)])
                        bput(b2, 2)
                        for h in range(H):
                            dcol = E_q[:, h, jc1 - 1:jc1]
                            ACT("activation", dict(out=S_bf[:, l, h * DV:(h + 1) * DV], in_=Pst[:, h * DV:(h + 1) * DV], func=AF.Identity, scale=dcol),
                                r=ar(f"E_q{h}", f"P{h // 2}"), w=[("S_bf", l, h)])
                        for h in range(H):
                            dcol = E_q[:, h, jc1 - 1:jc1]
                            DVE("tensor_scalar", dict(out=S[:, l, h * DV:(h + 1) * DV], in0=Pst[:, h * DV:(h + 1) * DV], scalar1=dcol, scalar2=None,
                                                      op0=ALU.mult),
                                r=ar(f"E_q{h}", f"P{h // 2}"), w=[("S", l, h)])
                        if j + 1 < GT:
                            stage_A(j + 1)
                        ov = ps[:, bo:bo + 2, :].rearrange("p b (c t) -> p (b c) t", c=4)
                        ACT("activation", dict(out=sq[:], in_=ov, func=AF.Square), r=ar(), w=aw("sq"), x=[BK(bo), BK(bo + 1)])
                        bss = bget()
                        for h in range(H):
                            for vc in range(2):
                                mm(ps[:, bss, h * 128:(h + 1) * 128], ones_b[:], sq[:, h * 2 + vc, :], vc == 0, vc == 1, ["ones_b"] + ar("sq"), bss)
                        ACT("activation", dict(out=lnv, in_=ps[:, bss, :], func=AF.Ln, bias=EPS, scale=1.0 / DV), r=ar(), w=aw("lnv"), x=[BK(bss)])
                        bput(bss)
                        ACT("activation", dict(out=rstd[:].rearrange("p h t -> p (h t)"), in_=lnv, func=AF.Exp, scale=-0.5), r=ar("lnv"), w=aw("rstd"))
                        for vc in range(2):
                            DVE("scalar_tensor_tensor", dict(out=t1[:, vc::2, :], in0=ov[:, vc::2, :], scalar=gnorm[:, l, vc:vc + 1], in1=rstd[:],
                                                             op0=ALU.mult, op1=ALU.mult),
                                r=["gnorm"] + ar("rstd"), w=aw("t1"), x=[BK(bo), BK(bo + 1)])
                        bput(bo, 2)
                        DVE("tensor_tensor", dict(out=og[:, :, jc0:jc1], in0=t1[:], in1=sgt[:, :, jc0:jc1], op=ALU.mult),
                            r=ar("t1", *[f"sg{fc}" for fc in range(8)]), w=[("og", j)])
                        if has_s:
                            s0 = j * 4
                            b2n = samp_A(s0)
                            for s in range(s0, s0 + 4):
                                b2c = b2n
                                if s + 1 < s0 + 4:
                                    b2n = samp_A(s + 1)
                                samp_B(s, b2c)
                    if has_s:
                        ovs = ps[:, bos, 0:NKC * NS].rearrange("p (c t) -> p c t", c=NKC)
                        ACT("activation", dict(out=sq_s[:], in_=ovs, func=AF.Square), r=ar(), w=aw("sq_s"), x=[BK(bos)])
                        bss = bget()
                        for h in range(H):
                            for vc in range(2):
                                mm(ps[:, bss, h * NS:(h + 1) * NS], ones_b[:], sq_s[:, h * 2 + vc, :], vc == 0, vc == 1, ["ones_b"] + ar("sq_s"), bss)
                        ACT("activation", dict(out=lnv_s, in_=ps[:, bss, 0:H * NS], func=AF.Ln, bias=EPS, scale=1.0 / DV), r=ar(), w=aw("lnv_s"), x=[BK(bss)])
                        bput(bss)
                        ACT("activation", dict(out=rstd_s[:].rearrange("p h t -> p (h t)"), in_=lnv_s, func=AF.Exp, scale=-0.5), r=ar("lnv_s"), w=aw("rstd_s"))
                        for vc in range(2):
                            DVE("scalar_tensor_tensor", dict(out=t1_s[:, vc::2, :], in0=ovs[:, vc::2, :], scalar=gnorm[:, l, vc:vc + 1], in1=rstd_s[:],
                                                                                 op0=ALU.mult, op1=ALU.mult),
                                r=["gnorm"] + ar("rstd_s"), w=aw("t1_s"), x=[BK(bos)])
                        bput(bos)
                        DVE("tensor_tensor", dict(out=og[:, :, BT:TW], in0=t1_s[:], in1=sgt[:, :, BT:TW], op=ALU.mult), r=ar("t1_s", "sg_s"), w=["og_s"])
                    if last_g:
                        DMA("sp", "dma_start", dict(out=gp[l].rearrange("h k v -> k h v"), in_=S[:, l, :].rearrange("p (h v) -> p h v", h=H)),
                            r=[("S", l, h_) for h_ in range(H)], w=[("gp", l)])

                    epoch_barrier()
                    DVE("tensor_copy", dict(out=pbuf[:, :, 0:2], in_=carry[:, l, :, :]), r=["carry"] + ar(), w=aw("pb0", "pb1", "pb2", "pb3"))
                    if has_s:
                        DMA("sp", "dma_start", dict(out=sc_tok, in_=scv[l].rearrange("s r d -> (s r) d")), r=[EP], w=aw("sc_tok"))
                        b = bget()
                        for c in range(4):
                            tp(ps[:, b, c * 2 * NS:(c + 1) * 2 * NS], sc_tok[:, c * 128:(c + 1) * 128], ident_f[0:2 * NS, 0:2 * NS], ["ident_f"] + ar("sc_tok"), b)
                        ACT("copy", dict(out=scT[:].rearrange("p c t -> p (c t)"), in_=ps[:, b, 0:4 * 2 * NS]), r=ar(), w=aw("scT"), x=[BK(b)])
                        bput(b)
                        DMA("sp", "dma_start", dict(out=cs[l, :, 0, :], in_=scv[l, :, 1, :]), w=[("cs0", l)])

                    def conv_mm(nm, post, post_s):
                        bs = bget() if has_s else None
                        for wi in range(2):
                            i, slot, wk = wtake((nm, g, l, wi))
                            for sub in range(2):
                                c = wi * 2 + sub
                                b = bget()
                                for kc in range(NKC):
                                    mm(ps[:, b, :], slot[:, kc, sub * 128:(sub + 1) * 128], uT[:, kc, 0:BT], kc == 0, kc == NKC - 1, [wk, ("uT", kc)], b)
                                post(c, b)
                                bput(b)
                                if has_s:
                                    for kc in range(NKC):
                                        mm(ps[:, bs, c * NS:(c + 1) * NS], slot[:, kc, sub * 128:(sub + 1) * 128], uT[:, kc, BT:TW],
                                           kc == 0, kc == NKC - 1, [wk, "uTs"], bs)
                            wrel(i)
                        if has_s:
                            post_s(ps[:, bs, 0:4 * NS].rearrange("p (c t) -> p c t", c=4), bs)
                            bput(bs)

                    conv_mm("cc",
                            lambda c, b: ACT("copy", dict(out=ccy[:, c, 0:BT], in_=ps[:, b, :]), r=ar(), w=aw(f"ccy{c}"), x=[BK(b)]),
                            lambda pv, bs: ACT("copy", dict(out=ccy[:, :, BT:TW], in_=pv), r=ar(), w=aw("ccy_s"), x=[BK(bs)]))

                    def post_ch(c, b):
                        DVE("tensor_tensor", dict(out=pbuf[:, c, 2:BT + 2], in0=ps[:, b, :], in1=ccy[:, c, 0:BT], op=ALU.mult),
                            r=ar(f"ccy{c}"), w=aw(f"pb{c}"), x=[BK(b)])
                        ACT("activation", dict(out=tmp1, in_=pbuf[:, c, 0:BT], func=AF.Identity, scale=convw[:, l, c, 0:1]),
                            r=["convw"] + ar(f"pb{c}"), w=aw("tmp1"))
                        DVE("scalar_tensor_tensor", dict(out=tmp2, in0=pbuf[:, c, 1:BT + 1], scalar=convw[:, l, c, 1:2], in1=tmp1, op0=ALU.mult, op1=ALU.add),
                            r=["convw"] + ar(f"pb{c}", "tmp1"), w=aw("tmp2"))
                        DVE("scalar_tensor_tensor", dict(out=ccy[:, c, 0:BT], in0=pbuf[:, c, 2:BT + 2], scalar=convw[:, l, c, 2:3], in1=tmp2, op0=ALU.mult, op1=ALU.add),
                            r=["convw"] + ar(f"pb{c}", "tmp2"), w=aw(f"ccy{c}"))

                    def post_ch_s(pv, bs):
                        DVE("tensor_tensor", dict(out=p_s[:], in0=pv, in1=ccy[:, :, BT:TW], op=ALU.mult), r=ar("ccy_s"), w=aw("p_s"), x=[BK(bs)])
                        for c in range(4):
                            ACT("activation", dict(out=tcs1, in_=scT[:, c, 0:2 * NS:2], func=AF.Identity, scale=convw[:, l, c, 0:1]),
                                r=["convw"] + ar("scT"), w=aw("tcs1"))
                            DVE("scalar_tensor_tensor", dict(out=tcs2, in0=scT[:, c, 1:2 * NS:2], scalar=convw[:, l, c, 1:2], in1=tcs1, op0=ALU.mult, op1=ALU.add),
                                r=["convw"] + ar("scT", "tcs1"), w=aw("tcs2"))
                            DVE("scalar_tensor_tensor", dict(out=ccy[:, c, BT:TW], in0=p_s[:, c, :], scalar=convw[:, l, c, 2:3], in1=tcs2, op0=ALU.mult, op1=ALU.add),
                                r=["convw"] + ar("p_s", "tcs2"), w=aw("ccy_s"))

                    conv_mm("ch", post_ch, post_ch_s)
                    DVE("tensor_copy", dict(out=carry[:, l, :, :], in_=pbuf[:, :, BT:BT + 2]), r=ar("pb0", "pb1", "pb2", "pb3"), w=["carry"])
                    if last_g:
                        b = bget()
                        for c in range(4):
                            tp(ps[0:2, b, c * 128:(c + 1) * 128], carry[:, l, c, :], ident_f[:], ["ident_f", "carry"], b)
                        ACT("copy", dict(out=cst, in_=ps[0:2, b, :]), r=ar(), w=aw("cst"), x=[BK(b)])
                        bput(b)
                        DMA("sp", "dma_start", dict(out=cp[l], in_=cst), r=ar("cst"), w=[("cp", l)])
                    if has_s:
                        b = bget()
                        for c in range(4):
                            tp(ps[0:NS, b, c * 128:(c + 1) * 128], p_s[:, c, :], ident_f[:], ["ident_f"] + ar("p_s"), b)
                        ACT("copy", dict(out=pst, in_=ps[0:NS, b, :]), r=ar(), w=aw("pst"), x=[BK(b)])
                        bput(b)
                        DMA("sp", "dma_start", dict(out=cs[l, :, 1, :], in_=pst), r=ar("pst"), w=[("cs1", l)])
                    conv_mm("cb",
                            lambda c, b: DVE("tensor_tensor", dict(out=ccy[:, c, 0:BT], in0=ps[:, b, :], in1=ccy[:, c, 0:BT], op=ALU.mult),
                                             r=ar(f"ccy{c}"), w=aw(f"ccy{c}"), x=[BK(b)]),
                            lambda pv, bs: DVE("tensor_tensor", dict(out=ccy[:, :, BT:TW], in0=pv, in1=ccy[:, :, BT:TW], op=ALU.mult),
                                               r=ar("ccy_s"), w=aw("ccy_s"), x=[BK(bs)]))

                    def post_gc(c, b):
                        ACT("activation", dict(out=tmp3, in_=ps[:, b, :], func=AF.Silu), r=ar(), w=aw("tmp3"), x=[BK(b)])
                        DVE("tensor_tensor", dict(out=yc[:, c, 0:BT], in0=ccy[:, c, 0:BT], in1=tmp3, op=ALU.mult), r=ar(f"ccy{c}", "tmp3"), w=[("yc", c)])

                    def post_gc_s(pv, bs):
                        ACT("activation", dict(out=tmp3s[:], in_=pv, func=AF.Silu), r=ar(), w=aw("tmp3s"), x=[BK(bs)])
                        DVE("tensor_tensor", dict(out=yc[:, :, BT:TW], in0=ccy[:, :, BT:TW], in1=tmp3s[:], op=ALU.mult), r=ar("ccy_s", "tmp3s"), w=["yc_s"])

                    conv_mm("gc", post_gc, post_gc_s)

                    ogr = [("og", j) for j in range(GT)]
                    ycr = [("yc", c) for c in range(4)]
                    for cpi in range(4):
                        ia, sa, ka = wtake(("pa", g, l, cpi))
                        ib, sbb, kb = wtake(("pb", g, l, cpi))
                        ig, sgw, kg = wtake(("mg", g, l, cpi))
                        ic, scw, kcw = wtake(("mc", g, l, cpi))
                        for sub in range(2):
                            c = cpi * 2 + sub
                            cs0, cs1 = sub * 128, (sub + 1) * 128
                            bA = bget()
                            for kc in range(NKC):
                                mm(ps[:, bA, :], sa[:, kc, cs0:cs1], og[:, kc, 0:BT], kc == 0, kc == NKC - 1, [ka] + ogr, bA)
                            bB = bget()
                            for kc in range(4):
                                mm(ps[:, bB, :], sbb[:, kc, cs0:cs1], yc[:, kc, 0:BT], kc == 0, kc == 3, [kb, ("yc", kc)], bB)
                            bG = bget()
                            for kc in range(NKC):
                                mm(ps[:, bG, :], sgw[:, kc, cs0:cs1], uT[:, kc, 0:BT], kc == 0, kc == NKC - 1, [kg, ("uT", kc)], bG)
                            bC = bget()
                            for kc in range(NKC):
                                mm(ps[:, bC, :], scw[:, kc, cs0:cs1], uT[:, kc, 0:BT], kc == 0, kc == NKC - 1, [kcw, ("uT", kc)], bC)
                            ACT("activation", dict(out=sgm, in_=ps[:, bG, :], func=AF.Sigmoid), r=ar(), w=aw("sgm"), x=[BK(bG)])
                            bput(bG)
                            ACT("activation", dict(out=scm, in_=ps[:, bC, :], func=AF.Sigmoid), r=ar(), w=aw("scm"), x=[BK(bC)])
                            bput(bC)
                            DVE("tensor_tensor", dict(out=tm1, in0=ps[:, bA, :], in1=sgm, op=ALU.mult), r=ar("sgm"), w=aw("tm1"), x=[BK(bA)])
                            bput(bA)
                            DVE("tensor_tensor", dict(out=tm2, in0=ps[:, bB, :], in1=scm, op=ALU.mult), r=ar("scm"), w=aw("tm2"), x=[BK(bB)])
                            bput(bB)
                            DVE("tensor_tensor", dict(out=mg[:, c, 0:BT], in0=tm1, in1=tm2, op=ALU.add), r=ar("tm1", "tm2"), w=aw(f"mg{c}"))
                            if has_s:
                                bS = bget()
                                for kc in range(NKC):
                                    mm(ps[:, bS, 0:NS], sa[:, kc, cs0:cs1], og[:, kc, BT:TW], kc == 0, kc == NKC - 1, [ka, "og_s"], bS)
                                for kc in range(4):
                                    mm(ps[:, bS, NS:2 * NS], sbb[:, kc, cs0:cs1], yc[:, kc, BT:TW], kc == 0, kc == 3, [kb, "yc_s"], bS)
                                for kc in range(NKC):
                                    mm(ps[:, bS, 2 * NS:3 * NS], sgw[:, kc, cs0:cs1], uT[:, kc, BT:TW], kc == 0, kc == NKC - 1, [kg, "uTs"], bS)
                                for kc in range(NKC):
                                    mm(ps[:, bS, 3 * NS:4 * NS], scw[:, kc, cs0:cs1], uT[:, kc, BT:TW], kc == 0, kc == NKC - 1, [kcw, "uTs"], bS)
                                ACT("activation", dict(out=sg_s, in_=ps[:, bS, 2 * NS:4 * NS], func=AF.Sigmoid), r=ar(), w=aw("sg_s2"), x=[BK(bS)])
                                DVE("tensor_tensor", dict(out=t_s, in0=ps[:, bS, 0:2 * NS], in1=sg_s, op=ALU.mult), r=ar("sg_s2"), w=aw("t_s"), x=[BK(bS)])
                                bput(bS)
                                DVE("tensor_tensor", dict(out=mg[:, c, BT:TW], in0=t_s[:, 0:NS], in1=t_s[:, NS:2 * NS], op=ALU.add), r=ar("t_s"), w=aw("mg_s"))
                        wrel(ia)
                        wrel(ib)
                        wrel(ig)
                        wrel(ic)

                    so = [wtake(("o", g, l, wi)) for wi in range(4)]
                    mgr = ar(*[f"mg{c}" for c in range(NKC)])

                    def o_tail(xv, gv, tov, stv, mvv, lg, lb, bo, npart, xkey, mkeys):
                        if gv is not None:
                            DVE("tensor_tensor", dict(out=tov.rearrange("p (b n) -> p b n", b=2), in0=ps[0:npart, bo:bo + 2, :],
                                                      in1=gv.rearrange("p (b n) -> p b n", b=2), op=ALU.mult),
                                r=["gate_bc", "gate_s"] + ar(), w=aw(mkeys + "to"), x=[BK(bo), BK(bo + 1)])
                            bput(bo, 2)
                            DVE("scalar_tensor_tensor", dict(out=xv, in0=xv, scalar=ALPHA, in1=tov, op0=ALU.mult, op1=ALU.add),
                                r=[xkey] + ar(mkeys + "to"), w=[xkey])
                        else:
                            DVE("scalar_tensor_tensor", dict(out=xv.rearrange("p (b n) -> p b n", b=2), in0=xv.rearrange("p (b n) -> p b n", b=2), scalar=ALPHA,
                                                             in1=ps[0:npart, bo:bo + 2, :], op0=ALU.mult, op1=ALU.add),
                                r=[xkey], w=[xkey], x=[BK(bo), BK(bo + 1)])
                            bput(bo, 2)
                        for hh in range(2):
                            DVE("bn_stats", dict(out=stv[:, hh, :], in_=xv[:, hh * 512:(hh + 1) * 512]), r=[xkey], w=[mkeys + "st"])
                        DVE("bn_aggr", dict(out=mvv[:, 0:2], in_=stv.rearrange("p a b -> p (a b)")), r=[mkeys + "st"], w=[mkeys + "mv"])
                        ACT("activation", dict(out=mvv[:, 2:3], in_=mvv[:, 1:2], func=AF.Ln, bias=EPS), r=[mkeys + "mv"], w=[mkeys + "mv2"])
                        ACT("activation", dict(out=mvv[:, 3:4], in_=mvv[:, 2:3], func=AF.Exp, scale=-0.5), r=[mkeys + "mv2"], w=[mkeys + "mv3"])
                        DVE("tensor_scalar", dict(out=mvv[:, 4:5], in0=mvv[:, 0:1], scalar1=mvv[:, 3:4], scalar2=-1.0, op0=ALU.mult, op1=ALU.mult),
                            r=[mkeys + "mv", mkeys + "mv3"], w=[mkeys + "mv4"])
                        ACT("activation", dict(out=xv, in_=xv, func=AF.Identity, bias=mvv[:, 4:5], scale=mvv[:, 3:4]),
                            r=[xkey, mkeys + "mv3", mkeys + "mv4"], w=[xkey])
                        DVE("tensor_tensor", dict(out=xv, in0=xv, in1=lg, op=ALU.mult), r=[xkey, "lng"], w=[xkey])
                        DVE("tensor_tensor", dict(out=xv, in0=xv, in1=lb, op=ALU.add), r=[xkey, "lnb"], w=[xkey])

                    if has_s:
                        bo = bget2()
                        for wi in range(4):
                            for kc in range(NKC):
                                mm(ps[0:NS, bo + wi // 2, (wi % 2) * WC:(wi % 2 + 1) * WC], mg[:, kc, BT:TW], so[wi][1][:, kc, :],
                                   kc == 0, kc == NKC - 1, [so[wi][2]] + ar("mg_s"), bo + wi // 2)
                        o_tail(xs_sb[:], gate_s[:, l, :], to_s, st_s[:], mv_s[:], lng_bc[0:NS, :], lnb_bc[0:NS, :], bo, NS, "xs", "s")
                    for j in range(GT):
                        bo = bget2()
                        for wi in range(4):
                            for kc in range(NKC):
                                mm(ps[:, bo + wi // 2, (wi % 2) * WC:(wi % 2 + 1) * WC], mg[:, kc, j * 128:(j + 1) * 128], so[wi][1][:, kc, :],
                                   kc == 0, kc == NKC - 1, [so[wi][2]] + mgr, bo + wi // 2)
                        o_tail(x_sb[:, j, :], gate_bc[:, l, :], to, st[:], mv[:], lng_bc[:], lnb_bc[:], bo, 128, ("x", j), "p")
                        if l == nl_run - 1:
                            DMA("sp", "dma_start", dict(out=yp[g * BT + j * 128: g * BT + (j + 1) * 128, :], in_=x_sb[:, j, :]),
                                r=[("x", j)], w=[("yp", g, j)])
                    for wi in range(4):
                        wrel(so[wi][0])
                    if l == nl_run - 1:
                        pass
                        if has_s:
                            DMA("sp", "dma_start", dict(out=ys, in_=xs_sb[:]), r=["xs"], w=["ys"])

            assert wstate["taken"] == len(wlist), (wstate, len(wlist))
            return wlist

        wl = emit_all(Prog(), True, None)
        P = Prog()
        emit_all(P, False, wl)
        P.finalize()

        @block.tensor
        def _(e):
            P.run_engine("pe", e, sems, dma_sems)

        @block.scalar
        def _(e):
            P.run_engine("act", e, sems, dma_sems)

        @block.vector
        def _(e):
            P.run_engine("dve", e, sems, dma_sems)

        @block.gpsimd
        def _(e):
            P.run_engine("pool", e, sems, dma_sems)

        @block.sync
        def _(e):
            P.run_engine("sp", e, sems, dma_sems)
            P.final_wait(e, dma_sems)

    return nc


_NC_CACHE = {}


def kernel(x_prompt, x_sample, c_prompt, c_sample, state_gla, state_conv, w_ada, b_ada, w_in, w_a2, b_a,
           gla_norm_g, conv_w, w_pa, w_pb, w_o, ln_g, ln_b):
    f = lambda a: np.ascontiguousarray(np.asarray(a, dtype=np.float32))
    x_prompt, x_sample, c_prompt, c_sample = f(x_prompt), f(x_sample), f(c_prompt), f(c_sample)
    state_gla, state_conv = f(state_gla), f(state_conv)
    shared = dict(w_ada=f(w_ada), b_ada=f(b_ada), w_in=f(w_in), w_a2=f(w_a2), b_a=f(b_a), gng=f(gla_norm_g),
                  conv_w=f(conv_w), w_pa=f(w_pa), w_pb=f(w_pb), w_o=f(w_o), ln_g=f(ln_g), ln_b=f(ln_b))
    ncores = 8
    in_maps = []
    for i in range(ncores):
        sl = slice(i * NS, (i + 1) * NS)
        m = dict(shared)
        m["xp"] = x_prompt[i]
        m["xs"] = np.ascontiguousarray(x_sample[sl, 0, :])
        m["c17"] = np.ascontiguousarray(np.concatenate([c_prompt[i:i + 1], c_sample[sl]], axis=0))
        m["sgl"] = np.ascontiguousarray(state_gla[:, sl])
        m["scv"] = np.ascontiguousarray(state_conv[:, sl])
        in_maps.append(m)
    if "nc" not in _NC_CACHE:
        _NC_CACHE["nc"] = build_program()
    res = run_bass_kernel_spmd(_NC_CACHE["nc"], in_maps, core_ids=list(range(ncores)))
    R = res.results
    y_prompt = np.stack([R[i]["yp"] for i in range(ncores)], axis=0)
    y_sample = np.concatenate([R[i]["ys"] for i in range(ncores)], axis=0)[:, None, :]
    gla_p = np.stack([R[i]["gp"] for i in range(ncores)], axis=1)
    conv_p = np.stack([R[i]["cp"] for i in range(ncores)], axis=1)
    gla_s = np.concatenate([R[i]["gs"] for i in range(ncores)], axis=1)
    conv_s = np.concatenate([R[i]["cs"] for i in range(ncores)], axis=1)
    return (y_prompt.astype(np.float32), y_sample.astype(np.float32), gla_p.astype(np.float32),
            conv_p.astype(np.float32), gla_s.astype(np.float32), conv_s.astype(np.float32))
```

```python
import os
import numpy as np
from contextlib import ExitStack
import concourse.bass as bass
import concourse.mybir as mybir
from concourse.bass_utils import run_bass_kernel_spmd

F32 = mybir.dt.float32
BF16 = mybir.dt.bfloat16
AF = mybir.ActivationFunctionType
ALU = mybir.AluOpType

ENGS = ("pe", "act", "dve", "pool", "sp")

D = 1024
SEQ = 2048
DEPTH = 2
NS = 16
H = 4
DK = 128
DV = 256
NKC = 8
BT = 512
GT = 4
NG = SEQ // BT
TW = BT + NS
SG = 0
ALPHA = (2 * DEPTH) ** 0.25
EPS = 1e-5
QSCALE = DK ** -0.5
O_Q, O_K, O_V, O_GG, O_ALR, O_CB, O_CC, O_CH, O_GC, O_MG, O_MC = 0, 512, 1024, 2048, 3072, 3088, 3600, 4112, 4624, 5136, 6160
NW = 8
WC = 256


class Ins:
    __slots__ = ("eng", "fn", "deps", "dma", "sig", "semid", "semval", "pos")

    def __init__(self, eng, fn, dma, pos):
        self.eng = eng
        self.fn = fn
        self.deps = []
        self.dma = dma
        self.sig = False
        self.semid = None
        self.semval = None
        self.pos = pos


class Prog:
    def __init__(self, n_dma_sems=8):
        self.lists = {e: [] for e in ENGS}
        self.last_w = {}
        self.rd_eng = {}
        self.rd_dma = {}
        self.n_dma_sems = n_dma_sems

    def add(self, eng, fn, reads=(), writes=(), excl=(), dma=False):
        I = Ins(eng, fn, dma, len(self.lists[eng]))
        cand = []
        writes = list(writes) + list(excl)
        for k in reads:
            w = self.last_w.get(k)
            if w is not None:
                cand.append((w, True))
        for k in writes:
            w = self.last_w.get(k)
            if w is not None:
                cand.append((w, False))
            for r in self.rd_eng.get(k, {}).values():
                cand.append((r, False))
            for r in self.rd_dma.get(k, ()):
                cand.append((r, False))
        for k in writes:
            self.last_w[k] = I
            self.rd_eng[k] = {}
            self.rd_dma[k] = []
        for k in reads:
            if dma:
                self.rd_dma.setdefault(k, []).append(I)
            else:
                self.rd_eng.setdefault(k, {})[eng] = I
        best = {}
        seen = set()
        for d, raw in cand:
            if d is I:
                continue
            if d.dma:
                if id(d) not in seen:
                    seen.add(id(d))
                    I.deps.append(d)
                continue
            if d.eng == eng and not dma:
                if eng == "pe":
                    continue
            b = best.get(d.eng)
            if b is None or d.pos > b.pos:
                best[d.eng] = d
        for d in best.values():
            d.sig = True
            I.deps.append(d)
        self.lists[eng].append(I)
        return I

    def finalize(self):
        for e in ENGS:
            cnt = 0
            dcnt = {}
            k = 0
            for I in self.lists[e]:
                if I.dma:
                    I.semid = k % self.n_dma_sems
                    k += 1
                    dcnt[I.semid] = dcnt.get(I.semid, 0) + 16
                    I.semval = dcnt[I.semid]
                elif I.sig:
                    cnt += 1
                    I.semval = cnt

    def run_engine(self, e, engobj, sems, dma_sems):
        waited = {}
        for I in self.lists[e]:
            need = {}
            for d in I.deps:
                if d.dma:
                    key = ("d", d.eng, d.semid)
                    s = dma_sems[d.eng][d.semid]
                else:
                    key = ("e", d.eng)
                    s = sems[d.eng]
                if need.get(key, (None, 0))[1] < d.semval:
                    need[key] = (s, d.semval)
            if I.dma and I.semval > 16:
                key = ("d", e, I.semid)
                if need.get(key, (None, 0))[1] < I.semval - 16:
                    need[key] = (dma_sems[e][I.semid], I.semval - 16)
            for key, (s, v) in need.items():
                if waited.get(key, 0) < v:
                    engobj.wait_ge(s, v)
                    waited[key] = v
            ins = getattr(engobj, I.fn[0])(**I.fn[1])
            if I.dma:
                ins.then_inc(dma_sems[e][I.semid], 16)
            elif I.sig:
                ins.then_inc(sems[e], 1)

    def final_wait(self, engobj, dma_sems):
        for e in ENGS:
            last = {}
            for I in self.lists[e]:
                if I.dma:
                    last[I.semid] = I.semval
            for sid, v in last.items():
                engobj.wait_ge(dma_sems[e][sid], v)


def build_program(ng_run=NG, nl_run=DEPTH):
    nc = bass.Bass("TRN2", target_bir_lowering=False)

    def din(name, shape):
        return nc.dram_tensor(name, list(shape), F32, kind="ExternalInput").ap()

    def dout(name, shape):
        return nc.dram_tensor(name, list(shape), F32, kind="ExternalOutput").ap()

    xp = din("xp", [SEQ, D])
    xs = din("xs", [NS, D])
    c17d = din("c17", [NS + 1, D])
    sgl = din("sgl", [DEPTH, NS, H, DK, DV])
    scv = din("scv", [DEPTH, NS, 2, 512])
    w_ada = din("w_ada", [DEPTH, D, 3 * D])
    b_ada = din("b_ada", [DEPTH, 3 * D])
    w_in = din("w_in", [DEPTH, D, 7184])
    w_a2 = din("w_a2", [DEPTH, 16, 512])
    b_a = din("b_a", [DEPTH, 512])
    gng = din("gng", [DEPTH, DV])
    conv_w = din("conv_w", [DEPTH, 3, 512])
    w_pa = din("w_pa", [DEPTH, D, D])
    w_pb = din("w_pb", [DEPTH, 512, D])
    w_o = din("w_o", [DEPTH, D, D])
    ln_g = din("ln_g", [DEPTH, D])
    ln_b = din("ln_b", [DEPTH, D])
    yp = dout("yp", [SEQ, D])
    ys = dout("ys", [NS, D])
    gp = dout("gp", [DEPTH, H, DK, DV])
    cp = dout("cp", [DEPTH, 2, 512])
    gs = dout("gs", [DEPTH, NS, H, DK, DV])
    cs = dout("cs", [DEPTH, NS, 2, 512])

    with ExitStack() as es:
        def sb(name, shape, dt):
            return es.enter_context(nc.sbuf_tensor(name, list(shape), dt))

        x_sb = sb("x_sb", [128, GT, D], F32)
        xs_sb = sb("xs_sb", [NS, D], F32)
        uT = sb("uT", [128, NKC, TW], BF16)
        og = sb("og", [128, NKC, TW], BF16)
        yc = sb("yc", [128, 4, TW], BF16)
        wring = sb("wring", [128, NW, NKC, WC], BF16)
        gate_bc = sb("gate_bc", [128, DEPTH, D], F32)
        gate_s = sb("gate_s", [NS, DEPTH, D], F32)
        lng_bc = sb("lng_bc", [128, D], F32)
        lnb_bc = sb("lnb_bc", [128, D], F32)
        S = sb("S", [128, DEPTH, H * DV], F32)
        S_bf = sb("S_bf", [128, DEPTH, H * DV], BF16)
        Sbuf = sb("Sbuf", [128, 3, H * DV], F32)
        sbf = sb("sbf", [128, 2, H * DV], BF16)
        ident_f = sb("ident_f", [128, 128], F32)
        ident_b = sb("ident_b", [128, 128], BF16)
        tri = sb("tri", [128, 128], F32)
        ones_b = sb("ones_b", [128, 128], BF16)
        alr_aug = sb("alr_aug", [17, TW], F32)
        w_a2aug = sb("w_a2aug", [17, DEPTH, 512], F32)
        gnorm = sb("gnorm", [128, DEPTH, 2], F32)
        convw = sb("convw", [128, DEPTH, 4, 3], F32)
        ada_sb = sb("ada_sb", [128, DEPTH, 16, 17], F32)
        b_adaT = sb("b_adaT", [128, DEPTH, 24], F32)
        carry = sb("carry", [128, DEPTH, 4, 2], F32)
        cT = sb("cT", [128, NKC, 17], BF16)
        tmp_s = sb("tmp_s", [128, NKC, NS], F32)
        st = sb("st", [128, 2, 6], F32)
        mv = sb("mv", [128, 8], F32)
        st_s = sb("st_s", [NS, 2, 6], F32)
        mv_s = sb("mv_s", [NS, 8], F32)
        dummy = sb("dummy_t", [128, 8], F32)
        ARENA_F = 16384 + 256 + 1024 + 256 + 512
        arena = sb("arena", [128, ARENA_F], F32)

        ps = es.enter_context(nc.psum_tensor("ps", [128, 8, 512], F32))

        sems = {e: es.enter_context(nc.semaphore(f"s_{e}")) for e in ENGS}
        dma_sems = {e: [es.enter_context(nc.semaphore(f"d_{e}{i}")) for i in range(8)]
                    for e in ("sp", "pool")}
        block = es.enter_context(nc.Block())

        class Arena:
            def __init__(self):
                self.off = 0

            def f32(self, n, parts=128):
                a = arena[0:parts, self.off:self.off + n]
                self.off += n
                assert self.off <= ARENA_F, self.off
                return a

            def bf16(self, n, parts=128):
                nf = (n + 1) // 2
                a = arena[0:parts, self.off:self.off + nf].bitcast(BF16)
                self.off += nf
                assert self.off <= ARENA_F, self.off
                return a

        def v3(ap, c):
            return ap.rearrange("p (c t) -> p c t", c=c)

        A = Arena()
        E_q = v3(A.f32(4 * BT), 4)
        E_k = v3(A.f32(4 * BT), 4)
        q_in = v3(A.bf16(4 * BT), 4)
        k_in = v3(A.bf16(4 * BT), 4)
        sgt = v3(A.bf16(8 * TW), 8)
        v_tok = v3(A.bf16(GT * 1024), GT)
        k_tok = v3(A.bf16(2 * 512), 2)
        _o = A.off
        e1 = A.f32(512)
        A.off = _o
        ktok_s = A.f32(512, NS)
        sp = v3(A.f32(2 * 512), 2)
        v_s = A.bf16(1024, NS)
        attm = v3(A.bf16(2 * 512), 2)
        sq = v3(A.bf16(8 * 128), 8)
        lnv = A.f32(512)
        rstd2 = A.f32(2 * 512).rearrange("p (b h t) -> p b h t", b=2, h=4)
        t1 = v3(A.f32(1024), 8)
        Pst = A.f32(1024)
        q_s = v3(A.bf16(4 * NS), 4)
        a_s = v3(A.f32(4 * NS), 4)
        km2 = v3(A.bf16(2 * 512, NS), 2)
        sq_s = v3(A.bf16(8 * NS), 8)
        lnv_s = A.f32(4 * NS)
        rstd_s = v3(A.f32(4 * NS), 4)
        t1_s = v3(A.f32(8 * NS), 8)
        g_end = A.off
        A = Arena()
        c17 = A.f32(D, NS + 1)
        cTb = v3(A.bf16(NKC * 128), NKC)
        bgate = A.f32(D)
        A = Arena()
        ccy = v3(A.f32(4 * TW), 4)
        pbuf = v3(A.f32(4 * (BT + 2)), 4)
        tmp1 = A.f32(512)
        tmp2 = A.f32(512)
        tmp3 = A.f32(512)
        p_s = v3(A.f32(4 * NS), 4)
        scT = v3(A.f32(4 * 2 * NS), 4)
        sc_tok = A.f32(512, 2 * NS)
        tcs1 = A.f32(NS)
        tcs2 = A.f32(NS)
        tmp3s = v3(A.f32(4 * NS), 4)
        cst = A.f32(512, 2)
        pst = A.f32(512, NS)
        sgm = A.f32(512)
        scm = A.f32(512)
        tm1 = A.f32(512)
        tm2 = A.f32(512)
        sg_s = A.f32(2 * NS)
        t_s = A.f32(2 * NS)
        mg = v3(A.bf16(NKC * TW), NKC)
        to = A.f32(D)
        to_s = A.f32(D, NS)
        cmo_end = A.off

        def emit_all(P, record, wlist_in):
            def PE(meth, kw, r=(), x=()):
                return P.add("pe", (meth, kw), reads=r, excl=x)

            def ACT(meth, kw, r=(), w=(), x=()):
                return P.add("act", (meth, kw), reads=r, writes=w, excl=x)

            def DVE(meth, kw, r=(), w=(), x=()):
                return P.add("dve", (meth, kw), reads=r, writes=w, excl=x)

            def POOL(meth, kw, r=(), w=()):
                return P.add("pool", (meth, kw), reads=r, writes=w)

            def DMA(q, meth, kw, r=(), w=()):
                return P.add(q, (meth, kw), reads=r, writes=w, dma=True)

            EP = "EPOCH"

            def ar(*keys):
                return [EP] + ["A:" + k for k in keys]

            def aw(*keys):
                return ["A:" + k for k in keys]

            def epoch_barrier():
                DVE("memset", dict(ap=dummy[:, 0:1], constant=0.0), w=[EP])

            free_banks = list(range(8))

            def bget():
                assert free_banks, "PSUM banks exhausted"
                return free_banks.pop(0)

            def bget2():
                for i in range(len(free_banks)):
                    b = free_banks[i]
                    if b % 2 == 0 and (b + 1) in free_banks:
                        free_banks.remove(b)
                        free_banks.remove(b + 1)
                        return b
                raise AssertionError("no PSUM bank pair free")

            def bput(b, n=1):
                for i in range(n):
                    free_banks.append(b + i)

            def BK(b):
                return ("B", b)

            def mm(out, lhsT, rhs, start, stop, r, b):
                PE("matmul", dict(out=out, lhsT=lhsT, rhs=rhs, start=start, stop=stop), r=r, x=[BK(b)])

            def tp(out, in_, ident, r, b):
                PE("transpose", dict(out=out, in_=in_, identity=ident), r=r, x=[BK(b)])

            wlist = [] if record else wlist_in
            OFFS = {"q": O_Q, "k": O_K, "v": O_V, "gg": O_GG, "cc": O_CC, "ch": O_CH, "cb": O_CB, "gc": O_GC, "mg": O_MG, "mc": O_MC}

            def wspec(tag):
                nm = tag[0]
                if nm == "ada":
                    return w_ada[tag[1], :, tag[2] * WC:(tag[2] + 1) * WC], NKC, WC
                if nm == "alr":
                    return w_in[tag[2], :, O_ALR:O_ALR + 16], NKC, 16
                l, i = tag[2], tag[3]
                if nm in OFFS:
                    return w_in[l, :, OFFS[nm] + i * WC: OFFS[nm] + (i + 1) * WC], NKC, WC
                if nm == "pa":
                    return w_pa[l, :, i * WC:(i + 1) * WC], NKC, WC
                if nm == "pb":
                    return w_pb[l, :, i * WC:(i + 1) * WC], 4, WC
                if nm == "o":
                    return w_o[l, :, i * WC:(i + 1) * WC], NKC, WC
                raise KeyError(tag)

            wstate = {"issued": 0, "taken": 0}
            released = [False] * (100000 if record else len(wlist))

            def wpump():
                while wstate["issued"] < len(wlist):
                    j = wstate["issued"]
                    if j >= NW and not released[j - NW]:
                        break
                    src, nkc, ncols = wspec(wlist[j])
                    slot = j % NW
                    dst = wring[:, slot, 0:nkc, 0:ncols]
                    srcv = src.rearrange("(kc p) c -> p kc c", p=128)
                    DMA("pool", "dma_start", dict(out=dst, in_=srcv), w=[("W", slot)])
                    wstate["issued"] += 1

            def wtake(tag):
                i = wstate["taken"]
                if record:
                    wlist.append(tag)
                    wstate["taken"] += 1
                    return i, wring[:, i % NW], ("W", i % NW)
                assert wlist[i] == tag, (wlist[i], tag)
                wpump()
                assert wstate["issued"] > i, "weight ring deadlock"
                wstate["taken"] += 1
                return i, wring[:, i % NW], ("W", i % NW)

            def wrel(i):
                released[i] = True
                if not record:
                    wpump()

            POOL("memset", dict(ap=ident_f[:], constant=0.0), w=["ident_f"])
            POOL("affine_select", dict(out=ident_f[:], in_=ident_f[:], pattern=[[-1, 128]], compare_op=ALU.not_equal,
                                           fill=1.0, base=0, channel_multiplier=1), r=["ident_f"], w=["ident_f"])
            POOL("tensor_copy", dict(out=ident_b[:], in_=ident_f[:]), r=["ident_f"], w=["ident_b"])
            POOL("memset", dict(ap=tri[:], constant=1.0), w=["tri"])
            POOL("affine_select", dict(out=tri[:], in_=tri[:], pattern=[[1, 128]], compare_op=ALU.is_ge,
                                           fill=0.0, base=0, channel_multiplier=-1), r=["tri"], w=["tri"])
            POOL("memset", dict(ap=ones_b[:], constant=1.0), w=["ones_b"])
            POOL("memset", dict(ap=alr_aug[:], constant=1.0), w=["alr_aug"])
            POOL("memset", dict(ap=carry[:], constant=0.0), w=["carry"])
            POOL("memset", dict(ap=S[:], constant=0.0), w=[("S", l_, h_) for l_ in range(DEPTH) for h_ in range(H)])
            POOL("memset", dict(ap=S_bf[:], constant=0.0), w=[("S_bf", l_, h_) for l_ in range(DEPTH) for h_ in range(H)])

            DMA("sp", "dma_start", dict(out=c17, in_=c17d), w=aw("c17"))
            DMA("sp", "dma_start", dict(out=xs_sb[:], in_=xs), w=["xs"])
            for l in range(DEPTH):
                DMA("sp", "dma_start", dict(out=w_a2aug[0:16, l, :], in_=w_a2[l]), w=["w_a2aug"])
                DMA("sp", "dma_start", dict(out=w_a2aug[16:17, l, :], in_=b_a[l:l + 1, :]), w=["w_a2aug"])
                DMA("sp", "dma_start", dict(out=gnorm[:, l, :], in_=gng[l].rearrange("(c p) -> p c", p=128),
                                                     allow_slow_non_contiguous=True), w=["gnorm"])
                for ti in range(3):
                    DMA("sp", "dma_start", dict(out=convw[:, l, :, ti], in_=conv_w[l, ti].rearrange("(c p) -> p c", p=128),
                                                allow_slow_non_contiguous=True), w=["convw"])
                DMA("sp", "dma_start", dict(out=b_adaT[:, l, :], in_=b_ada[l].rearrange("(f p) -> p f", p=128),
                                                     allow_slow_non_contiguous=True), w=["b_adaT"])

            b = bget()
            for c in range(NKC):
                tp(ps[:, b, c * 17:(c + 1) * 17], c17[:, c * 128:(c + 1) * 128], ident_f[0:17, 0:17], ["ident_f"] + ar("c17"), b)
            ACT("copy", dict(out=cT[:].rearrange("p c t -> p (c t)"), in_=ps[:, b, 0:NKC * 17]), w=["cT"], x=[BK(b)])
            bput(b)
            for c in range(NKC):
                DVE("tensor_scalar", dict(out=cTb[:, c, :], in0=ones_b[:], scalar1=cT[:, c, 0:1], scalar2=None, op0=ALU.mult),
                    r=["ones_b", "cT"] + ar(), w=aw("cTb"))
            for l in range(DEPTH):
                DVE("tensor_scalar", dict(out=b_adaT[:, l, 8:16], in0=b_adaT[:, l, 8:16], scalar1=1.0, scalar2=None, op0=ALU.add),
                    r=["b_adaT"], w=["b_adaT"])
                DMA("sp", "dma_start", dict(out=bgate, in_=b_ada[l, 2 * D:3 * D].partition_broadcast(128)), w=aw("bgate"))
                b = bget()
                for wi in range(8):
                    i, slot, wk = wtake(("ada", l, wi))
                    for sub in range(2):
                        fc = wi * 2 + sub
                        for kc in range(NKC):
                            mm(ps[:, b, fc * 17:(fc + 1) * 17], slot[:, kc, sub * 128:(sub + 1) * 128], cT[:, kc, :],
                               kc == 0, kc == NKC - 1, [wk, "cT"], b)
                    wrel(i)
                DVE("tensor_tensor", dict(out=ada_sb[:, l, :, :], in0=ps[:, b, 0:16 * 17].rearrange("p (f t) -> p f t", f=16),
                                                       in1=b_adaT[:, l, 0:16].unsqueeze(2).to_broadcast([128, 16, 17]), op=ALU.add),
                    r=["b_adaT"], w=["ada_sb"], x=[BK(b)])
                bput(b)
                for half in range(2):
                    bp = bget()
                    bs = bget()
                    for wj in range(2):
                        wi = 8 + half * 2 + wj
                        i, slot, wk = wtake(("ada", l, wi))
                        for kc in range(NKC):
                            mm(ps[:, bp, wj * WC:(wj + 1) * WC], cTb[:, kc, :], slot[:, kc, :], kc == 0, kc == NKC - 1, [wk] + ar("cTb"), bp)
                        for kc in range(NKC):
                            mm(ps[0:NS, bs, wj * WC:(wj + 1) * WC], cT[:, kc, 1:17], slot[:, kc, :], kc == 0, kc == NKC - 1, [wk, "cT"], bs)
                        wrel(i)
                    DVE("tensor_tensor", dict(out=gate_bc[:, l, half * 512:(half + 1) * 512], in0=ps[:, bp, :],
                                                                        in1=bgate[:, half * 512:(half + 1) * 512], op=ALU.add),
                        r=ar("bgate"), w=["gate_bc"], x=[BK(bp)])
                    DVE("tensor_tensor", dict(out=gate_s[:, l, half * 512:(half + 1) * 512], in0=ps[0:NS, bs, :],
                                                                        in1=bgate[0:NS, half * 512:(half + 1) * 512], op=ALU.add),
                        r=ar("bgate"), w=["gate_s"], x=[BK(bs)])
                    bput(bp)
                    bput(bs)

            for g in range(ng_run):
                has_s = (g == SG)
                for l in range(nl_run):
                    last_g = (g == ng_run - 1)
                    if l == 0:
                        for j in range(GT):
                            DMA("sp", "dma_start", dict(out=x_sb[:, j, :], in_=xp[g * BT + j * 128: g * BT + (j + 1) * 128, :]), w=[("x", j)])
                    DMA("sp", "dma_start", dict(out=lng_bc[:], in_=ln_g[l].partition_broadcast(128)), w=["lng"])
                    DMA("sp", "dma_start", dict(out=lnb_bc[:], in_=ln_b[l].partition_broadcast(128)), w=["lnb"])

                    for c in range(NKC):
                        b = bget()
                        for j in range(GT):
                            tp(ps[:, b, j * 128:(j + 1) * 128], x_sb[:, j, c * 128:(c + 1) * 128], ident_f[:], ["ident_f", ("x", j)], b)
                        ACT("activation", dict(out=uT[:, c, 0:BT], in_=ps[:, b, :], func=AF.Identity,
                                                                  bias=ada_sb[:, l, c, 0:1], scale=ada_sb[:, l, 8 + c, 0:1]),
                            r=["ada_sb"], w=[("uT", c)], x=[BK(b)])
                        bput(b)
                    if has_s:
                        b = bget()
                        for c in range(NKC):
                            tp(ps[:, b, c * NS:(c + 1) * NS], xs_sb[:, c * 128:(c + 1) * 128], ident_f[0:NS, 0:NS], ["ident_f", "xs"], b)
                        DVE("tensor_tensor", dict(out=tmp_s[:], in0=ps[:, b, 0:NKC * NS].rearrange("p (c t) -> p c t", c=NKC),
                                                               in1=ada_sb[:, l, 8:16, 1:17], op=ALU.mult),
                            r=["ada_sb"], w=["tmp_s"], x=[BK(b)])
                        bput(b)
                        DVE("tensor_tensor", dict(out=uT[:, :, BT:TW], in0=tmp_s[:], in1=ada_sb[:, l, 0:8, 1:17], op=ALU.add),
                            r=["ada_sb", "tmp_s"], w=["uTs"])

                    uTr = [("uT", c) for c in range(NKC)]

                    epoch_barrier()
                    i, slot, wk = wtake(("alr", g, l))
                    b = bget()
                    for kc in range(NKC):
                        mm(ps[0:16, b, :], slot[:, kc, 0:16], uT[:, kc, 0:BT], kc == 0, kc == NKC - 1, [wk, ("uT", kc)], b)
                    ACT("copy", dict(out=alr_aug[0:16, 0:BT], in_=ps[0:16, b, :]), w=["alr_aug"], x=[BK(b)])
                    bput(b)
                    if has_s:
                        b = bget()
                        for kc in range(NKC):
                            mm(ps[0:16, b, 0:NS], slot[:, kc, 0:16], uT[:, kc, BT:TW], kc == 0, kc == NKC - 1, [wk, "uTs"], b)
                        ACT("copy", dict(out=alr_aug[0:16, BT:TW], in_=ps[0:16, b, 0:NS]), w=["alr_aug_s"], x=[BK(b)])
                        bput(b)
                    wrel(i)
                    bvs_h = [None]
                    bgs_h = [None]

                    def do_v(h):
                        i, slot, wk = wtake(("v", g, l, h))
                        for jp in range(2):
                            b = bget()
                            for jj in range(2):
                                j = jp * 2 + jj
                                for kc in range(NKC):
                                    mm(ps[:, b, jj * WC:(jj + 1) * WC], uT[:, kc, j * 128:(j + 1) * 128], slot[:, kc, :], kc == 0, kc == NKC - 1,
                                       [wk, ("uT", kc)], b)
                            DVE("tensor_copy", dict(out=v_tok[:, jp * 2:jp * 2 + 2, h * DV:(h + 1) * DV],
                                                    in_=ps[:, b, :].rearrange("p (j v) -> p j v", j=2)),
                                r=ar(), w=aw(f"v_tok{jp * 2}", f"v_tok{jp * 2 + 1}"), x=[BK(b)])
                            bput(b)
                        if has_s:
                            if h % 2 == 0:
                                bvs_h[0] = bget()
                            bvs = bvs_h[0]
                            for kc in range(NKC):
                                mm(ps[0:NS, bvs, (h % 2) * WC:(h % 2 + 1) * WC], uT[:, kc, BT:TW], slot[:, kc, :], kc == 0, kc == NKC - 1, [wk, "uTs"], bvs)
                            if h % 2 == 1:
                                ACT("copy", dict(out=v_s[:, (h - 1) * DV:(h + 1) * DV], in_=ps[0:NS, bvs, :]),
                                    r=ar(), w=aw("v_s"), x=[BK(bvs)])
                                bput(bvs)
                        wrel(i)

                    def do_gg(wi):
                        if has_s and wi == 0:
                            bgs_h[0] = bget()
                        bgs = bgs_h[0]
                        i, slot, wk = wtake(("gg", g, l, wi))
                        for sub in range(2):
                            fc = wi * 2 + sub
                            b = bget()
                            for kc in range(NKC):
                                mm(ps[:, b, :], slot[:, kc, sub * 128:(sub + 1) * 128], uT[:, kc, 0:BT], kc == 0, kc == NKC - 1, [wk, ("uT", kc)], b)
                            ACT("activation", dict(out=sgt[:, fc, 0:BT], in_=ps[:, b, :], func=AF.Silu), r=ar(), w=aw(f"sg{fc}"), x=[BK(b)])
                            bput(b)
                            if has_s:
                                for kc in range(NKC):
                                    mm(ps[:, bgs, fc * NS:(fc + 1) * NS], slot[:, kc, sub * 128:(sub + 1) * 128], uT[:, kc, BT:TW],
                                       kc == 0, kc == NKC - 1, [wk, "uTs"], bgs)
                        wrel(i)
                        if has_s and wi == 3:
                            ACT("activation", dict(out=sgt[:, :, BT:TW], in_=ps[:, bgs, 0:NKC * NS].rearrange("p (c t) -> p c t", c=NKC), func=AF.Silu),
                                r=ar(), w=aw("sg_s"), x=[BK(bgs)])
                            bput(bgs)

                    bulk = [(do_v, h) for h in range(H)] + [(do_gg, wi) for wi in range(4)]
                    EQK = aw(*[f"E_q{h}" for h in range(H)])
                    EKK = aw(*[f"E_k{h}" for h in range(H)])
                    for j in range(GT):
                        jj = j % 2
                        b = bget()
                        mm(ps[:, b, :], alr_aug[0:17, j * 128:(j + 1) * 128], w_a2aug[0:17, l, :], True, True, ["alr_aug", "w_a2aug"], b)
                        ACT("activation", dict(out=e1, in_=ps[:, b, :], func=AF.Exp, scale=-1.0), r=ar(), w=aw("e1"), x=[BK(b)])
                        bput(b)
                        ACT("activation", dict(out=sp[:, jj, :], in_=e1, func=AF.Ln, bias=1.0), r=ar("e1"), w=aw("sp"))
                        bc = bget()
                        for h in range(H):
                            mm(ps[:, bc, h * 128:(h + 1) * 128], sp[:, jj, h * 128:(h + 1) * 128], tri[:], True, True, ["tri"] + ar("sp"), bc)
                        csv = ps[:, bc, :].rearrange("p (h t) -> p h t", h=H)
                        ACT("activation", dict(out=E_q[:, :, j * 128:(j + 1) * 128], in_=csv, func=AF.Exp, scale=-1.0 / 16), r=ar(), w=EQK, x=[BK(bc)])
                        ACT("activation", dict(out=E_k[:, :, j * 128:(j + 1) * 128], in_=csv, func=AF.Exp, scale=1.0 / 16), r=ar(), w=EKK, x=[BK(bc)])
                        bput(bc)
                        f_, a_ = bulk.pop(0)
                        f_(a_)
                    if has_s:
                        b = bget()
                        mm(ps[0:NS, b, :], alr_aug[0:17, BT:TW], w_a2aug[0:17, l, :], True, True, ["alr_aug_s", "w_a2aug"], b)
                        ACT("activation", dict(out=e1[0:NS, :], in_=ps[0:NS, b, :], func=AF.Exp, scale=-1.0), r=ar(), w=aw("e1"), x=[BK(b)])
                        bput(b)
                        ACT("activation", dict(out=sp[0:NS, 0, :], in_=e1[0:NS, :], func=AF.Ln, bias=1.0), r=ar("e1"), w=aw("sp"))
                        b = bget()
                        for h in range(H):
                            tp(ps[:, b, h * NS:(h + 1) * NS], sp[0:NS, 0, h * 128:(h + 1) * 128], ident_f[0:NS, 0:NS], ["ident_f"] + ar("sp"), b)
                        ACT("activation", dict(out=a_s[:].rearrange("p h s -> p (h s)"), in_=ps[:, b, 0:H * NS], func=AF.Exp, scale=-1.0 / 16),
                            r=ar(), w=aw("a_s"), x=[BK(b)])
                        bput(b)
                    while bulk:
                        f_, a_ = bulk.pop(0)
                        f_(a_)
                    for nm, Eb, dst in (("q", E_q, q_in), ("k", E_k, k_in)):
                        bs = bget() if (has_s and nm == "q") else None
                        bkt = bget() if (has_s and nm == "k") else None
                        for wi in range(2):
                            i, slot, wk = wtake((nm, g, l, wi))
                            for sub in range(2):
                                h = wi * 2 + sub
                                b = bget()
                                for kc in range(NKC):
                                    mm(ps[:, b, :], slot[:, kc, sub * 128:(sub + 1) * 128], uT[:, kc, 0:BT], kc == 0, kc == NKC - 1, [wk, ("uT", kc)], b)
                                if nm == "q":
                                    DVE("scalar_tensor_tensor", dict(out=q_in[:, h, :], in0=ps[:, b, :], scalar=QSCALE, in1=E_q[:, h, :],
                                                                                   op0=ALU.mult, op1=ALU.mult),
                                        r=ar(f"E_q{h}"), w=aw(f"q_in{h}"), x=[BK(b)])
                                else:
                                    DVE("tensor_tensor", dict(out=k_in[:, h, :], in0=ps[:, b, :], in1=E_k[:, h, :], op=ALU.mult),
                                        r=ar(f"E_k{h}"), w=aw(f"k_in{h}"), x=[BK(b)])
                                bput(b)
                                if has_s and nm == "q":
                                    for kc in range(NKC):
                                        mm(ps[:, bs, h * NS:(h + 1) * NS], slot[:, kc, sub * 128:(sub + 1) * 128], uT[:, kc, BT:TW],
                                           kc == 0, kc == NKC - 1, [wk, "uTs"], bs)
                            if has_s and nm == "k":
                                for kc in range(NKC):
                                    mm(ps[0:NS, bkt, wi * WC:(wi + 1) * WC], uT[:, kc, BT:TW], slot[:, kc, :], kc == 0, kc == NKC - 1, [wk, "uTs"], bkt)
                            wrel(i)
                        if has_s and nm == "q":
                            ACT("activation", dict(out=q_s[:].rearrange("p h s -> p (h s)"), in_=ps[:, bs, 0:H * NS], func=AF.Identity, scale=QSCALE),
                                r=ar(), w=aw("q_s"), x=[BK(bs)])
                        if has_s and nm == "k":
                            ACT("copy", dict(out=ktok_s, in_=ps[0:NS, bkt, :]), r=ar(), w=aw("e1"), x=[BK(bkt)])
                            bput(bkt)
                        if has_s and nm == "q":
                            bput(bs)
                    bos = bget() if has_s else None

                    def sample_load(s):
                        DMA("sp", "dma_start", dict(out=Sbuf[:, s % 3, :].rearrange("p (h v) -> p h v", h=H), in_=sgl[l, s].rearrange("h k v -> k h v")),
                            w=[("Sbuf", s % 3)])

                    if has_s:
                        for s_ in range(3):
                            sample_load(s_)

                    def stage_A(j):
                        jj = j % 2
                        jc0, jc1 = j * 128, (j + 1) * 128
                        b = bget()
                        psb = ps[:, b, :].bitcast(BF16)
                        for h in range(H):
                            tp(psb[:, h * 128:(h + 1) * 128], k_in[:, h, jc0:jc1], ident_b[:], ["ident_b"] + ar(f"k_in{h}"), b)
                        ACT("copy", dict(out=k_tok[:, jj, :], in_=psb[:, 0:512]), r=ar(), w=aw(f"k_tok{jj}"), x=[BK(b)])
                        bput(b)
                        b = bget()
                        for h in range(H):
                            mm(ps[:, b, h * 128:(h + 1) * 128], k_in[:, h, jc0:jc1], q_in[:, h, jc0:jc1], True, True, ar(f"k_in{h}", f"q_in{h}"), b)
                        DVE("tensor_tensor", dict(out=attm[:, jj, :].rearrange("p (h t) -> p h t", h=H), in0=ps[:, b, :].rearrange("p (h t) -> p h t", h=H),
                                                  in1=tri[:].unsqueeze(1).to_broadcast([128, H, 128]), op=ALU.mult),
                            r=["tri"] + ar(), w=aw(f"attm{jj}"), x=[BK(b)])
                        bput(b)

                    def samp_A(s):
                        kmv = km2[:, s % 2, :]
                        DVE("tensor_scalar", dict(out=kmv, in0=ktok_s, scalar1=ident_f[0:NS, s:s + 1], scalar2=None, op0=ALU.mult),
                            r=["ident_f"] + ar("e1"), w=aw(f"km{s % 2}"))
                        b2 = bget2()
                        for h in range(H):
                            mm(ps[:, b2 + h // 2, (h % 2) * DV:(h % 2 + 1) * DV], kmv[:, h * 128:(h + 1) * 128], v_s[:, h * DV:(h + 1) * DV], True, True,
                               ar(f"km{s % 2}", "v_s"), b2 + h // 2)
                        return b2

                    def samp_B(s, b2):
                        buf = Sbuf[:, s % 3, :]
                        bk = ("Sbuf", s % 3)
                        sbv = sbf[:, s % 2, :]
                        sk = ("sbf", s % 2)
                        for h in range(H):
                            ACT("activation", dict(out=buf[:, h * DV:(h + 1) * DV], in_=buf[:, h * DV:(h + 1) * DV], func=AF.Identity,
                                                   scale=a_s[:, h, s:s + 1]),
                                r=[bk] + ar("a_s"), w=[bk])
                        for half in range(2):
                            DVE("tensor_tensor", dict(out=buf[:, half * 512:(half + 1) * 512], in0=ps[:, b2 + half, :],
                                                      in1=buf[:, half * 512:(half + 1) * 512], op=ALU.add),
                                r=[bk], w=[bk], x=[BK(b2 + half)])
                        bput(b2, 2)
                        DMA("sp", "dma_start", dict(out=gs[l, s].rearrange("h k v -> k h v"), in_=buf.rearrange("p (h v) -> p h v", h=H)),
                            r=[bk], w=[("gs", l, s)])
                        DVE("tensor_copy", dict(out=sbv, in_=buf), r=[bk], w=[sk])
                        if s + 3 < NS:
                            sample_load(s + 3)
                        for h in range(H):
                            for vc in range(2):
                                idx = h * 2 + vc
                                mm(ps[:, bos, idx * NS + s: idx * NS + s + 1], sbv[:, idx * 128:(idx + 1) * 128], q_s[:, h, s:s + 1], True, True,
                                   [sk] + ar("q_s"), bos)

                    stage_A(0)
                    pend_norm = [None]
                    for j in range(GT):
                        jj = j % 2
                        jc0, jc1 = j * 128, (j + 1) * 128
                        bo = bget2()
                        for h in range(H):
                            for vc in range(2):
                                idx = h * 2 + vc
                                out = ps[:, bo + idx // 4, (idx % 4) * 128:(idx % 4 + 1) * 128]
                                mm(out, v_tok[:, j, idx * 128:(idx + 1) * 128], attm[:, jj, h * 128:(h + 1) * 128], True, False,
                                   ar(f"v_tok{j}", f"attm{jj}"), bo + idx // 4)
                                mm(out, S_bf[:, l, idx * 128:(idx + 1) * 128], q_in[:, h, jc0:jc1], False, True, [("S_bf", l, h)] + ar(f"q_in{h}"), bo + idx // 4)
                        b2 = bget2()
                        for h in range(H):
                            mm(ps[:, b2 + h // 2, (h % 2) * DV:(h % 2 + 1) * DV], k_tok[:, jj, h * 128:(h + 1) * 128], v_tok[:, j, h * DV:(h + 1) * DV], True, True,
                               ar(f"k_tok{jj}", f"v_tok{j}"), b2 + h // 2)
                        for half in range(2):
                            DVE("tensor_tensor", dict(out=Pst[:, half * 512:(half + 1) * 512], in0=ps[:, b2 + half, :],
                                                      in1=S[:, l, half * 512:(half + 1) * 512], op=ALU.add),
                                r=[("S", l, 2 * half), ("S", l, 2 * half + 1)] + ar(), w=aw(f"P{half}"), x=[BK(b2 + half)])
                        bput(b2, 2)
                        for h in range(H):
                            dcol = E_q[:, h, jc1 - 1:jc1]
                            ACT("activation", dict(out=S_bf[:, l, h * DV:(h + 1) * DV], in_=Pst[:, h * DV:(h + 1) * DV], func=AF.Identity, scale=dcol),
                                r=ar(f"E_q{h}", f"P{h // 2}"), w=[("S_bf", l, h)])
                        for h in range(H):
                            dcol = E_q[:, h, jc1 - 1:jc1]
                            DVE("tensor_scalar", dict(out=S[:, l, h * DV:(h + 1) * DV], in0=Pst[:, h * DV:(h + 1) * DV], scalar1=dcol, scalar2=None,
                                                      op0=ALU.mult),
                                r=ar(f"E_q{h}", f"P{h // 2}"), w=[("S", l, h)])
                        if j + 1 < GT:
                            stage_A(j + 1)
                        ov = ps[:, bo:bo + 2, :].rearrange("p b (c t) -> p (b c) t", c=4)
                        ACT("activation", dict(out=sq[:], in_=ov, func=AF.Square), r=ar(), w=aw("sq"), x=[BK(bo), BK(bo + 1)])
                        bss = bget()
                        for h in range(H):
                            for vc in range(2):
                                mm(ps[:, bss, h * 128:(h + 1) * 128], ones_b[:], sq[:, h * 2 + vc, :], vc == 0, vc == 1, ["ones_b"] + ar("sq"), bss)
                        ACT("activation", dict(out=lnv, in_=ps[:, bss, :], func=AF.Ln, bias=EPS, scale=1.0 / DV), r=ar(), w=aw("lnv"), x=[BK(bss)])
                        bput(bss)
                        ACT("activation", dict(out=rstd2[:, jj].rearrange("p h t -> p (h t)"), in_=lnv, func=AF.Exp, scale=-0.5), r=ar("lnv"), w=aw(f"rstd{jj}"))
                        if pend_norm[0] is not None:
                            pend_norm[0]()

                        def norm_dve(j=j, jj=jj, bo=bo, ov=ov, jc0=jc0, jc1=jc1):
                            for vc in range(2):
                                DVE("scalar_tensor_tensor", dict(out=t1[:, vc::2, :], in0=ov[:, vc::2, :], scalar=gnorm[:, l, vc:vc + 1], in1=rstd2[:, jj],
                                                                 op0=ALU.mult, op1=ALU.mult),
                                    r=["gnorm"] + ar(f"rstd{jj}"), w=aw("t1"), x=[BK(bo), BK(bo + 1)])
                            bput(bo, 2)
                            DVE("tensor_tensor", dict(out=og[:, :, jc0:jc1], in0=t1[:], in1=sgt[:, :, jc0:jc1], op=ALU.mult),
                                r=ar("t1", *[f"sg{fc}" for fc in range(8)]), w=[("og", j)])

                        pend_norm[0] = norm_dve
                        if has_s:
                            s0 = j * 4
                            b2n = samp_A(s0)
                            for s in range(s0, s0 + 4):
                                b2c = b2n
                                if s + 1 < s0 + 4:
                                    b2n = samp_A(s + 1)
                                samp_B(s, b2c)
                    pend_norm[0]()
                    pend_norm[0] = None
                    if has_s:
                        ovs = ps[:, bos, 0:NKC * NS].rearrange("p (c t) -> p c t", c=NKC)
                        ACT("activation", dict(out=sq_s[:], in_=ovs, func=AF.Square), r=ar(), w=aw("sq_s"), x=[BK(bos)])
                        bss = bget()
                        for h in range(H):
                            for vc in range(2):
                                mm(ps[:, bss, h * NS:(h + 1) * NS], ones_b[:], sq_s[:, h * 2 + vc, :], vc == 0, vc == 1, ["ones_b"] + ar("sq_s"), bss)
                        ACT("activation", dict(out=lnv_s, in_=ps[:, bss, 0:H * NS], func=AF.Ln, bias=EPS, scale=1.0 / DV), r=ar(), w=aw("lnv_s"), x=[BK(bss)])
                        bput(bss)
                        ACT("activation", dict(out=rstd_s[:].rearrange("p h t -> p (h t)"), in_=lnv_s, func=AF.Exp, scale=-0.5), r=ar("lnv_s"), w=aw("rstd_s"))
                        for vc in range(2):
                            DVE("scalar_tensor_tensor", dict(out=t1_s[:, vc::2, :], in0=ovs[:, vc::2, :], scalar=gnorm[:, l, vc:vc + 1], in1=rstd_s[:],
                                                                                 op0=ALU.mult, op1=ALU.mult),
                                r=["gnorm"] + ar("rstd_s"), w=aw("t1_s"), x=[BK(bos)])
                        bput(bos)
                        DVE("tensor_tensor", dict(out=og[:, :, BT:TW], in0=t1_s[:], in1=sgt[:, :, BT:TW], op=ALU.mult), r=ar("t1_s", "sg_s"), w=["og_s"])
                    if last_g:
                        DMA("sp", "dma_start", dict(out=gp[l].rearrange("h k v -> k h v"), in_=S[:, l, :].rearrange("p (h v) -> p h v", h=H)),
                            r=[("S", l, h_) for h_ in range(H)], w=[("gp", l)])

                    epoch_barrier()
                    DVE("tensor_copy", dict(out=pbuf[:, :, 0:2], in_=carry[:, l, :, :]), r=["carry"] + ar(), w=aw("pb0", "pb1", "pb2", "pb3"))
                    if has_s:
                        DMA("sp", "dma_start", dict(out=sc_tok, in_=scv[l].rearrange("s r d -> (s r) d")), r=[EP], w=aw("sc_tok"))
                        b = bget()
                        for c in range(4):
                            tp(ps[:, b, c * 2 * NS:(c + 1) * 2 * NS], sc_tok[:, c * 128:(c + 1) * 128], ident_f[0:2 * NS, 0:2 * NS], ["ident_f"] + ar("sc_tok"), b)
                        ACT("copy", dict(out=scT[:].rearrange("p c t -> p (c t)"), in_=ps[:, b, 0:4 * 2 * NS]), r=ar(), w=aw("scT"), x=[BK(b)])
                        bput(b)
                        DMA("sp", "dma_start", dict(out=cs[l, :, 0, :], in_=scv[l, :, 1, :]), w=[("cs0", l)])

                    def conv_mm(nm, post, post_s):
                        bs = bget() if has_s else None
                        for wi in range(2):
                            i, slot, wk = wtake((nm, g, l, wi))
                            for sub in range(2):
                                c = wi * 2 + sub
                                b = bget()
                                for kc in range(NKC):
                                    mm(ps[:, b, :], slot[:, kc, sub * 128:(sub + 1) * 128], uT[:, kc, 0:BT], kc == 0, kc == NKC - 1, [wk, ("uT", kc)], b)
                                post(c, b)
                                bput(b)
                                if has_s:
                                    for kc in range(NKC):
                                        mm(ps[:, bs, c * NS:(c + 1) * NS], slot[:, kc, sub * 128:(sub + 1) * 128], uT[:, kc, BT:TW],
                                           kc == 0, kc == NKC - 1, [wk, "uTs"], bs)
                            wrel(i)
                        if has_s:
                            post_s(ps[:, bs, 0:4 * NS].rearrange("p (c t) -> p c t", c=4), bs)
                            bput(bs)

                    conv_mm("cc",
                            lambda c, b: ACT("copy", dict(out=ccy[:, c, 0:BT], in_=ps[:, b, :]), r=ar(), w=aw(f"ccy{c}"), x=[BK(b)]),
                            lambda pv, bs: ACT("copy", dict(out=ccy[:, :, BT:TW], in_=pv), r=ar(), w=aw("ccy_s"), x=[BK(bs)]))

                    def post_ch(c, b):
                        DVE("tensor_tensor", dict(out=pbuf[:, c, 2:BT + 2], in0=ps[:, b, :], in1=ccy[:, c, 0:BT], op=ALU.mult),
                            r=ar(f"ccy{c}"), w=aw(f"pb{c}"), x=[BK(b)])
                        ACT("activation", dict(out=tmp1, in_=pbuf[:, c, 0:BT], func=AF.Identity, scale=convw[:, l, c, 0:1]),
                            r=["convw"] + ar(f"pb{c}"), w=aw("tmp1"))
                        DVE("scalar_tensor_tensor", dict(out=tmp2, in0=pbuf[:, c, 1:BT + 1], scalar=convw[:, l, c, 1:2], in1=tmp1, op0=ALU.mult, op1=ALU.add),
                            r=["convw"] + ar(f"pb{c}", "tmp1"), w=aw("tmp2"))
                        DVE("scalar_tensor_tensor", dict(out=ccy[:, c, 0:BT], in0=pbuf[:, c, 2:BT + 2], scalar=convw[:, l, c, 2:3], in1=tmp2, op0=ALU.mult, op1=ALU.add),
                            r=["convw"] + ar(f"pb{c}", "tmp2"), w=aw(f"ccy{c}"))

                    def post_ch_s(pv, bs):
                        DVE("tensor_tensor", dict(out=p_s[:], in0=pv, in1=ccy[:, :, BT:TW], op=ALU.mult), r=ar("ccy_s"), w=aw("p_s"), x=[BK(bs)])
                        for c in range(4):
                            ACT("activation", dict(out=tcs1, in_=scT[:, c, 0:2 * NS:2], func=AF.Identity, scale=convw[:, l, c, 0:1]),
                                r=["convw"] + ar("scT"), w=aw("tcs1"))
                            DVE("scalar_tensor_tensor", dict(out=tcs2, in0=scT[:, c, 1:2 * NS:2], scalar=convw[:, l, c, 1:2], in1=tcs1, op0=ALU.mult, op1=ALU.add),
                                r=["convw"] + ar("scT", "tcs1"), w=aw("tcs2"))
                            DVE("scalar_tensor_tensor", dict(out=ccy[:, c, BT:TW], in0=p_s[:, c, :], scalar=convw[:, l, c, 2:3], in1=tcs2, op0=ALU.mult, op1=ALU.add),
                                r=["convw"] + ar("p_s", "tcs2"), w=aw("ccy_s"))

                    conv_mm("ch", post_ch, post_ch_s)
                    DVE("tensor_copy", dict(out=carry[:, l, :, :], in_=pbuf[:, :, BT:BT + 2]), r=ar("pb0", "pb1", "pb2", "pb3"), w=["carry"])
                    if last_g:
                        b = bget()
                        for c in range(4):
                            tp(ps[0:2, b, c * 128:(c + 1) * 128], carry[:, l, c, :], ident_f[:], ["ident_f", "carry"], b)
                        ACT("copy", dict(out=cst, in_=ps[0:2, b, :]), r=ar(), w=aw("cst"), x=[BK(b)])
                        bput(b)
                        DMA("sp", "dma_start", dict(out=cp[l], in_=cst), r=ar("cst"), w=[("cp", l)])
                    if has_s:
                        b = bget()
                        for c in range(4):
                            tp(ps[0:NS, b, c * 128:(c + 1) * 128], p_s[:, c, :], ident_f[:], ["ident_f"] + ar("p_s"), b)
                        ACT("copy", dict(out=pst, in_=ps[0:NS, b, :]), r=ar(), w=aw("pst"), x=[BK(b)])
                        bput(b)
                        DMA("sp", "dma_start", dict(out=cs[l, :, 1, :], in_=pst), r=ar("pst"), w=[("cs1", l)])
                    conv_mm("cb",
                            lambda c, b: DVE("tensor_tensor", dict(out=ccy[:, c, 0:BT], in0=ps[:, b, :], in1=ccy[:, c, 0:BT], op=ALU.mult),
                                             r=ar(f"ccy{c}"), w=aw(f"ccy{c}"), x=[BK(b)]),
                            lambda pv, bs: DVE("tensor_tensor", dict(out=ccy[:, :, BT:TW], in0=pv, in1=ccy[:, :, BT:TW], op=ALU.mult),
                                               r=ar("ccy_s"), w=aw("ccy_s"), x=[BK(bs)]))

                    def post_gc(c, b):
                        ACT("activation", dict(out=tmp3, in_=ps[:, b, :], func=AF.Silu), r=ar(), w=aw("tmp3"), x=[BK(b)])
                        DVE("tensor_tensor", dict(out=yc[:, c, 0:BT], in0=ccy[:, c, 0:BT], in1=tmp3, op=ALU.mult), r=ar(f"ccy{c}", "tmp3"), w=[("yc", c)])

                    def post_gc_s(pv, bs):
                        ACT("activation", dict(out=tmp3s[:], in_=pv, func=AF.Silu), r=ar(), w=aw("tmp3s"), x=[BK(bs)])
                        DVE("tensor_tensor", dict(out=yc[:, :, BT:TW], in0=ccy[:, :, BT:TW], in1=tmp3s[:], op=ALU.mult), r=ar("ccy_s", "tmp3s"), w=["yc_s"])

                    conv_mm("gc", post_gc, post_gc_s)

                    ogr = [("og", j) for j in range(GT)]
                    ycr = [("yc", c) for c in range(4)]
                    for cpi in range(4):
                        ia, sa, ka = wtake(("pa", g, l, cpi))
                        ib, sbb, kb = wtake(("pb", g, l, cpi))
                        ig, sgw, kg = wtake(("mg", g, l, cpi))
                        ic, scw, kcw = wtake(("mc", g, l, cpi))
                        for sub in range(2):
                            c = cpi * 2 + sub
                            cs0, cs1 = sub * 128, (sub + 1) * 128
                            bA = bget()
                            for kc in range(NKC):
                                mm(ps[:, bA, :], sa[:, kc, cs0:cs1], og[:, kc, 0:BT], kc == 0, kc == NKC - 1, [ka] + ogr, bA)
                            bB = bget()
                            for kc in range(4):
                                mm(ps[:, bB, :], sbb[:, kc, cs0:cs1], yc[:, kc, 0:BT], kc == 0, kc == 3, [kb, ("yc", kc)], bB)
                            bG = bget()
                            for kc in range(NKC):
                                mm(ps[:, bG, :], sgw[:, kc, cs0:cs1], uT[:, kc, 0:BT], kc == 0, kc == NKC - 1, [kg, ("uT", kc)], bG)
                            bC = bget()
                            for kc in range(NKC):
                                mm(ps[:, bC, :], scw[:, kc, cs0:cs1], uT[:, kc, 0:BT], kc == 0, kc == NKC - 1, [kcw, ("uT", kc)], bC)
                            ACT("activation", dict(out=sgm, in_=ps[:, bG, :], func=AF.Sigmoid), r=ar(), w=aw("sgm"), x=[BK(bG)])
                            bput(bG)
                            ACT("activation", dict(out=scm, in_=ps[:, bC, :], func=AF.Sigmoid), r=ar(), w=aw("scm"), x=[BK(bC)])
                            bput(bC)
                            DVE("tensor_tensor", dict(out=tm1, in0=ps[:, bA, :], in1=sgm, op=ALU.mult), r=ar("sgm"), w=aw("tm1"), x=[BK(bA)])
                            bput(bA)
                            DVE("tensor_tensor", dict(out=tm2, in0=ps[:, bB, :], in1=scm, op=ALU.mult), r=ar("scm"), w=aw("tm2"), x=[BK(bB)])
                            bput(bB)
                            DVE("tensor_tensor", dict(out=mg[:, c, 0:BT], in0=tm1, in1=tm2, op=ALU.add), r=ar("tm1", "tm2"), w=aw(f"mg{c}"))
                            if has_s:
                                bS = bget()
                                for kc in range(NKC):
                                    mm(ps[:, bS, 0:NS], sa[:, kc, cs0:cs1], og[:, kc, BT:TW], kc == 0, kc == NKC - 1, [ka, "og_s"], bS)
                                for kc in range(4):
                                    mm(ps[:, bS, NS:2 * NS], sbb[:, kc, cs0:cs1], yc[:, kc, BT:TW], kc == 0, kc == 3, [kb, "yc_s"], bS)
                                for kc in range(NKC):
                                    mm(ps[:, bS, 2 * NS:3 * NS], sgw[:, kc, cs0:cs1], uT[:, kc, BT:TW], kc == 0, kc == NKC - 1, [kg, "uTs"], bS)
                                for kc in range(NKC):
                                    mm(ps[:, bS, 3 * NS:4 * NS], scw[:, kc, cs0:cs1], uT[:, kc, BT:TW], kc == 0, kc == NKC - 1, [kcw, "uTs"], bS)
                                ACT("activation", dict(out=sg_s, in_=ps[:, bS, 2 * NS:4 * NS], func=AF.Sigmoid), r=ar(), w=aw("sg_s2"), x=[BK(bS)])
                                DVE("tensor_tensor", dict(out=t_s, in0=ps[:, bS, 0:2 * NS], in1=sg_s, op=ALU.mult), r=ar("sg_s2"), w=aw("t_s"), x=[BK(bS)])
                                bput(bS)
                                DVE("tensor_tensor", dict(out=mg[:, c, BT:TW], in0=t_s[:, 0:NS], in1=t_s[:, NS:2 * NS], op=ALU.add), r=ar("t_s"), w=aw("mg_s"))
                        wrel(ia)
                        wrel(ib)
                        wrel(ig)
                        wrel(ic)

                    so = [wtake(("o", g, l, wi)) for wi in range(4)]
                    mgr = ar(*[f"mg{c}" for c in range(NKC)])

                    def o_tail(xv, gv, tov, stv, mvv, lg, lb, bo, npart, xkey, mkeys):
                        if gv is not None:
                            DVE("tensor_tensor", dict(out=tov.rearrange("p (b n) -> p b n", b=2), in0=ps[0:npart, bo:bo + 2, :],
                                                      in1=gv.rearrange("p (b n) -> p b n", b=2), op=ALU.mult),
                                r=["gate_bc", "gate_s"] + ar(), w=aw(mkeys + "to"), x=[BK(bo), BK(bo + 1)])
                            bput(bo, 2)
                            DVE("scalar_tensor_tensor", dict(out=xv, in0=xv, scalar=ALPHA, in1=tov, op0=ALU.mult, op1=ALU.add),
                                r=[xkey] + ar(mkeys + "to"), w=[xkey])
                        else:
                            DVE("scalar_tensor_tensor", dict(out=xv.rearrange("p (b n) -> p b n", b=2), in0=xv.rearrange("p (b n) -> p b n", b=2), scalar=ALPHA,
                                                             in1=ps[0:npart, bo:bo + 2, :], op0=ALU.mult, op1=ALU.add),
                                r=[xkey], w=[xkey], x=[BK(bo), BK(bo + 1)])
                            bput(bo, 2)
                        for hh in range(2):
                            DVE("bn_stats", dict(out=stv[:, hh, :], in_=xv[:, hh * 512:(hh + 1) * 512]), r=[xkey], w=[mkeys + "st"])
                        DVE("bn_aggr", dict(out=mvv[:, 0:2], in_=stv.rearrange("p a b -> p (a b)")), r=[mkeys + "st"], w=[mkeys + "mv"])
                        ACT("activation", dict(out=mvv[:, 2:3], in_=mvv[:, 1:2], func=AF.Ln, bias=EPS), r=[mkeys + "mv"], w=[mkeys + "mv2"])
                        ACT("activation", dict(out=mvv[:, 3:4], in_=mvv[:, 2:3], func=AF.Exp, scale=-0.5), r=[mkeys + "mv2"], w=[mkeys + "mv3"])
                        DVE("tensor_scalar", dict(out=mvv[:, 4:5], in0=mvv[:, 0:1], scalar1=mvv[:, 3:4], scalar2=-1.0, op0=ALU.mult, op1=ALU.mult),
                            r=[mkeys + "mv", mkeys + "mv3"], w=[mkeys + "mv4"])
                        ACT("activation", dict(out=xv, in_=xv, func=AF.Identity, bias=mvv[:, 4:5], scale=mvv[:, 3:4]),
                            r=[xkey, mkeys + "mv3", mkeys + "mv4"], w=[xkey])
                        DVE("tensor_tensor", dict(out=xv, in0=xv, in1=lg, op=ALU.mult), r=[xkey, "lng"], w=[xkey])
                        DVE("tensor_tensor", dict(out=xv, in0=xv, in1=lb, op=ALU.add), r=[xkey, "lnb"], w=[xkey])

                    if has_s:
                        bo = bget2()
                        for wi in range(4):
                            for kc in range(NKC):
                                mm(ps[0:NS, bo + wi // 2, (wi % 2) * WC:(wi % 2 + 1) * WC], mg[:, kc, BT:TW], so[wi][1][:, kc, :],
                                   kc == 0, kc == NKC - 1, [so[wi][2]] + ar("mg_s"), bo + wi // 2)
                        o_tail(xs_sb[:], gate_s[:, l, :], to_s, st_s[:], mv_s[:], lng_bc[0:NS, :], lnb_bc[0:NS, :], bo, NS, "xs", "s")
                    for j in range(GT):
                        bo = bget2()
                        for wi in range(4):
                            for kc in range(NKC):
                                mm(ps[:, bo + wi // 2, (wi % 2) * WC:(wi % 2 + 1) * WC], mg[:, kc, j * 128:(j + 1) * 128], so[wi][1][:, kc, :],
                                   kc == 0, kc == NKC - 1, [so[wi][2]] + mgr, bo + wi // 2)
                        o_tail(x_sb[:, j, :], gate_bc[:, l, :], to, st[:], mv[:], lng_bc[:], lnb_bc[:], bo, 128, ("x", j), "p")
                        if l == nl_run - 1:
                            DMA("sp", "dma_start", dict(out=yp[g * BT + j * 128: g * BT + (j + 1) * 128, :], in_=x_sb[:, j, :]),
                                r=[("x", j)], w=[("yp", g, j)])
                    for wi in range(4):
                        wrel(so[wi][0])
                    if l == nl_run - 1:
                        pass
                        if has_s:
                            DMA("sp", "dma_start", dict(out=ys, in_=xs_sb[:]), r=["xs"], w=["ys"])

            assert wstate["taken"] == len(wlist), (wstate, len(wlist))
            return wlist

        wl = emit_all(Prog(), True, None)
        P = Prog()
        emit_all(P, False, wl)
        P.finalize()

        @block.tensor
        def _(e):
            P.run_engine("pe", e, sems, dma_sems)

        @block.scalar
        def _(e):
            P.run_engine("act", e, sems, dma_sems)

        @block.vector
        def _(e):
            P.run_engine("dve", e, sems, dma_sems)

        @block.gpsimd
        def _(e):
            P.run_engine("pool", e, sems, dma_sems)

        @block.sync
        def _(e):
            P.run_engine("sp", e, sems, dma_sems)
            P.final_wait(e, dma_sems)

    return nc


_NC_CACHE = {}


def kernel(x_prompt, x_sample, c_prompt, c_sample, state_gla, state_conv, w_ada, b_ada, w_in, w_a2, b_a,
           gla_norm_g, conv_w, w_pa, w_pb, w_o, ln_g, ln_b):
    f = lambda a: np.ascontiguousarray(np.asarray(a, dtype=np.float32))
    x_prompt, x_sample, c_prompt, c_sample = f(x_prompt), f(x_sample), f(c_prompt), f(c_sample)
    state_gla, state_conv = f(state_gla), f(state_conv)
    shared = dict(w_ada=f(w_ada), b_ada=f(b_ada), w_in=f(w_in), w_a2=f(w_a2), b_a=f(b_a), gng=f(gla_norm_g),
                  conv_w=f(conv_w), w_pa=f(w_pa), w_pb=f(w_pb), w_o=f(w_o), ln_g=f(ln_g), ln_b=f(ln_b))
    ncores = 8
    in_maps = []
    for i in range(ncores):
        sl = slice(i * NS, (i + 1) * NS)
        m = dict(shared)
        m["xp"] = x_prompt[i]
        m["xs"] = np.ascontiguousarray(x_sample[sl, 0, :])
        m["c17"] = np.ascontiguousarray(np.concatenate([c_prompt[i:i + 1], c_sample[sl]], axis=0))
        m["sgl"] = np.ascontiguousarray(state_gla[:, sl])
        m["scv"] = np.ascontiguousarray(state_conv[:, sl])
        in_maps.append(m)
    if "nc" not in _NC_CACHE:
        _NC_CACHE["nc"] = build_program()
    res = run_bass_kernel_spmd(_NC_CACHE["nc"], in_maps, core_ids=list(range(ncores)))
    R = res.results
    y_prompt = np.stack([R[i]["yp"] for i in range(ncores)], axis=0)
    y_sample = np.concatenate([R[i]["ys"] for i in range(ncores)], axis=0)[:, None, :]
    gla_p = np.stack([R[i]["gp"] for i in range(ncores)], axis=1)
    conv_p = np.stack([R[i]["cp"] for i in range(ncores)], axis=1)
    gla_s = np.concatenate([R[i]["gs"] for i in range(ncores)], axis=1)
    conv_s = np.concatenate([R[i]["cs"] for i in range(ncores)], axis=1)
    return (y_prompt.astype(np.float32), y_sample.astype(np.float32), gla_p.astype(np.float32),
            conv_p.astype(np.float32), gla_s.astype(np.float32), conv_s.astype(np.float32))
```

```python
import os
import numpy as np
from contextlib import ExitStack
import concourse.bass as bass
import concourse.mybir as mybir
from concourse.bass_utils import run_bass_kernel_spmd

F32 = mybir.dt.float32
BF16 = mybir.dt.bfloat16
AF = mybir.ActivationFunctionType
ALU = mybir.AluOpType

ENGS = ("pe", "act", "dve", "pool", "sp")

D = 1024
SEQ = 2048
DEPTH = 2
NS = 16
H = 4
DK = 128
DV = 256
NKC = 8
BT = 512
GT = 4
NG = SEQ // BT
TW = BT + NS
SG = 0
ALPHA = (2 * DEPTH) ** 0.25
EPS = 1e-5
QSCALE = DK ** -0.5
O_Q, O_K, O_V, O_GG, O_ALR, O_CB, O_CC, O_CH, O_GC, O_MG, O_MC = 0, 512, 1024, 2048, 3072, 3088, 3600, 4112, 4624, 5136, 6160
NW = 8
WC = 256


class Ins:
    __slots__ = ("eng", "fn", "deps", "dma", "sig", "semid", "semval", "pos")

    def __init__(self, eng, fn, dma, pos):
        self.eng = eng
        self.fn = fn
        self.deps = []
        self.dma = dma
        self.sig = False
        self.semid = None
        self.semval = None
        self.pos = pos


class Prog:
    def __init__(self, n_dma_sems=8):
        self.lists = {e: [] for e in ENGS}
        self.last_w = {}
        self.rd_eng = {}
        self.rd_dma = {}
        self.n_dma_sems = n_dma_sems

    def add(self, eng, fn, reads=(), writes=(), excl=(), dma=False):
        I = Ins(eng, fn, dma, len(self.lists[eng]))
        cand = []
        writes = list(writes) + list(excl)
        for k in reads:
            w = self.last_w.get(k)
            if w is not None:
                cand.append((w, True))
        for k in writes:
            w = self.last_w.get(k)
            if w is not None:
                cand.append((w, False))
            for r in self.rd_eng.get(k, {}).values():
                cand.append((r, False))
            for r in self.rd_dma.get(k, ()):
                cand.append((r, False))
        for k in writes:
            self.last_w[k] = I
            self.rd_eng[k] = {}
            self.rd_dma[k] = []
        for k in reads:
            if dma:
                self.rd_dma.setdefault(k, []).append(I)
            else:
                self.rd_eng.setdefault(k, {})[eng] = I
        best = {}
        seen = set()
        for d, raw in cand:
            if d is I:
                continue
            if d.dma:
                if id(d) not in seen:
                    seen.add(id(d))
                    I.deps.append(d)
                continue
            if d.eng == eng and not dma:
                if eng == "pe":
                    continue
            b = best.get(d.eng)
            if b is None or d.pos > b.pos:
                best[d.eng] = d
        for d in best.values():
            d.sig = True
            I.deps.append(d)
        self.lists[eng].append(I)
        return I

    def finalize(self):
        for e in ENGS:
            cnt = 0
            dcnt = {}
            k = 0
            for I in self.lists[e]:
                if I.dma:
                    I.semid = k % self.n_dma_sems
                    k += 1
                    dcnt[I.semid] = dcnt.get(I.semid, 0) + 16
                    I.semval = dcnt[I.semid]
                elif I.sig:
                    cnt += 1
                    I.semval = cnt

    def run_engine(self, e, engobj, sems, dma_sems):
        waited = {}
        for I in self.lists[e]:
            need = {}
            for d in I.deps:
                if d.dma:
                    key = ("d", d.eng, d.semid)
                    s = dma_sems[d.eng][d.semid]
                else:
                    key = ("e", d.eng)
                    s = sems[d.eng]
                if need.get(key, (None, 0))[1] < d.semval:
                    need[key] = (s, d.semval)
            if I.dma and I.semval > 16:
                key = ("d", e, I.semid)
                if need.get(key, (None, 0))[1] < I.semval - 16:
                    need[key] = (dma_sems[e][I.semid], I.semval - 16)
            for key, (s, v) in need.items():
                if waited.get(key, 0) < v:
                    engobj.wait_ge(s, v)
                    waited[key] = v
            ins = getattr(engobj, I.fn[0])(**I.fn[1])
            if I.dma:
                ins.then_inc(dma_sems[e][I.semid], 16)
            elif I.sig:
                ins.then_inc(sems[e], 1)

    def final_wait(self, engobj, dma_sems):
        for e in ENGS:
            last = {}
            for I in self.lists[e]:
                if I.dma:
                    last[I.semid] = I.semval
            for sid, v in last.items():
                engobj.wait_ge(dma_sems[e][sid], v)


def build_program(ng_run=NG, nl_run=DEPTH):
    nc = bass.Bass("TRN2", target_bir_lowering=False)

    def din(name, shape):
        return nc.dram_tensor(name, list(shape), F32, kind="ExternalInput").ap()

    def dout(name, shape):
        return nc.dram_tensor(name, list(shape), F32, kind="ExternalOutput").ap()

    xp = din("xp", [SEQ, D])
    xs = din("xs", [NS, D])
    c17d = din("c17", [NS + 1, D])
    sgl = din("sgl", [DEPTH, NS, H, DK, DV])
    scv = din("scv", [DEPTH, NS, 2, 512])
    w_ada = din("w_ada", [DEPTH, D, 3 * D])
    b_ada = din("b_ada", [DEPTH, 3 * D])
    w_in = din("w_in", [DEPTH, D, 7184])
    w_a2 = din("w_a2", [DEPTH, 16, 512])
    b_a = din("b_a", [DEPTH, 512])
    gng = din("gng", [DEPTH, DV])
    conv_w = din("conv_w", [DEPTH, 3, 512])
    w_pa = din("w_pa", [DEPTH, D, D])
    w_pb = din("w_pb", [DEPTH, 512, D])
    w_o = din("w_o", [DEPTH, D, D])
    ln_g = din("ln_g", [DEPTH, D])
    ln_b = din("ln_b", [DEPTH, D])
    yp = dout("yp", [SEQ, D])
    ys = dout("ys", [NS, D])
    gp = dout("gp", [DEPTH, H, DK, DV])
    cp = dout("cp", [DEPTH, 2, 512])
    gs = dout("gs", [DEPTH, NS, H, DK, DV])
    cs = dout("cs", [DEPTH, NS, 2, 512])

    with ExitStack() as es:
        def sb(name, shape, dt):
            return es.enter_context(nc.sbuf_tensor(name, list(shape), dt))

        x_sb = sb("x_sb", [128, GT, D], F32)
        xs_sb = sb("xs_sb", [NS, D], F32)
        uT = sb("uT", [128, NKC, TW], BF16)
        og = sb("og", [128, NKC, TW], BF16)
        yc = sb("yc", [128, 4, TW], BF16)
        wring = sb("wring", [128, NW, NKC, WC], BF16)
        gate_bc = sb("gate_bc", [128, DEPTH, D], F32)
        gate_s = sb("gate_s", [NS, DEPTH, D], F32)
        lng_bc = sb("lng_bc", [128, D], F32)
        lnb_bc = sb("lnb_bc", [128, D], F32)
        S = sb("S", [128, DEPTH, H * DV], F32)
        S_bf = sb("S_bf", [128, DEPTH, H * DV], BF16)
        Sbuf = sb("Sbuf", [128, 3, H * DV], F32)
        sbf = sb("sbf", [128, 2, H * DV], BF16)
        ident_f = sb("ident_f", [128, 128], F32)
        ident_b = sb("ident_b", [128, 128], BF16)
        tri = sb("tri", [128, 128], F32)
        ones_b = sb("ones_b", [128, 128], BF16)
        alr_aug = sb("alr_aug", [17, TW], F32)
        w_a2aug = sb("w_a2aug", [17, DEPTH, 512], F32)
        gnorm = sb("gnorm", [128, DEPTH, 2], F32)
        convw = sb("convw", [128, DEPTH, 4, 3], F32)
        ada_sb = sb("ada_sb", [128, DEPTH, 16, 17], F32)
        b_adaT = sb("b_adaT", [128, DEPTH, 24], F32)
        carry = sb("carry", [128, DEPTH, 4, 2], F32)
        cT = sb("cT", [128, NKC, 17], BF16)
        tmp_s = sb("tmp_s", [128, NKC, NS], F32)
        st = sb("st", [128, 2, 6], F32)
        mv = sb("mv", [128, 8], F32)
        st_s = sb("st_s", [NS, 2, 6], F32)
        mv_s = sb("mv_s", [NS, 8], F32)
        dummy = sb("dummy_t", [128, 8], F32)
        ARENA_F = 16384 + 256 + 1024 + 256 + 512
        arena = sb("arena", [128, ARENA_F], F32)

        ps = es.enter_context(nc.psum_tensor("ps", [128, 8, 512], F32))

        sems = {e: es.enter_context(nc.semaphore(f"s_{e}")) for e in ENGS}
        dma_sems = {e: [es.enter_context(nc.semaphore(f"d_{e}{i}")) for i in range(8)]
                    for e in ("sp", "pool")}
        block = es.enter_context(nc.Block())

        class Arena:
            def __init__(self):
                self.off = 0

            def f32(self, n, parts=128):
                a = arena[0:parts, self.off:self.off + n]
                self.off += n
                assert self.off <= ARENA_F, self.off
                return a

            def bf16(self, n, parts=128):
                nf = (n + 1) // 2
                a = arena[0:parts, self.off:self.off + nf].bitcast(BF16)
                self.off += nf
                assert self.off <= ARENA_F, self.off
                return a

        def v3(ap, c):
            return ap.rearrange("p (c t) -> p c t", c=c)

        A = Arena()
        E_q = v3(A.f32(4 * BT), 4)
        E_k = v3(A.f32(4 * BT), 4)
        q_in = v3(A.bf16(4 * BT), 4)
        k_in = v3(A.bf16(4 * BT), 4)
        sgt = v3(A.bf16(8 * TW), 8)
        v_tok = v3(A.bf16(GT * 1024), GT)
        k_tok = v3(A.bf16(2 * 512), 2)
        _o = A.off
        e1 = A.f32(512)
        A.off = _o
        ktok_s = A.f32(512, NS)
        sp = v3(A.f32(2 * 512), 2)
        v_s = A.bf16(1024, NS)
        attm = v3(A.bf16(2 * 512), 2)
        sq = v3(A.bf16(8 * 128), 8)
        lnv = A.f32(512)
        rstd2 = A.f32(2 * 512).rearrange("p (b h t) -> p b h t", b=2, h=4)
        t1 = v3(A.f32(1024), 8)
        Pst = A.f32(1024)
        q_s = v3(A.bf16(4 * NS), 4)
        a_s = v3(A.f32(4 * NS), 4)
        km2 = v3(A.bf16(2 * 512, NS), 2)
        sq_s = v3(A.bf16(8 * NS), 8)
        lnv_s = A.f32(4 * NS)
        rstd_s = v3(A.f32(4 * NS), 4)
        t1_s = v3(A.f32(8 * NS), 8)
        g_end = A.off
        A = Arena()
        c17 = A.f32(D, NS + 1)
        cTb = v3(A.bf16(NKC * 128), NKC)
        bgate = A.f32(D)
        A = Arena()
        ccy = v3(A.f32(4 * TW), 4)
        pbuf = v3(A.f32(4 * (BT + 2)), 4)
        tmp1 = A.f32(512)
        tmp2 = A.f32(512)
        tmp3 = A.f32(512)
        p_s = v3(A.f32(4 * NS), 4)
        scT = v3(A.f32(4 * 2 * NS), 4)
        sc_tok = A.f32(512, 2 * NS)
        tcs1 = A.f32(NS)
        tcs2 = A.f32(NS)
        tmp3s = v3(A.f32(4 * NS), 4)
        cst = A.f32(512, 2)
        pst = A.f32(512, NS)
        sgm = A.f32(512)
        scm = A.f32(512)
        tm1 = A.f32(512)
        tm2 = A.f32(512)
        sg_s = A.f32(2 * NS)
        t_s = A.f32(2 * NS)
        mg = v3(A.bf16(NKC * TW), NKC)
        to = A.f32(D)
        to_s = A.f32(D, NS)
        cmo_end = A.off

        def emit_all(P, record, wlist_in):
            def PE(meth, kw, r=(), x=()):
                return P.add("pe", (meth, kw), reads=r, excl=x)

            def ACT(meth, kw, r=(), w=(), x=()):
                return P.add("act", (meth, kw), reads=r, writes=w, excl=x)

            def DVE(meth, kw, r=(), w=(), x=()):
                return P.add("dve", (meth, kw), reads=r, writes=w, excl=x)

            def POOL(meth, kw, r=(), w=()):
                return P.add("pool", (meth, kw), reads=r, writes=w)

            def DMA(q, meth, kw, r=(), w=()):
                return P.add(q, (meth, kw), reads=r, writes=w, dma=True)

            EP = "EPOCH"

            def ar(*keys):
                return [EP] + ["A:" + k for k in keys]

            def aw(*keys):
                return ["A:" + k for k in keys]

            def epoch_barrier():
                DVE("memset", dict(ap=dummy[:, 0:1], constant=0.0), w=[EP])

            free_banks = list(range(8))

            def bget():
                assert free_banks, "PSUM banks exhausted"
                return free_banks.pop(0)

            def bget2():
                for i in range(len(free_banks)):
                    b = free_banks[i]
                    if b % 2 == 0 and (b + 1) in free_banks:
                        free_banks.remove(b)
                        free_banks.remove(b + 1)
                        return b
                raise AssertionError("no PSUM bank pair free")

            def bput(b, n=1):
                for i in range(n):
                    free_banks.append(b + i)

            def BK(b):
                return ("B", b)

            def mm(out, lhsT, rhs, start, stop, r, b):
                PE("matmul", dict(out=out, lhsT=lhsT, rhs=rhs, start=start, stop=stop), r=r, x=[BK(b)])

            def tp(out, in_, ident, r, b):
                PE("transpose", dict(out=out, in_=in_, identity=ident), r=r, x=[BK(b)])

            wlist = [] if record else wlist_in
            OFFS = {"q": O_Q, "k": O_K, "v": O_V, "gg": O_GG, "cc": O_CC, "ch": O_CH, "cb": O_CB, "gc": O_GC, "mg": O_MG, "mc": O_MC}

            def wspec(tag):
                nm = tag[0]
                if nm == "ada":
                    return w_ada[tag[1], :, tag[2] * WC:(tag[2] + 1) * WC], NKC, WC
                if nm == "alr":
                    return w_in[tag[2], :, O_ALR:O_ALR + 16], NKC, 16
                l, i = tag[2], tag[3]
                if nm in OFFS:
                    return w_in[l, :, OFFS[nm] + i * WC: OFFS[nm] + (i + 1) * WC], NKC, WC
                if nm == "pa":
                    return w_pa[l, :, i * WC:(i + 1) * WC], NKC, WC
                if nm == "pb":
                    return w_pb[l, :, i * WC:(i + 1) * WC], 4, WC
                if nm == "o":
                    return w_o[l, :, i * WC:(i + 1) * WC], NKC, WC
                raise KeyError(tag)

            wstate = {"issued": 0, "taken": 0}
            released = [False] * (100000 if record else len(wlist))

            def wpump():
                while wstate["issued"] < len(wlist):
                    j = wstate["issued"]
                    if j >= NW and not released[j - NW]:
                        break
                    src, nkc, ncols = wspec(wlist[j])
                    slot = j % NW
                    dst = wring[:, slot, 0:nkc, 0:ncols]
                    srcv = src.rearrange("(kc p) c -> p kc c", p=128)
                    DMA("pool", "dma_start", dict(out=dst, in_=srcv), w=[("W", slot)])
                    wstate["issued"] += 1

            def wtake(tag):
                i = wstate["taken"]
                if record:
                    wlist.append(tag)
                    wstate["taken"] += 1
                    return i, wring[:, i % NW], ("W", i % NW)
                assert wlist[i] == tag, (wlist[i], tag)
                wpump()
                assert wstate["issued"] > i, "weight ring deadlock"
                wstate["taken"] += 1
                return i, wring[:, i % NW], ("W", i % NW)

            def wrel(i):
                released[i] = True
                if not record:
                    wpump()

            POOL("memset", dict(ap=ident_f[:], constant=0.0), w=["ident_f"])
            POOL("affine_select", dict(out=ident_f[:], in_=ident_f[:], pattern=[[-1, 128]], compare_op=ALU.not_equal,
                                           fill=1.0, base=0, channel_multiplier=1), r=["ident_f"], w=["ident_f"])
            POOL("tensor_copy", dict(out=ident_b[:], in_=ident_f[:]), r=["ident_f"], w=["ident_b"])
            POOL("memset", dict(ap=tri[:], constant=1.0), w=["tri"])
            POOL("affine_select", dict(out=tri[:], in_=tri[:], pattern=[[1, 128]], compare_op=ALU.is_ge,
                                           fill=0.0, base=0, channel_multiplier=-1), r=["tri"], w=["tri"])
            POOL("memset", dict(ap=ones_b[:], constant=1.0), w=["ones_b"])
            POOL("memset", dict(ap=alr_aug[:], constant=1.0), w=["alr_aug"])
            POOL("memset", dict(ap=carry[:], constant=0.0), w=["carry"])
            POOL("memset", dict(ap=S[:], constant=0.0), w=[("S", l_, h_) for l_ in range(DEPTH) for h_ in range(H)])
            POOL("memset", dict(ap=S_bf[:], constant=0.0), w=[("S_bf", l_, h_) for l_ in range(DEPTH) for h_ in range(H)])

            DMA("sp", "dma_start", dict(out=c17, in_=c17d), w=aw("c17"))
            DMA("sp", "dma_start", dict(out=xs_sb[:], in_=xs), w=["xs"])
            for l in range(DEPTH):
                DMA("sp", "dma_start", dict(out=w_a2aug[0:16, l, :], in_=w_a2[l]), w=["w_a2aug"])
                DMA("sp", "dma_start", dict(out=w_a2aug[16:17, l, :], in_=b_a[l:l + 1, :]), w=["w_a2aug"])
                DMA("sp", "dma_start", dict(out=gnorm[:, l, :], in_=gng[l].rearrange("(c p) -> p c", p=128),
                                                     allow_slow_non_contiguous=True), w=["gnorm"])
                for ti in range(3):
                    DMA("sp", "dma_start", dict(out=convw[:, l, :, ti], in_=conv_w[l, ti].rearrange("(c p) -> p c", p=128),
                                                allow_slow_non_contiguous=True), w=["convw"])
                DMA("sp", "dma_start", dict(out=b_adaT[:, l, :], in_=b_ada[l].rearrange("(f p) -> p f", p=128),
                                                     allow_slow_non_contiguous=True), w=["b_adaT"])

            b = bget()
            for c in range(NKC):
                tp(ps[:, b, c * 17:(c + 1) * 17], c17[:, c * 128:(c + 1) * 128], ident_f[0:17, 0:17], ["ident_f"] + ar("c17"), b)
            ACT("copy", dict(out=cT[:].rearrange("p c t -> p (c t)"), in_=ps[:, b, 0:NKC * 17]), w=["cT"], x=[BK(b)])
            bput(b)
            for c in range(NKC):
                DVE("tensor_scalar", dict(out=cTb[:, c, :], in0=ones_b[:], scalar1=cT[:, c, 0:1], scalar2=None, op0=ALU.mult),
                    r=["ones_b", "cT"] + ar(), w=aw("cTb"))
            for l in range(DEPTH):
                DVE("tensor_scalar", dict(out=b_adaT[:, l, 8:16], in0=b_adaT[:, l, 8:16], scalar1=1.0, scalar2=None, op0=ALU.add),
                    r=["b_adaT"], w=["b_adaT"])
                DMA("sp", "dma_start", dict(out=bgate, in_=b_ada[l, 2 * D:3 * D].partition_broadcast(128)), w=aw("bgate"))
                b = bget()
                for wi in range(8):
                    i, slot, wk = wtake(("ada", l, wi))
                    for sub in range(2):
                        fc = wi * 2 + sub
                        for kc in range(NKC):
                            mm(ps[:, b, fc * 17:(fc + 1) * 17], slot[:, kc, sub * 128:(sub + 1) * 128], cT[:, kc, :],
                               kc == 0, kc == NKC - 1, [wk, "cT"], b)
                    wrel(i)
                DVE("tensor_tensor", dict(out=ada_sb[:, l, :, :], in0=ps[:, b, 0:16 * 17].rearrange("p (f t) -> p f t", f=16),
                                                       in1=b_adaT[:, l, 0:16].unsqueeze(2).to_broadcast([128, 16, 17]), op=ALU.add),
                    r=["b_adaT"], w=["ada_sb"], x=[BK(b)])
                bput(b)
                for half in range(2):
                    bp = bget()
                    bs = bget()
                    for wj in range(2):
                        wi = 8 + half * 2 + wj
                        i, slot, wk = wtake(("ada", l, wi))
                        for kc in range(NKC):
                            mm(ps[:, bp, wj * WC:(wj + 1) * WC], cTb[:, kc, :], slot[:, kc, :], kc == 0, kc == NKC - 1, [wk] + ar("cTb"), bp)
                        for kc in range(NKC):
                            mm(ps[0:NS, bs, wj * WC:(wj + 1) * WC], cT[:, kc, 1:17], slot[:, kc, :], kc == 0, kc == NKC - 1, [wk, "cT"], bs)
                        wrel(i)
                    DVE("tensor_tensor", dict(out=gate_bc[:, l, half * 512:(half + 1) * 512], in0=ps[:, bp, :],
                                                                        in1=bgate[:, half * 512:(half + 1) * 512], op=ALU.add),
                        r=ar("bgate"), w=["gate_bc"], x=[BK(bp)])
                    DVE("tensor_tensor", dict(out=gate_s[:, l, half * 512:(half + 1) * 512], in0=ps[0:NS, bs, :],
                                                                        in1=bgate[0:NS, half * 512:(half + 1) * 512], op=ALU.add),
                        r=ar("bgate"), w=["gate_s"], x=[BK(bs)])
                    bput(bp)
                    bput(bs)

            for g in range(ng_run):
                has_s = (g == SG)
                for l in range(nl_run):
                    last_g = (g == ng_run - 1)
                    if l == 0:
                        for j in range(GT):
                            DMA("sp", "dma_start", dict(out=x_sb[:, j, :], in_=xp[g * BT + j * 128: g * BT + (j + 1) * 128, :]), w=[("x", j)])
                    DMA("sp", "dma_start", dict(out=lng_bc[:], in_=ln_g[l].partition_broadcast(128)), w=["lng"])
                    DMA("sp", "dma_start", dict(out=lnb_bc[:], in_=ln_b[l].partition_broadcast(128)), w=["lnb"])

                    for c in range(NKC):
                        b = bget()
                        for j in range(GT):
                            tp(ps[:, b, j * 128:(j + 1) * 128], x_sb[:, j, c * 128:(c + 1) * 128], ident_f[:], ["ident_f", ("x", j)], b)
                        if c % 2 == 0:
                            ACT("activation", dict(out=uT[:, c, 0:BT], in_=ps[:, b, :], func=AF.Identity,
                                                   bias=ada_sb[:, l, c, 0:1], scale=ada_sb[:, l, 8 + c, 0:1]),
                                r=["ada_sb"], w=[("uT", c)], x=[BK(b)])
                        else:
                            DVE("tensor_scalar", dict(out=uT[:, c, 0:BT], in0=ps[:, b, :], scalar1=ada_sb[:, l, 8 + c, 0:1], scalar2=ada_sb[:, l, c, 0:1],
                                                      op0=ALU.mult, op1=ALU.add),
                                r=["ada_sb"], w=[("uT", c)], x=[BK(b)])
                        bput(b)
                    if has_s:
                        b = bget()
                        for c in range(NKC):
                            tp(ps[:, b, c * NS:(c + 1) * NS], xs_sb[:, c * 128:(c + 1) * 128], ident_f[0:NS, 0:NS], ["ident_f", "xs"], b)
                        DVE("tensor_tensor", dict(out=tmp_s[:], in0=ps[:, b, 0:NKC * NS].rearrange("p (c t) -> p c t", c=NKC),
                                                               in1=ada_sb[:, l, 8:16, 1:17], op=ALU.mult),
                            r=["ada_sb"], w=["tmp_s"], x=[BK(b)])
                        bput(b)
                        DVE("tensor_tensor", dict(out=uT[:, :, BT:TW], in0=tmp_s[:], in1=ada_sb[:, l, 0:8, 1:17], op=ALU.add),
                            r=["ada_sb", "tmp_s"], w=["uTs"])

                    uTr = [("uT", c) for c in range(NKC)]

                    epoch_barrier()
                    i, slot, wk = wtake(("alr", g, l))
                    b = bget()
                    for kc in range(NKC):
                        mm(ps[0:16, b, :], slot[:, kc, 0:16], uT[:, kc, 0:BT], kc == 0, kc == NKC - 1, [wk, ("uT", kc)], b)
                    ACT("copy", dict(out=alr_aug[0:16, 0:BT], in_=ps[0:16, b, :]), w=["alr_aug"], x=[BK(b)])
                    bput(b)
                    if has_s:
                        b = bget()
                        for kc in range(NKC):
                            mm(ps[0:16, b, 0:NS], slot[:, kc, 0:16], uT[:, kc, BT:TW], kc == 0, kc == NKC - 1, [wk, "uTs"], b)
                        ACT("copy", dict(out=alr_aug[0:16, BT:TW], in_=ps[0:16, b, 0:NS]), w=["alr_aug_s"], x=[BK(b)])
                        bput(b)
                    wrel(i)
                    bvs_h = [None]
                    bgs_h = [None]

                    def do_v(h):
                        i, slot, wk = wtake(("v", g, l, h))
                        for jp in range(2):
                            b = bget()
                            for jj in range(2):
                                j = jp * 2 + jj
                                for kc in range(NKC):
                                    mm(ps[:, b, jj * WC:(jj + 1) * WC], uT[:, kc, j * 128:(j + 1) * 128], slot[:, kc, :], kc == 0, kc == NKC - 1,
                                       [wk, ("uT", kc)], b)
                            DVE("tensor_copy", dict(out=v_tok[:, jp * 2:jp * 2 + 2, h * DV:(h + 1) * DV],
                                                    in_=ps[:, b, :].rearrange("p (j v) -> p j v", j=2)),
                                r=ar(), w=aw(f"v_tok{jp * 2}", f"v_tok{jp * 2 + 1}"), x=[BK(b)])
                            bput(b)
                        if has_s:
                            if h % 2 == 0:
                                bvs_h[0] = bget()
                            bvs = bvs_h[0]
                            for kc in range(NKC):
                                mm(ps[0:NS, bvs, (h % 2) * WC:(h % 2 + 1) * WC], uT[:, kc, BT:TW], slot[:, kc, :], kc == 0, kc == NKC - 1, [wk, "uTs"], bvs)
                            if h % 2 == 1:
                                ACT("copy", dict(out=v_s[:, (h - 1) * DV:(h + 1) * DV], in_=ps[0:NS, bvs, :]),
                                    r=ar(), w=aw("v_s"), x=[BK(bvs)])
                                bput(bvs)
                        wrel(i)

                    def do_gg(wi):
                        if has_s and wi == 0:
                            bgs_h[0] = bget()
                        bgs = bgs_h[0]
                        i, slot, wk = wtake(("gg", g, l, wi))
                        for sub in range(2):
                            fc = wi * 2 + sub
                            b = bget()
                            for kc in range(NKC):
                                mm(ps[:, b, :], slot[:, kc, sub * 128:(sub + 1) * 128], uT[:, kc, 0:BT], kc == 0, kc == NKC - 1, [wk, ("uT", kc)], b)
                            ACT("activation", dict(out=sgt[:, fc, 0:BT], in_=ps[:, b, :], func=AF.Silu), r=ar(), w=aw(f"sg{fc}"), x=[BK(b)])
                            bput(b)
                            if has_s:
                                for kc in range(NKC):
                                    mm(ps[:, bgs, fc * NS:(fc + 1) * NS], slot[:, kc, sub * 128:(sub + 1) * 128], uT[:, kc, BT:TW],
                                       kc == 0, kc == NKC - 1, [wk, "uTs"], bgs)
                        wrel(i)
                        if has_s and wi == 3:
                            ACT("activation", dict(out=sgt[:, :, BT:TW], in_=ps[:, bgs, 0:NKC * NS].rearrange("p (c t) -> p c t", c=NKC), func=AF.Silu),
                                r=ar(), w=aw("sg_s"), x=[BK(bgs)])
                            bput(bgs)

                    bulk = [(do_v, h) for h in range(H)] + [(do_gg, wi) for wi in range(4)]
                    EQK = aw(*[f"E_q{h}" for h in range(H)])
                    EKK = aw(*[f"E_k{h}" for h in range(H)])
                    for j in range(GT):
                        jj = j % 2
                        b = bget()
                        mm(ps[:, b, :], alr_aug[0:17, j * 128:(j + 1) * 128], w_a2aug[0:17, l, :], True, True, ["alr_aug", "w_a2aug"], b)
                        ACT("activation", dict(out=e1, in_=ps[:, b, :], func=AF.Exp, scale=-1.0), r=ar(), w=aw("e1"), x=[BK(b)])
                        bput(b)
                        ACT("activation", dict(out=sp[:, jj, :], in_=e1, func=AF.Ln, bias=1.0), r=ar("e1"), w=aw("sp"))
                        f_, a_ = bulk.pop(0)
                        f_(a_)
                        bc = bget()
                        for h in range(H):
                            mm(ps[:, bc, h * 128:(h + 1) * 128], sp[:, jj, h * 128:(h + 1) * 128], tri[:], True, True, ["tri"] + ar("sp"), bc)
                        csv = ps[:, bc, :].rearrange("p (h t) -> p h t", h=H)
                        ACT("activation", dict(out=E_q[:, :, j * 128:(j + 1) * 128], in_=csv, func=AF.Exp, scale=-1.0 / 16), r=ar(), w=EQK, x=[BK(bc)])
                        ACT("activation", dict(out=E_k[:, :, j * 128:(j + 1) * 128], in_=csv, func=AF.Exp, scale=1.0 / 16), r=ar(), w=EKK, x=[BK(bc)])
                        bput(bc)
                    if has_s:
                        b = bget()
                        mm(ps[0:NS, b, :], alr_aug[0:17, BT:TW], w_a2aug[0:17, l, :], True, True, ["alr_aug_s", "w_a2aug"], b)
                        ACT("activation", dict(out=e1[0:NS, :], in_=ps[0:NS, b, :], func=AF.Exp, scale=-1.0), r=ar(), w=aw("e1"), x=[BK(b)])
                        bput(b)
                        ACT("activation", dict(out=sp[0:NS, 0, :], in_=e1[0:NS, :], func=AF.Ln, bias=1.0), r=ar("e1"), w=aw("sp"))
                        b = bget()
                        for h in range(H):
                            tp(ps[:, b, h * NS:(h + 1) * NS], sp[0:NS, 0, h * 128:(h + 1) * 128], ident_f[0:NS, 0:NS], ["ident_f"] + ar("sp"), b)
                        ACT("activation", dict(out=a_s[:].rearrange("p h s -> p (h s)"), in_=ps[:, b, 0:H * NS], func=AF.Exp, scale=-1.0 / 16),
                            r=ar(), w=aw("a_s"), x=[BK(b)])
                        bput(b)
                    while bulk:
                        f_, a_ = bulk.pop(0)
                        f_(a_)
                    for nm, Eb, dst in (("q", E_q, q_in), ("k", E_k, k_in)):
                        bs = bget() if (has_s and nm == "q") else None
                        bkt = bget() if (has_s and nm == "k") else None
                        for wi in range(2):
                            i, slot, wk = wtake((nm, g, l, wi))
                            for sub in range(2):
                                h = wi * 2 + sub
                                b = bget()
                                for kc in range(NKC):
                                    mm(ps[:, b, :], slot[:, kc, sub * 128:(sub + 1) * 128], uT[:, kc, 0:BT], kc == 0, kc == NKC - 1, [wk, ("uT", kc)], b)
                                if nm == "q":
                                    DVE("scalar_tensor_tensor", dict(out=q_in[:, h, :], in0=ps[:, b, :], scalar=QSCALE, in1=E_q[:, h, :],
                                                                                   op0=ALU.mult, op1=ALU.mult),
                                        r=ar(f"E_q{h}"), w=aw(f"q_in{h}"), x=[BK(b)])
                                else:
                                    DVE("tensor_tensor", dict(out=k_in[:, h, :], in0=ps[:, b, :], in1=E_k[:, h, :], op=ALU.mult),
                                        r=ar(f"E_k{h}"), w=aw(f"k_in{h}"), x=[BK(b)])
                                bput(b)
                                if has_s and nm == "q":
                                    for kc in range(NKC):
                                        mm(ps[:, bs, h * NS:(h + 1) * NS], slot[:, kc, sub * 128:(sub + 1) * 128], uT[:, kc, BT:TW],
                                           kc == 0, kc == NKC - 1, [wk, "uTs"], bs)
                            if has_s and nm == "k":
                                for kc in range(NKC):
                                    mm(ps[0:NS, bkt, wi * WC:(wi + 1) * WC], uT[:, kc, BT:TW], slot[:, kc, :], kc == 0, kc == NKC - 1, [wk, "uTs"], bkt)
                            wrel(i)
                        if has_s and nm == "q":
                            ACT("activation", dict(out=q_s[:].rearrange("p h s -> p (h s)"), in_=ps[:, bs, 0:H * NS], func=AF.Identity, scale=QSCALE),
                                r=ar(), w=aw("q_s"), x=[BK(bs)])
                        if has_s and nm == "k":
                            ACT("copy", dict(out=ktok_s, in_=ps[0:NS, bkt, :]), r=ar(), w=aw("e1"), x=[BK(bkt)])
                            bput(bkt)
                        if has_s and nm == "q":
                            bput(bs)
                    bos = bget() if has_s else None

                    def sample_load(s):
                        DMA("sp", "dma_start", dict(out=Sbuf[:, s % 3, :].rearrange("p (h v) -> p h v", h=H), in_=sgl[l, s].rearrange("h k v -> k h v")),
                            w=[("Sbuf", s % 3)])

                    if has_s:
                        for s_ in range(3):
                            sample_load(s_)

                    def stage_A(j):
                        jj = j % 2
                        jc0, jc1 = j * 128, (j + 1) * 128
                        b = bget()
                        psb = ps[:, b, :].bitcast(BF16)
                        for h in range(H):
                            tp(psb[:, h * 128:(h + 1) * 128], k_in[:, h, jc0:jc1], ident_b[:], ["ident_b"] + ar(f"k_in{h}"), b)
                        ACT("copy", dict(out=k_tok[:, jj, :], in_=psb[:, 0:512]), r=ar(), w=aw(f"k_tok{jj}"), x=[BK(b)])
                        bput(b)
                        b = bget()
                        for h in range(H):
                            mm(ps[:, b, h * 128:(h + 1) * 128], k_in[:, h, jc0:jc1], q_in[:, h, jc0:jc1], True, True, ar(f"k_in{h}", f"q_in{h}"), b)
                        DVE("tensor_tensor", dict(out=attm[:, jj, :].rearrange("p (h t) -> p h t", h=H), in0=ps[:, b, :].rearrange("p (h t) -> p h t", h=H),
                                                  in1=tri[:].unsqueeze(1).to_broadcast([128, H, 128]), op=ALU.mult),
                            r=["tri"] + ar(), w=aw(f"attm{jj}"), x=[BK(b)])
                        bput(b)

                    def samp_A(s):
                        kmv = km2[:, s % 2, :]
                        DVE("tensor_scalar", dict(out=kmv, in0=ktok_s, scalar1=ident_f[0:NS, s:s + 1], scalar2=None, op0=ALU.mult),
                            r=["ident_f"] + ar("e1"), w=aw(f"km{s % 2}"))
                        b2 = bget2()
                        for h in range(H):
                            mm(ps[:, b2 + h // 2, (h % 2) * DV:(h % 2 + 1) * DV], kmv[:, h * 128:(h + 1) * 128], v_s[:, h * DV:(h + 1) * DV], True, True,
                               ar(f"km{s % 2}", "v_s"), b2 + h // 2)
                        return b2

                    def samp_B(s, b2):
                        buf = Sbuf[:, s % 3, :]
                        bk = ("Sbuf", s % 3)
                        sbv = sbf[:, s % 2, :]
                        sk = ("sbf", s % 2)
                        for h in range(H):
                            ACT("activation", dict(out=buf[:, h * DV:(h + 1) * DV], in_=buf[:, h * DV:(h + 1) * DV], func=AF.Identity,
                                                   scale=a_s[:, h, s:s + 1]),
                                r=[bk] + ar("a_s"), w=[bk])
                        for half in range(2):
                            DVE("tensor_tensor", dict(out=buf[:, half * 512:(half + 1) * 512], in0=ps[:, b2 + half, :],
                                                      in1=buf[:, half * 512:(half + 1) * 512], op=ALU.add),
                                r=[bk], w=[bk], x=[BK(b2 + half)])
                        bput(b2, 2)
                        DMA("sp", "dma_start", dict(out=gs[l, s].rearrange("h k v -> k h v"), in_=buf.rearrange("p (h v) -> p h v", h=H)),
                            r=[bk], w=[("gs", l, s)])
                        DVE("tensor_copy", dict(out=sbv, in_=buf), r=[bk], w=[sk])
                        if s + 3 < NS:
                            sample_load(s + 3)
                        for h in range(H):
                            for vc in range(2):
                                idx = h * 2 + vc
                                mm(ps[:, bos, idx * NS + s: idx * NS + s + 1], sbv[:, idx * 128:(idx + 1) * 128], q_s[:, h, s:s + 1], True, True,
                                   [sk] + ar("q_s"), bos)

                    stage_A(0)
                    pend_norm = [None]
                    for j in range(GT):
                        jj = j % 2
                        jc0, jc1 = j * 128, (j + 1) * 128
                        bo = bget2()
                        for h in range(H):
                            for vc in range(2):
                                idx = h * 2 + vc
                                out = ps[:, bo + idx // 4, (idx % 4) * 128:(idx % 4 + 1) * 128]
                                mm(out, v_tok[:, j, idx * 128:(idx + 1) * 128], attm[:, jj, h * 128:(h + 1) * 128], True, False,
                                   ar(f"v_tok{j}", f"attm{jj}"), bo + idx // 4)
                                mm(out, S_bf[:, l, idx * 128:(idx + 1) * 128], q_in[:, h, jc0:jc1], False, True, [("S_bf", l, h)] + ar(f"q_in{h}"), bo + idx // 4)
                        b2 = bget2()
                        for h in range(H):
                            mm(ps[:, b2 + h // 2, (h % 2) * DV:(h % 2 + 1) * DV], k_tok[:, jj, h * 128:(h + 1) * 128], v_tok[:, j, h * DV:(h + 1) * DV], True, True,
                               ar(f"k_tok{jj}", f"v_tok{j}"), b2 + h // 2)
                        for half in range(2):
                            DVE("tensor_tensor", dict(out=Pst[:, half * 512:(half + 1) * 512], in0=ps[:, b2 + half, :],
                                                      in1=S[:, l, half * 512:(half + 1) * 512], op=ALU.add),
                                r=[("S", l, 2 * half), ("S", l, 2 * half + 1)] + ar(), w=aw(f"P{half}"), x=[BK(b2 + half)])
                        bput(b2, 2)
                        for h in range(H):
                            dcol = E_q[:, h, jc1 - 1:jc1]
                            ACT("activation", dict(out=S_bf[:, l, h * DV:(h + 1) * DV], in_=Pst[:, h * DV:(h + 1) * DV], func=AF.Identity, scale=dcol),
                                r=ar(f"E_q{h}", f"P{h // 2}"), w=[("S_bf", l, h)])
                        for h in range(H):
                            dcol = E_q[:, h, jc1 - 1:jc1]
                            DVE("tensor_scalar", dict(out=S[:, l, h * DV:(h + 1) * DV], in0=Pst[:, h * DV:(h + 1) * DV], scalar1=dcol, scalar2=None,
                                                      op0=ALU.mult),
                                r=ar(f"E_q{h}", f"P{h // 2}"), w=[("S", l, h)])
                        if j + 1 < GT:
                            stage_A(j + 1)
                        ov = ps[:, bo:bo + 2, :].rearrange("p b (c t) -> p (b c) t", c=4)
                        ACT("activation", dict(out=sq[:], in_=ov, func=AF.Square), r=ar(), w=aw("sq"), x=[BK(bo), BK(bo + 1)])
                        bss = bget()
                        for h in range(H):
                            for vc in range(2):
                                mm(ps[:, bss, h * 128:(h + 1) * 128], ones_b[:], sq[:, h * 2 + vc, :], vc == 0, vc == 1, ["ones_b"] + ar("sq"), bss)
                        ACT("activation", dict(out=lnv, in_=ps[:, bss, :], func=AF.Ln, bias=EPS, scale=1.0 / DV), r=ar(), w=aw("lnv"), x=[BK(bss)])
                        bput(bss)
                        ACT("activation", dict(out=rstd2[:, jj].rearrange("p h t -> p (h t)"), in_=lnv, func=AF.Exp, scale=-0.5), r=ar("lnv"), w=aw(f"rstd{jj}"))
                        if pend_norm[0] is not None:
                            pend_norm[0]()

                        def norm_dve(j=j, jj=jj, bo=bo, ov=ov, jc0=jc0, jc1=jc1):
                            for vc in range(2):
                                DVE("scalar_tensor_tensor", dict(out=t1[:, vc::2, :], in0=ov[:, vc::2, :], scalar=gnorm[:, l, vc:vc + 1], in1=rstd2[:, jj],
                                                                 op0=ALU.mult, op1=ALU.mult),
                                    r=["gnorm"] + ar(f"rstd{jj}"), w=aw("t1"), x=[BK(bo), BK(bo + 1)])
                            bput(bo, 2)
                            DVE("tensor_tensor", dict(out=og[:, :, jc0:jc1], in0=t1[:], in1=sgt[:, :, jc0:jc1], op=ALU.mult),
                                r=ar("t1", *[f"sg{fc}" for fc in range(8)]), w=[("og", j)])

                        pend_norm[0] = norm_dve
                        if has_s:
                            s0 = j * 4
                            b2n = samp_A(s0)
                            for s in range(s0, s0 + 4):
                                b2c = b2n
                                if s + 1 < s0 + 4:
                                    b2n = samp_A(s + 1)
                                samp_B(s, b2c)
                    pend_norm[0]()
                    pend_norm[0] = None
                    if has_s:
                        ovs = ps[:, bos, 0:NKC * NS].rearrange("p (c t) -> p c t", c=NKC)
                        ACT("activation", dict(out=sq_s[:], in_=ovs, func=AF.Square), r=ar(), w=aw("sq_s"), x=[BK(bos)])
                        bss = bget()
                        for h in range(H):
                            for vc in range(2):
                                mm(ps[:, bss, h * NS:(h + 1) * NS], ones_b[:], sq_s[:, h * 2 + vc, :], vc == 0, vc == 1, ["ones_b"] + ar("sq_s"), bss)
                        ACT("activation", dict(out=lnv_s, in_=ps[:, bss, 0:H * NS], func=AF.Ln, bias=EPS, scale=1.0 / DV), r=ar(), w=aw("lnv_s"), x=[BK(bss)])
                        bput(bss)
                        ACT("activation", dict(out=rstd_s[:].rearrange("p h t -> p (h t)"), in_=lnv_s, func=AF.Exp, scale=-0.5), r=ar("lnv_s"), w=aw("rstd_s"))
                        for vc in range(2):
                            DVE("scalar_tensor_tensor", dict(out=t1_s[:, vc::2, :], in0=ovs[:, vc::2, :], scalar=gnorm[:, l, vc:vc + 1], in1=rstd_s[:],
                                                                                 op0=ALU.mult, op1=ALU.mult),
                                r=["gnorm"] + ar("rstd_s"), w=aw("t1_s"), x=[BK(bos)])
                        bput(bos)
                        DVE("tensor_tensor", dict(out=og[:, :, BT:TW], in0=t1_s[:], in1=sgt[:, :, BT:TW], op=ALU.mult), r=ar("t1_s", "sg_s"), w=["og_s"])
                    if last_g:
                        DMA("sp", "dma_start", dict(out=gp[l].rearrange("h k v -> k h v"), in_=S[:, l, :].rearrange("p (h v) -> p h v", h=H)),
                            r=[("S", l, h_) for h_ in range(H)], w=[("gp", l)])

                    epoch_barrier()
                    DVE("tensor_copy", dict(out=pbuf[:, :, 0:2], in_=carry[:, l, :, :]), r=["carry"] + ar(), w=aw("pb0", "pb1", "pb2", "pb3"))
                    if has_s:
                        DMA("sp", "dma_start", dict(out=sc_tok, in_=scv[l].rearrange("s r d -> (s r) d")), r=[EP], w=aw("sc_tok"))
                        b = bget()
                        for c in range(4):
                            tp(ps[:, b, c * 2 * NS:(c + 1) * 2 * NS], sc_tok[:, c * 128:(c + 1) * 128], ident_f[0:2 * NS, 0:2 * NS], ["ident_f"] + ar("sc_tok"), b)
                        ACT("copy", dict(out=scT[:].rearrange("p c t -> p (c t)"), in_=ps[:, b, 0:4 * 2 * NS]), r=ar(), w=aw("scT"), x=[BK(b)])
                        bput(b)
                        DMA("sp", "dma_start", dict(out=cs[l, :, 0, :], in_=scv[l, :, 1, :]), w=[("cs0", l)])

                    def conv_mm(nm, post, post_s):
                        bs = bget() if has_s else None
                        for wi in range(2):
                            i, slot, wk = wtake((nm, g, l, wi))
                            for sub in range(2):
                                c = wi * 2 + sub
                                b = bget()
                                for kc in range(NKC):
                                    mm(ps[:, b, :], slot[:, kc, sub * 128:(sub + 1) * 128], uT[:, kc, 0:BT], kc == 0, kc == NKC - 1, [wk, ("uT", kc)], b)
                                post(c, b)
                                bput(b)
                                if has_s:
                                    for kc in range(NKC):
                                        mm(ps[:, bs, c * NS:(c + 1) * NS], slot[:, kc, sub * 128:(sub + 1) * 128], uT[:, kc, BT:TW],
                                           kc == 0, kc == NKC - 1, [wk, "uTs"], bs)
                            wrel(i)
                        if has_s:
                            post_s(ps[:, bs, 0:4 * NS].rearrange("p (c t) -> p c t", c=4), bs)
                            bput(bs)

                    conv_mm("cc",
                            lambda c, b: ACT("copy", dict(out=ccy[:, c, 0:BT], in_=ps[:, b, :]), r=ar(), w=aw(f"ccy{c}"), x=[BK(b)]),
                            lambda pv, bs: ACT("copy", dict(out=ccy[:, :, BT:TW], in_=pv), r=ar(), w=aw("ccy_s"), x=[BK(bs)]))

                    def post_ch(c, b):
                        DVE("tensor_tensor", dict(out=pbuf[:, c, 2:BT + 2], in0=ps[:, b, :], in1=ccy[:, c, 0:BT], op=ALU.mult),
                            r=ar(f"ccy{c}"), w=aw(f"pb{c}"), x=[BK(b)])
                        ACT("activation", dict(out=tmp1, in_=pbuf[:, c, 0:BT], func=AF.Identity, scale=convw[:, l, c, 0:1]),
                            r=["convw"] + ar(f"pb{c}"), w=aw("tmp1"))
                        DVE("scalar_tensor_tensor", dict(out=tmp2, in0=pbuf[:, c, 1:BT + 1], scalar=convw[:, l, c, 1:2], in1=tmp1, op0=ALU.mult, op1=ALU.add),
                            r=["convw"] + ar(f"pb{c}", "tmp1"), w=aw("tmp2"))
                        DVE("scalar_tensor_tensor", dict(out=ccy[:, c, 0:BT], in0=pbuf[:, c, 2:BT + 2], scalar=convw[:, l, c, 2:3], in1=tmp2, op0=ALU.mult, op1=ALU.add),
                            r=["convw"] + ar(f"pb{c}", "tmp2"), w=aw(f"ccy{c}"))

                    def post_ch_s(pv, bs):
                        DVE("tensor_tensor", dict(out=p_s[:], in0=pv, in1=ccy[:, :, BT:TW], op=ALU.mult), r=ar("ccy_s"), w=aw("p_s"), x=[BK(bs)])
                        for c in range(4):
                            ACT("activation", dict(out=tcs1, in_=scT[:, c, 0:2 * NS:2], func=AF.Identity, scale=convw[:, l, c, 0:1]),
                                r=["convw"] + ar("scT"), w=aw("tcs1"))
                            DVE("scalar_tensor_tensor", dict(out=tcs2, in0=scT[:, c, 1:2 * NS:2], scalar=convw[:, l, c, 1:2], in1=tcs1, op0=ALU.mult, op1=ALU.add),
                                r=["convw"] + ar("scT", "tcs1"), w=aw("tcs2"))
                            DVE("scalar_tensor_tensor", dict(out=ccy[:, c, BT:TW], in0=p_s[:, c, :], scalar=convw[:, l, c, 2:3], in1=tcs2, op0=ALU.mult, op1=ALU.add),
                                r=["convw"] + ar("p_s", "tcs2"), w=aw("ccy_s"))

                    conv_mm("ch", post_ch, post_ch_s)
                    DVE("tensor_copy", dict(out=carry[:, l, :, :], in_=pbuf[:, :, BT:BT + 2]), r=ar("pb0", "pb1", "pb2", "pb3"), w=["carry"])
                    if last_g:
                        b = bget()
                        for c in range(4):
                            tp(ps[0:2, b, c * 128:(c + 1) * 128], carry[:, l, c, :], ident_f[:], ["ident_f", "carry"], b)
                        ACT("copy", dict(out=cst, in_=ps[0:2, b, :]), r=ar(), w=aw("cst"), x=[BK(b)])
                        bput(b)
                        DMA("sp", "dma_start", dict(out=cp[l], in_=cst), r=ar("cst"), w=[("cp", l)])
                    if has_s:
                        b = bget()
                        for c in range(4):
                            tp(ps[0:NS, b, c * 128:(c + 1) * 128], p_s[:, c, :], ident_f[:], ["ident_f"] + ar("p_s"), b)
                        ACT("copy", dict(out=pst, in_=ps[0:NS, b, :]), r=ar(), w=aw("pst"), x=[BK(b)])
                        bput(b)
                        DMA("sp", "dma_start", dict(out=cs[l, :, 1, :], in_=pst), r=ar("pst"), w=[("cs1", l)])
                    conv_mm("cb",
                            lambda c, b: DVE("tensor_tensor", dict(out=ccy[:, c, 0:BT], in0=ps[:, b, :], in1=ccy[:, c, 0:BT], op=ALU.mult),
                                             r=ar(f"ccy{c}"), w=aw(f"ccy{c}"), x=[BK(b)]),
                            lambda pv, bs: DVE("tensor_tensor", dict(out=ccy[:, :, BT:TW], in0=pv, in1=ccy[:, :, BT:TW], op=ALU.mult),
                                               r=ar("ccy_s"), w=aw("ccy_s"), x=[BK(bs)]))

                    def post_gc(c, b):
                        ACT("activation", dict(out=tmp3, in_=ps[:, b, :], func=AF.Silu), r=ar(), w=aw("tmp3"), x=[BK(b)])
                        DVE("tensor_tensor", dict(out=yc[:, c, 0:BT], in0=ccy[:, c, 0:BT], in1=tmp3, op=ALU.mult), r=ar(f"ccy{c}", "tmp3"), w=[("yc", c)])

                    def post_gc_s(pv, bs):
                        ACT("activation", dict(out=tmp3s[:], in_=pv, func=AF.Silu), r=ar(), w=aw("tmp3s"), x=[BK(bs)])
                        DVE("tensor_tensor", dict(out=yc[:, :, BT:TW], in0=ccy[:, :, BT:TW], in1=tmp3s[:], op=ALU.mult), r=ar("ccy_s", "tmp3s"), w=["yc_s"])

                    conv_mm("gc", post_gc, post_gc_s)

                    ogr = [("og", j) for j in range(GT)]
                    ycr = [("yc", c) for c in range(4)]
                    for cpi in range(4):
                        ia, sa, ka = wtake(("pa", g, l, cpi))
                        ib, sbb, kb = wtake(("pb", g, l, cpi))
                        ig, sgw, kg = wtake(("mg", g, l, cpi))
                        ic, scw, kcw = wtake(("mc", g, l, cpi))
                        for sub in range(2):
                            c = cpi * 2 + sub
                            cs0, cs1 = sub * 128, (sub + 1) * 128
                            bA = bget()
                            for kc in range(NKC):
                                mm(ps[:, bA, :], sa[:, kc, cs0:cs1], og[:, kc, 0:BT], kc == 0, kc == NKC - 1, [ka] + ogr, bA)
                            bB = bget()
                            for kc in range(4):
                                mm(ps[:, bB, :], sbb[:, kc, cs0:cs1], yc[:, kc, 0:BT], kc == 0, kc == 3, [kb, ("yc", kc)], bB)
                            bG = bget()
                            for kc in range(NKC):
                                mm(ps[:, bG, :], sgw[:, kc, cs0:cs1], uT[:, kc, 0:BT], kc == 0, kc == NKC - 1, [kg, ("uT", kc)], bG)
                            bC = bget()
                            for kc in range(NKC):
                                mm(ps[:, bC, :], scw[:, kc, cs0:cs1], uT[:, kc, 0:BT], kc == 0, kc == NKC - 1, [kcw, ("uT", kc)], bC)
                            ACT("activation", dict(out=sgm, in_=ps[:, bG, :], func=AF.Sigmoid), r=ar(), w=aw("sgm"), x=[BK(bG)])
                            bput(bG)
                            ACT("activation", dict(out=scm, in_=ps[:, bC, :], func=AF.Sigmoid), r=ar(), w=aw("scm"), x=[BK(bC)])
                            bput(bC)
                            DVE("tensor_tensor", dict(out=tm1, in0=ps[:, bA, :], in1=sgm, op=ALU.mult), r=ar("sgm"), w=aw("tm1"), x=[BK(bA)])
                            bput(bA)
                            DVE("tensor_tensor", dict(out=tm2, in0=ps[:, bB, :], in1=scm, op=ALU.mult), r=ar("scm"), w=aw("tm2"), x=[BK(bB)])
                            bput(bB)
                            DVE("tensor_tensor", dict(out=mg[:, c, 0:BT], in0=tm1, in1=tm2, op=ALU.add), r=ar("tm1", "tm2"), w=aw(f"mg{c}"))
                            if has_s:
                                bS = bget()
                                for kc in range(NKC):
                                    mm(ps[:, bS, 0:NS], sa[:, kc, cs0:cs1], og[:, kc, BT:TW], kc == 0, kc == NKC - 1, [ka, "og_s"], bS)
                                for kc in range(4):
                                    mm(ps[:, bS, NS:2 * NS], sbb[:, kc, cs0:cs1], yc[:, kc, BT:TW], kc == 0, kc == 3, [kb, "yc_s"], bS)
                                for kc in range(NKC):
                                    mm(ps[:, bS, 2 * NS:3 * NS], sgw[:, kc, cs0:cs1], uT[:, kc, BT:TW], kc == 0, kc == NKC - 1, [kg, "uTs"], bS)
                                for kc in range(NKC):
                                    mm(ps[:, bS, 3 * NS:4 * NS], scw[:, kc, cs0:cs1], uT[:, kc, BT:TW], kc == 0, kc == NKC - 1, [kcw, "uTs"], bS)
                                ACT("activation", dict(out=sg_s, in_=ps[:, bS, 2 * NS:4 * NS], func=AF.Sigmoid), r=ar(), w=aw("sg_s2"), x=[BK(bS)])
                                DVE("tensor_tensor", dict(out=t_s, in0=ps[:, bS, 0:2 * NS], in1=sg_s, op=ALU.mult), r=ar("sg_s2"), w=aw("t_s"), x=[BK(bS)])
                                bput(bS)
                                DVE("tensor_tensor", dict(out=mg[:, c, BT:TW], in0=t_s[:, 0:NS], in1=t_s[:, NS:2 * NS], op=ALU.add), r=ar("t_s"), w=aw("mg_s"))
                        wrel(ia)
                        wrel(ib)
                        wrel(ig)
                        wrel(ic)

                    so = [wtake(("o", g, l, wi)) for wi in range(4)]
                    mgr = ar(*[f"mg{c}" for c in range(NKC)])

                    def o_tail(xv, gv, tov, stv, mvv, lg, lb, bo, npart, xkey, mkeys):
                        if gv is not None:
                            DVE("tensor_tensor", dict(out=tov.rearrange("p (b n) -> p b n", b=2), in0=ps[0:npart, bo:bo + 2, :],
                                                      in1=gv.rearrange("p (b n) -> p b n", b=2), op=ALU.mult),
                                r=["gate_bc", "gate_s"] + ar(), w=aw(mkeys + "to"), x=[BK(bo), BK(bo + 1)])
                            bput(bo, 2)
                            DVE("scalar_tensor_tensor", dict(out=xv, in0=xv, scalar=ALPHA, in1=tov, op0=ALU.mult, op1=ALU.add),
                                r=[xkey] + ar(mkeys + "to"), w=[xkey])
                        else:
                            DVE("scalar_tensor_tensor", dict(out=xv.rearrange("p (b n) -> p b n", b=2), in0=xv.rearrange("p (b n) -> p b n", b=2), scalar=ALPHA,
                                                             in1=ps[0:npart, bo:bo + 2, :], op0=ALU.mult, op1=ALU.add),
                                r=[xkey], w=[xkey], x=[BK(bo), BK(bo + 1)])
                            bput(bo, 2)
                        ACT("activation", dict(out=tov, in_=xv, func=AF.Identity, accum_out=mvv[:, 5:6]), r=[xkey] + ar(), w=aw(mkeys + "to") + [mkeys + "s1"])
                        ACT("activation", dict(out=tov, in_=xv, func=AF.Square, accum_out=mvv[:, 6:7]), r=[xkey] + ar(), w=aw(mkeys + "to") + [mkeys + "s2"])
                        DVE("tensor_scalar", dict(out=mvv[:, 0:1], in0=mvv[:, 5:6], scalar1=1.0 / D, scalar2=None, op0=ALU.mult), r=[mkeys + "s1"], w=[mkeys + "mean"])
                        DVE("tensor_tensor", dict(out=mvv[:, 7:8], in0=mvv[:, 0:1], in1=mvv[:, 0:1], op=ALU.mult), r=[mkeys + "mean"], w=[mkeys + "msq"])
                        DVE("tensor_scalar", dict(out=mvv[:, 1:2], in0=mvv[:, 6:7], scalar1=1.0 / D, scalar2=mvv[:, 7:8], op0=ALU.mult, op1=ALU.subtract),
                            r=[mkeys + "s2", mkeys + "msq"], w=[mkeys + "mv"])
                        ACT("activation", dict(out=mvv[:, 2:3], in_=mvv[:, 1:2], func=AF.Ln, bias=EPS), r=[mkeys + "mv"], w=[mkeys + "mv2"])
                        ACT("activation", dict(out=mvv[:, 3:4], in_=mvv[:, 2:3], func=AF.Exp, scale=-0.5), r=[mkeys + "mv2"], w=[mkeys + "mv3"])
                        DVE("tensor_scalar", dict(out=mvv[:, 4:5], in0=mvv[:, 0:1], scalar1=mvv[:, 3:4], scalar2=-1.0, op0=ALU.mult, op1=ALU.mult),
                            r=[mkeys + "mean", mkeys + "mv3"], w=[mkeys + "mv4"])
                        ACT("activation", dict(out=xv, in_=xv, func=AF.Identity, bias=mvv[:, 4:5], scale=mvv[:, 3:4]),
                            r=[xkey, mkeys + "mv3", mkeys + "mv4"], w=[xkey])
                        DVE("tensor_tensor", dict(out=xv, in0=xv, in1=lg, op=ALU.mult), r=[xkey, "lng"], w=[xkey])
                        DVE("tensor_tensor", dict(out=xv, in0=xv, in1=lb, op=ALU.add), r=[xkey, "lnb"], w=[xkey])

                    if has_s:
                        bo = bget2()
                        for wi in range(4):
                            for kc in range(NKC):
                                mm(ps[0:NS, bo + wi // 2, (wi % 2) * WC:(wi % 2 + 1) * WC], mg[:, kc, BT:TW], so[wi][1][:, kc, :],
                                   kc == 0, kc == NKC - 1, [so[wi][2]] + ar("mg_s"), bo + wi // 2)
                        o_tail(xs_sb[:], gate_s[:, l, :], to_s, st_s[:], mv_s[:], lng_bc[0:NS, :], lnb_bc[0:NS, :], bo, NS, "xs", "s")
                    for j in range(GT):
                        bo = bget2()
                        for wi in range(4):
                            for kc in range(NKC):
                                mm(ps[:, bo + wi // 2, (wi % 2) * WC:(wi % 2 + 1) * WC], mg[:, kc, j * 128:(j + 1) * 128], so[wi][1][:, kc, :],
                                   kc == 0, kc == NKC - 1, [so[wi][2]] + mgr, bo + wi // 2)
                        o_tail(x_sb[:, j, :], gate_bc[:, l, :], to, st[:], mv[:], lng_bc[:], lnb_bc[:], bo, 128, ("x", j), "p")
                        if l == nl_run - 1:
                            DMA("sp", "dma_start", dict(out=yp[g * BT + j * 128: g * BT + (j + 1) * 128, :], in_=x_sb[:, j, :]),
                                r=[("x", j)], w=[("yp", g, j)])
                    for wi in range(4):
                        wrel(so[wi][0])
                    if l == nl_run - 1:
                        pass
                        if has_s:
                            DMA("sp", "dma_start", dict(out=ys, in_=xs_sb[:]), r=["xs"], w=["ys"])

            assert wstate["taken"] == len(wlist), (wstate, len(wlist))
            return wlist

        wl = emit_all(Prog(), True, None)
        P = Prog()
        emit_all(P, False, wl)
        P.finalize()

        @block.tensor
        def _(e):
            P.run_engine("pe", e, sems, dma_sems)

        @block.scalar
        def _(e):
            P.run_engine("act", e, sems, dma_sems)

        @block.vector
        def _(e):
            P.run_engine("dve", e, sems, dma_sems)

        @block.gpsimd
        def _(e):
            P.run_engine("pool", e, sems, dma_sems)

        @block.sync
        def _(e):
            P.run_engine("sp", e, sems, dma_sems)
            P.final_wait(e, dma_sems)

    return nc


_NC_CACHE = {}


def kernel(x_prompt, x_sample, c_prompt, c_sample, state_gla, state_conv, w_ada, b_ada, w_in, w_a2, b_a,
           gla_norm_g, conv_w, w_pa, w_pb, w_o, ln_g, ln_b):
    f = lambda a: np.ascontiguousarray(np.asarray(a, dtype=np.float32))
    x_prompt, x_sample, c_prompt, c_sample = f(x_prompt), f(x_sample), f(c_prompt), f(c_sample)
    state_gla, state_conv = f(state_gla), f(state_conv)
    shared = dict(w_ada=f(w_ada), b_ada=f(b_ada), w_in=f(w_in), w_a2=f(w_a2), b_a=f(b_a), gng=f(gla_norm_g),
                  conv_w=f(conv_w), w_pa=f(w_pa), w_pb=f(w_pb), w_o=f(w_o), ln_g=f(ln_g), ln_b=f(ln_b))
    ncores = 8
    in_maps = []
    for i in range(ncores):
        sl = slice(i * NS, (i + 1) * NS)
        m = dict(shared)
        m["xp"] = x_prompt[i]
        m["xs"] = np.ascontiguousarray(x_sample[sl, 0, :])
        m["c17"] = np.ascontiguousarray(np.concatenate([c_prompt[i:i + 1], c_sample[sl]], axis=0))
        m["sgl"] = np.ascontiguousarray(state_gla[:, sl])
        m["scv"] = np.ascontiguousarray(state_conv[:, sl])
        in_maps.append(m)
    if "nc" not in _NC_CACHE:
        _NC_CACHE["nc"] = build_program()
    res = run_bass_kernel_spmd(_NC_CACHE["nc"], in_maps, core_ids=list(range(ncores)))
    R = res.results
    y_prompt = np.stack([R[i]["yp"] for i in range(ncores)], axis=0)
    y_sample = np.concatenate([R[i]["ys"] for i in range(ncores)], axis=0)[:, None, :]
    gla_p = np.stack([R[i]["gp"] for i in range(ncores)], axis=1)
    conv_p = np.stack([R[i]["cp"] for i in range(ncores)], axis=1)
    gla_s = np.concatenate([R[i]["gs"] for i in range(ncores)], axis=1)
    conv_s = np.concatenate([R[i]["cs"] for i in range(ncores)], axis=1)
    return (y_prompt.astype(np.float32), y_sample.astype(np.float32), gla_p.astype(np.float32),
            conv_p.astype(np.float32), gla_s.astype(np.float32), conv_s.astype(np.float32))
```

```python
import os
import numpy as np
from contextlib import ExitStack
import concourse.bass as bass
import concourse.mybir as mybir
from concourse.bass_utils import run_bass_kernel_spmd

F32 = mybir.dt.float32
BF16 = mybir.dt.bfloat16
AF = mybir.ActivationFunctionType
ALU = mybir.AluOpType

ENGS = ("pe", "act", "dve", "pool", "sp")

D = 1024
SEQ = 2048
DEPTH = 2
NS = 16
H = 4
DK = 128
DV = 256
NKC = 8
BT = 512
GT = 4
NG = SEQ // BT
TW = BT + NS
SG = 0
ALPHA = (2 * DEPTH) ** 0.25
EPS = 1e-5
QSCALE = DK ** -0.5
O_Q, O_K, O_V, O_GG, O_ALR, O_CB, O_CC, O_CH, O_GC, O_MG, O_MC = 0, 512, 1024, 2048, 3072, 3088, 3600, 4112, 4624, 5136, 6160
NW = 8
WC = 256


class Ins:
    __slots__ = ("eng", "fn", "deps", "dma", "sig", "semid", "semval", "pos")

    def __init__(self, eng, fn, dma, pos):
        self.eng = eng
        self.fn = fn
        self.deps = []
        self.dma = dma
        self.sig = False
        self.semid = None
        self.semval = None
        self.pos = pos


class Prog:
    def __init__(self, n_dma_sems=8):
        self.lists = {e: [] for e in ENGS}
        self.last_w = {}
        self.rd_eng = {}
        self.rd_dma = {}
        self.n_dma_sems = n_dma_sems

    def add(self, eng, fn, reads=(), writes=(), excl=(), dma=False):
        I = Ins(eng, fn, dma, len(self.lists[eng]))
        cand = []
        writes = list(writes) + list(excl)
        for k in reads:
            w = self.last_w.get(k)
            if w is not None:
                cand.append((w, True))
        for k in writes:
            w = self.last_w.get(k)
            if w is not None:
                cand.append((w, False))
            for r in self.rd_eng.get(k, {}).values():
                cand.append((r, False))
            for r in self.rd_dma.get(k, ()):
                cand.append((r, False))
        for k in writes:
            self.last_w[k] = I
            self.rd_eng[k] = {}
            self.rd_dma[k] = []
        for k in reads:
            if dma:
                self.rd_dma.setdefault(k, []).append(I)
            else:
                self.rd_eng.setdefault(k, {})[eng] = I
        best = {}
        seen = set()
        for d, raw in cand:
            if d is I:
                continue
            if d.dma:
                if id(d) not in seen:
                    seen.add(id(d))
                    I.deps.append(d)
                continue
            if d.eng == eng and not dma:
                if eng == "pe":
                    continue
            b = best.get(d.eng)
            if b is None or d.pos > b.pos:
                best[d.eng] = d
        for d in best.values():
            d.sig = True
            I.deps.append(d)
        self.lists[eng].append(I)
        return I

    def finalize(self):
        for e in ENGS:
            cnt = 0
            dcnt = {}
            k = 0
            for I in self.lists[e]:
                if I.dma:
                    I.semid = k % self.n_dma_sems
                    k += 1
                    dcnt[I.semid] = dcnt.get(I.semid, 0) + 16
                    I.semval = dcnt[I.semid]
                elif I.sig:
                    cnt += 1
                    I.semval = cnt

    def run_engine(self, e, engobj, sems, dma_sems):
        waited = {}
        for I in self.lists[e]:
            need = {}
            for d in I.deps:
                if d.dma:
                    key = ("d", d.eng, d.semid)
                    s = dma_sems[d.eng][d.semid]
                else:
                    key = ("e", d.eng)
                    s = sems[d.eng]
                if need.get(key, (None, 0))[1] < d.semval:
                    need[key] = (s, d.semval)
            if I.dma and I.semval > 16:
                key = ("d", e, I.semid)
                if need.get(key, (None, 0))[1] < I.semval - 16:
                    need[key] = (dma_sems[e][I.semid], I.semval - 16)
            for key, (s, v) in need.items():
                if waited.get(key, 0) < v:
                    engobj.wait_ge(s, v)
                    waited[key] = v
            ins = getattr(engobj, I.fn[0])(**I.fn[1])
            if I.dma:
                ins.then_inc(dma_sems[e][I.semid], 16)
            elif I.sig:
                ins.then_inc(sems[e], 1)

    def final_wait(self, engobj, dma_sems):
        for e in ENGS:
            last = {}
            for I in self.lists[e]:
                if I.dma:
                    last[I.semid] = I.semval
            for sid, v in last.items():
                engobj.wait_ge(dma_sems[e][sid], v)


def build_program(ng_run=NG, nl_run=DEPTH):
    nc = bass.Bass("TRN2", target_bir_lowering=False)

    def din(name, shape):
        return nc.dram_tensor(name, list(shape), F32, kind="ExternalInput").ap()

    def dout(name, shape):
        return nc.dram_tensor(name, list(shape), F32, kind="ExternalOutput").ap()

    xp = din("xp", [SEQ, D])
    xs = din("xs", [NS, D])
    c17d = din("c17", [NS + 1, D])
    sgl = din("sgl", [DEPTH, NS, H, DK, DV])
    scv = din("scv", [DEPTH, NS, 2, 512])
    w_ada = din("w_ada", [DEPTH, D, 3 * D])
    b_ada = din("b_ada", [DEPTH, 3 * D])
    w_in = din("w_in", [DEPTH, D, 7184])
    w_a2 = din("w_a2", [DEPTH, 16, 512])
    b_a = din("b_a", [DEPTH, 512])
    gng = din("gng", [DEPTH, DV])
    conv_w = din("conv_w", [DEPTH, 3, 512])
    w_pa = din("w_pa", [DEPTH, D, D])
    w_pb = din("w_pb", [DEPTH, 512, D])
    w_o = din("w_o", [DEPTH, D, D])
    ln_g = din("ln_g", [DEPTH, D])
    ln_b = din("ln_b", [DEPTH, D])
    yp = dout("yp", [SEQ, D])
    ys = dout("ys", [NS, D])
    gp = dout("gp", [DEPTH, H, DK, DV])
    cp = dout("cp", [DEPTH, 2, 512])
    gs = dout("gs", [DEPTH, NS, H, DK, DV])
    cs = dout("cs", [DEPTH, NS, 2, 512])

    with ExitStack() as es:
        def sb(name, shape, dt):
            return es.enter_context(nc.sbuf_tensor(name, list(shape), dt))

        x_sb = sb("x_sb", [128, GT, D], F32)
        xs_sb = sb("xs_sb", [NS, D], F32)
        uT = sb("uT", [128, NKC, TW], BF16)
        og = sb("og", [128, NKC, TW], BF16)
        yc = sb("yc", [128, 4, TW], BF16)
        wring = sb("wring", [128, NW, NKC, WC], BF16)
        gate_bc = sb("gate_bc", [128, DEPTH, D], F32)
        gate_s = sb("gate_s", [NS, DEPTH, D], F32)
        lng_bc = sb("lng_bc", [128, D], F32)
        lnb_bc = sb("lnb_bc", [128, D], F32)
        S = sb("S", [128, DEPTH, H * DV], F32)
        S_bf = sb("S_bf", [128, DEPTH, H * DV], BF16)
        Sbuf = sb("Sbuf", [128, 3, H * DV], F32)
        sbf = sb("sbf", [128, 2, H * DV], BF16)
        ident_f = sb("ident_f", [128, 128], F32)
        ident_b = sb("ident_b", [128, 128], BF16)
        tri = sb("tri", [128, 128], F32)
        ones_b = sb("ones_b", [128, 128], BF16)
        alr_aug = sb("alr_aug", [17, TW], F32)
        w_a2aug = sb("w_a2aug", [17, DEPTH, 512], F32)
        gnorm = sb("gnorm", [128, DEPTH, 2], F32)
        convw = sb("convw", [128, DEPTH, 4, 3], F32)
        ada_sb = sb("ada_sb", [128, DEPTH, 16, 17], F32)
        b_adaT = sb("b_adaT", [128, DEPTH, 24], F32)
        carry = sb("carry", [128, DEPTH, 4, 2], F32)
        cT = sb("cT", [128, NKC, 17], BF16)
        tmp_s = sb("tmp_s", [128, NKC, NS], F32)
        st2 = sb("st2", [128, 2, 2, 6], F32)
        mv = sb("mv", [128, 16], F32)
        st_s = sb("st_s", [NS, 2, 6], F32)
        mv_s = sb("mv_s", [NS, 8], F32)
        dummy = sb("dummy_t", [128, 8], F32)
        ARENA_F = 16384 + 256 + 1024 + 256 + 512
        arena = sb("arena", [128, ARENA_F], F32)

        ps = es.enter_context(nc.psum_tensor("ps", [128, 8, 512], F32))

        sems = {e: es.enter_context(nc.semaphore(f"s_{e}")) for e in ENGS}
        dma_sems = {e: [es.enter_context(nc.semaphore(f"d_{e}{i}")) for i in range(8)]
                    for e in ("sp", "pool")}
        block = es.enter_context(nc.Block())

        class Arena:
            def __init__(self):
                self.off = 0

            def f32(self, n, parts=128):
                a = arena[0:parts, self.off:self.off + n]
                self.off += n
                assert self.off <= ARENA_F, self.off
                return a

            def bf16(self, n, parts=128):
                nf = (n + 1) // 2
                a = arena[0:parts, self.off:self.off + nf].bitcast(BF16)
                self.off += nf
                assert self.off <= ARENA_F, self.off
                return a

        def v3(ap, c):
            return ap.rearrange("p (c t) -> p c t", c=c)

        A = Arena()
        E_q = v3(A.f32(4 * BT), 4)
        E_k = v3(A.f32(4 * BT), 4)
        q_in = v3(A.bf16(4 * BT), 4)
        k_in = v3(A.bf16(4 * BT), 4)
        sgt = v3(A.bf16(8 * TW), 8)
        v_tok = v3(A.bf16(GT * 1024), GT)
        k_tok = v3(A.bf16(2 * 512), 2)
        _o = A.off
        e1 = A.f32(512)
        A.off = _o
        ktok_s = A.f32(512, NS)
        sp = v3(A.f32(2 * 512), 2)
        v_s = A.bf16(1024, NS)
        attm = v3(A.bf16(2 * 512), 2)
        sq = v3(A.bf16(8 * 128), 8)
        lnv = A.f32(512)
        rstd2 = A.f32(2 * 512).rearrange("p (b h t) -> p b h t", b=2, h=4)
        t1 = v3(A.f32(1024), 8)
        Pst = A.f32(1024)
        q_s = v3(A.bf16(4 * NS), 4)
        a_s = v3(A.f32(4 * NS), 4)
        km2 = v3(A.bf16(2 * 512, NS), 2)
        sq_s = v3(A.bf16(8 * NS), 8)
        lnv_s = A.f32(4 * NS)
        rstd_s = v3(A.f32(4 * NS), 4)
        t1_s = v3(A.f32(8 * NS), 8)
        g_end = A.off
        A = Arena()
        c17 = A.f32(D, NS + 1)
        cTb = v3(A.bf16(NKC * 128), NKC)
        bgate = A.f32(D)
        A = Arena()
        ccy = v3(A.f32(4 * TW), 4)
        pbuf = v3(A.f32(4 * (BT + 2)), 4)
        tmp1 = A.f32(512)
        tmp2 = A.f32(512)
        tmp3 = A.f32(512)
        p_s = v3(A.f32(4 * NS), 4)
        scT = v3(A.f32(4 * 2 * NS), 4)
        sc_tok = A.f32(512, 2 * NS)
        tcs1 = A.f32(NS)
        tcs2 = A.f32(NS)
        tmp3s = v3(A.f32(4 * NS), 4)
        cst = A.f32(512, 2)
        pst = A.f32(512, NS)
        sgm = A.f32(512)
        scm = A.f32(512)
        tm1 = A.f32(512)
        tm2 = A.f32(512)
        sg_s = A.f32(2 * NS)
        t_s = A.f32(2 * NS)
        mg = v3(A.bf16(NKC * TW), NKC)
        to2 = A.f32(2 * D).rearrange("p (b n) -> p b n", b=2)
        junk = A.bf16(D)
        to_s = A.f32(D, NS)
        cmo_end = A.off

        def emit_all(P, record, wlist_in):
            def PE(meth, kw, r=(), x=()):
                return P.add("pe", (meth, kw), reads=r, excl=x)

            def ACT(meth, kw, r=(), w=(), x=()):
                return P.add("act", (meth, kw), reads=r, writes=w, excl=x)

            def DVE(meth, kw, r=(), w=(), x=()):
                return P.add("dve", (meth, kw), reads=r, writes=w, excl=x)

            def POOL(meth, kw, r=(), w=()):
                return P.add("pool", (meth, kw), reads=r, writes=w)

            def DMA(q, meth, kw, r=(), w=()):
                return P.add(q, (meth, kw), reads=r, writes=w, dma=True)

            EP = "EPOCH"

            def ar(*keys):
                return [EP] + ["A:" + k for k in keys]

            def aw(*keys):
                return ["A:" + k for k in keys]

            def epoch_barrier():
                DVE("memset", dict(ap=dummy[:, 0:1], constant=0.0), w=[EP])

            free_banks = list(range(8))

            def bget():
                assert free_banks, "PSUM banks exhausted"
                return free_banks.pop(0)

            def bget2():
                for i in range(len(free_banks)):
                    b = free_banks[i]
                    if b % 2 == 0 and (b + 1) in free_banks:
                        free_banks.remove(b)
                        free_banks.remove(b + 1)
                        return b
                raise AssertionError("no PSUM bank pair free")

            def bput(b, n=1):
                for i in range(n):
                    free_banks.append(b + i)

            def BK(b):
                return ("B", b)

            def mm(out, lhsT, rhs, start, stop, r, b):
                PE("matmul", dict(out=out, lhsT=lhsT, rhs=rhs, start=start, stop=stop), r=r, x=[BK(b)])

            def tp(out, in_, ident, r, b):
                PE("transpose", dict(out=out, in_=in_, identity=ident), r=r, x=[BK(b)])

            wlist = [] if record else wlist_in
            OFFS = {"q": O_Q, "k": O_K, "v": O_V, "gg": O_GG, "cc": O_CC, "ch": O_CH, "cb": O_CB, "gc": O_GC, "mg": O_MG, "mc": O_MC}

            def wspec(tag):
                nm = tag[0]
                if nm == "ada":
                    return w_ada[tag[1], :, tag[2] * WC:(tag[2] + 1) * WC], NKC, WC
                if nm == "alr":
                    return w_in[tag[2], :, O_ALR:O_ALR + 16], NKC, 16
                l, i = tag[2], tag[3]
                if nm in OFFS:
                    return w_in[l, :, OFFS[nm] + i * WC: OFFS[nm] + (i + 1) * WC], NKC, WC
                if nm == "pa":
                    return w_pa[l, :, i * WC:(i + 1) * WC], NKC, WC
                if nm == "pb":
                    return w_pb[l, :, i * WC:(i + 1) * WC], 4, WC
                if nm == "o":
                    return w_o[l, :, i * WC:(i + 1) * WC], NKC, WC
                raise KeyError(tag)

            wstate = {"issued": 0, "taken": 0}
            released = [False] * (100000 if record else len(wlist))

            def wpump():
                while wstate["issued"] < len(wlist):
                    j = wstate["issued"]
                    if j >= NW and not released[j - NW]:
                        break
                    src, nkc, ncols = wspec(wlist[j])
                    slot = j % NW
                    dst = wring[:, slot, 0:nkc, 0:ncols]
                    srcv = src.rearrange("(kc p) c -> p kc c", p=128)
                    DMA("pool", "dma_start", dict(out=dst, in_=srcv), w=[("W", slot)])
                    wstate["issued"] += 1

            def wtake(tag):
                i = wstate["taken"]
                if record:
                    wlist.append(tag)
                    wstate["taken"] += 1
                    return i, wring[:, i % NW], ("W", i % NW)
                assert wlist[i] == tag, (wlist[i], tag)
                wpump()
                assert wstate["issued"] > i, "weight ring deadlock"
                wstate["taken"] += 1
                return i, wring[:, i % NW], ("W", i % NW)

            def wrel(i):
                released[i] = True
                if not record:
                    wpump()

            POOL("memset", dict(ap=ident_f[:], constant=0.0), w=["ident_f"])
            POOL("affine_select", dict(out=ident_f[:], in_=ident_f[:], pattern=[[-1, 128]], compare_op=ALU.not_equal,
                                           fill=1.0, base=0, channel_multiplier=1), r=["ident_f"], w=["ident_f"])
            POOL("tensor_copy", dict(out=ident_b[:], in_=ident_f[:]), r=["ident_f"], w=["ident_b"])
            POOL("memset", dict(ap=tri[:], constant=1.0), w=["tri"])
            POOL("affine_select", dict(out=tri[:], in_=tri[:], pattern=[[1, 128]], compare_op=ALU.is_ge,
                                           fill=0.0, base=0, channel_multiplier=-1), r=["tri"], w=["tri"])
            POOL("memset", dict(ap=ones_b[:], constant=1.0), w=["ones_b"])
            POOL("memset", dict(ap=alr_aug[:], constant=1.0), w=["alr_aug"])
            POOL("memset", dict(ap=carry[:], constant=0.0), w=["carry"])
            POOL("memset", dict(ap=S[:], constant=0.0), w=[("S", l_, h_) for l_ in range(DEPTH) for h_ in range(H)])
            POOL("memset", dict(ap=S_bf[:], constant=0.0), w=[("S_bf", l_, h_) for l_ in range(DEPTH) for h_ in range(H)])

            DMA("sp", "dma_start", dict(out=c17, in_=c17d), w=aw("c17"))
            DMA("sp", "dma_start", dict(out=xs_sb[:], in_=xs), w=["xs"])
            for l in range(DEPTH):
                DMA("sp", "dma_start", dict(out=w_a2aug[0:16, l, :], in_=w_a2[l]), w=["w_a2aug"])
                DMA("sp", "dma_start", dict(out=w_a2aug[16:17, l, :], in_=b_a[l:l + 1, :]), w=["w_a2aug"])
                DMA("sp", "dma_start", dict(out=gnorm[:, l, :], in_=gng[l].rearrange("(c p) -> p c", p=128),
                                                     allow_slow_non_contiguous=True), w=["gnorm"])
                for ti in range(3):
                    DMA("sp", "dma_start", dict(out=convw[:, l, :, ti], in_=conv_w[l, ti].rearrange("(c p) -> p c", p=128),
                                                allow_slow_non_contiguous=True), w=["convw"])
                DMA("sp", "dma_start", dict(out=b_adaT[:, l, :], in_=b_ada[l].rearrange("(f p) -> p f", p=128),
                                                     allow_slow_non_contiguous=True), w=["b_adaT"])

            b = bget()
            for c in range(NKC):
                tp(ps[:, b, c * 17:(c + 1) * 17], c17[:, c * 128:(c + 1) * 128], ident_f[0:17, 0:17], ["ident_f"] + ar("c17"), b)
            ACT("copy", dict(out=cT[:].rearrange("p c t -> p (c t)"), in_=ps[:, b, 0:NKC * 17]), w=["cT"], x=[BK(b)])
            bput(b)
            for c in range(NKC):
                DVE("tensor_scalar", dict(out=cTb[:, c, :], in0=ones_b[:], scalar1=cT[:, c, 0:1], scalar2=None, op0=ALU.mult),
                    r=["ones_b", "cT"] + ar(), w=aw("cTb"))
            for l in range(DEPTH):
                DVE("tensor_scalar", dict(out=b_adaT[:, l, 8:16], in0=b_adaT[:, l, 8:16], scalar1=1.0, scalar2=None, op0=ALU.add),
                    r=["b_adaT"], w=["b_adaT"])
                DMA("sp", "dma_start", dict(out=bgate, in_=b_ada[l, 2 * D:3 * D].partition_broadcast(128)), w=aw("bgate"))
                b = bget()
                for wi in range(8):
                    i, slot, wk = wtake(("ada", l, wi))
                    for sub in range(2):
                        fc = wi * 2 + sub
                        for kc in range(NKC):
                            mm(ps[:, b, fc * 17:(fc + 1) * 17], slot[:, kc, sub * 128:(sub + 1) * 128], cT[:, kc, :],
                               kc == 0, kc == NKC - 1, [wk, "cT"], b)
                    wrel(i)
                DVE("tensor_tensor", dict(out=ada_sb[:, l, :, :], in0=ps[:, b, 0:16 * 17].rearrange("p (f t) -> p f t", f=16),
                                                       in1=b_adaT[:, l, 0:16].unsqueeze(2).to_broadcast([128, 16, 17]), op=ALU.add),
                    r=["b_adaT"], w=["ada_sb"], x=[BK(b)])
                bput(b)
                for half in range(2):
                    bp = bget()
                    bs = bget()
                    for wj in range(2):
                        wi = 8 + half * 2 + wj
                        i, slot, wk = wtake(("ada", l, wi))
                        for kc in range(NKC):
                            mm(ps[:, bp, wj * WC:(wj + 1) * WC], cTb[:, kc, :], slot[:, kc, :], kc == 0, kc == NKC - 1, [wk] + ar("cTb"), bp)
                        for kc in range(NKC):
                            mm(ps[0:NS, bs, wj * WC:(wj + 1) * WC], cT[:, kc, 1:17], slot[:, kc, :], kc == 0, kc == NKC - 1, [wk, "cT"], bs)
                        wrel(i)
                    DVE("tensor_tensor", dict(out=gate_bc[:, l, half * 512:(half + 1) * 512], in0=ps[:, bp, :],
                                                                        in1=bgate[:, half * 512:(half + 1) * 512], op=ALU.add),
                        r=ar("bgate"), w=["gate_bc"], x=[BK(bp)])
                    DVE("tensor_tensor", dict(out=gate_s[:, l, half * 512:(half + 1) * 512], in0=ps[0:NS, bs, :],
                                                                        in1=bgate[0:NS, half * 512:(half + 1) * 512], op=ALU.add),
                        r=ar("bgate"), w=["gate_s"], x=[BK(bs)])
                    bput(bp)
                    bput(bs)

            for g in range(ng_run):
                has_s = (g == SG)
                for l in range(nl_run):
                    last_g = (g == ng_run - 1)
                    if l == 0:
                        for j in range(GT):
                            DMA("sp", "dma_start", dict(out=x_sb[:, j, :], in_=xp[g * BT + j * 128: g * BT + (j + 1) * 128, :]), w=[("x", j)])
                    DMA("sp", "dma_start", dict(out=lng_bc[:], in_=ln_g[l].partition_broadcast(128)), w=["lng"])
                    DMA("sp", "dma_start", dict(out=lnb_bc[:], in_=ln_b[l].partition_broadcast(128)), w=["lnb"])

                    for c in range(NKC):
                        b = bget()
                        for j in range(GT):
                            tp(ps[:, b, j * 128:(j + 1) * 128], x_sb[:, j, c * 128:(c + 1) * 128], ident_f[:], ["ident_f", ("x", j)], b)
                        if c % 2 == 0:
                            ACT("activation", dict(out=uT[:, c, 0:BT], in_=ps[:, b, :], func=AF.Identity,
                                                   bias=ada_sb[:, l, c, 0:1], scale=ada_sb[:, l, 8 + c, 0:1]),
                                r=["ada_sb"], w=[("uT", c)], x=[BK(b)])
                        else:
                            DVE("tensor_scalar", dict(out=uT[:, c, 0:BT], in0=ps[:, b, :], scalar1=ada_sb[:, l, 8 + c, 0:1], scalar2=ada_sb[:, l, c, 0:1],
                                                      op0=ALU.mult, op1=ALU.add),
                                r=["ada_sb"], w=[("uT", c)], x=[BK(b)])
                        bput(b)
                    if has_s:
                        b = bget()
                        for c in range(NKC):
                            tp(ps[:, b, c * NS:(c + 1) * NS], xs_sb[:, c * 128:(c + 1) * 128], ident_f[0:NS, 0:NS], ["ident_f", "xs"], b)
                        DVE("tensor_tensor", dict(out=tmp_s[:], in0=ps[:, b, 0:NKC * NS].rearrange("p (c t) -> p c t", c=NKC),
                                                               in1=ada_sb[:, l, 8:16, 1:17], op=ALU.mult),
                            r=["ada_sb"], w=["tmp_s"], x=[BK(b)])
                        bput(b)
                        DVE("tensor_tensor", dict(out=uT[:, :, BT:TW], in0=tmp_s[:], in1=ada_sb[:, l, 0:8, 1:17], op=ALU.add),
                            r=["ada_sb", "tmp_s"], w=["uTs"])

                    uTr = [("uT", c) for c in range(NKC)]

                    epoch_barrier()
                    i, slot, wk = wtake(("alr", g, l))
                    b = bget()
                    for kc in range(NKC):
                        mm(ps[0:16, b, :], slot[:, kc, 0:16], uT[:, kc, 0:BT], kc == 0, kc == NKC - 1, [wk, ("uT", kc)], b)
                    ACT("copy", dict(out=alr_aug[0:16, 0:BT], in_=ps[0:16, b, :]), w=["alr_aug"], x=[BK(b)])
                    bput(b)
                    if has_s:
                        b = bget()
                        for kc in range(NKC):
                            mm(ps[0:16, b, 0:NS], slot[:, kc, 0:16], uT[:, kc, BT:TW], kc == 0, kc == NKC - 1, [wk, "uTs"], b)
                        ACT("copy", dict(out=alr_aug[0:16, BT:TW], in_=ps[0:16, b, 0:NS]), w=["alr_aug_s"], x=[BK(b)])
                        bput(b)
                    wrel(i)
                    bvs_h = [None]
                    bgs_h = [None]

                    def do_v(h):
                        i, slot, wk = wtake(("v", g, l, h))
                        for jp in range(2):
                            b = bget()
                            for jj in range(2):
                                j = jp * 2 + jj
                                for kc in range(NKC):
                                    mm(ps[:, b, jj * WC:(jj + 1) * WC], uT[:, kc, j * 128:(j + 1) * 128], slot[:, kc, :], kc == 0, kc == NKC - 1,
                                       [wk, ("uT", kc)], b)
                            DVE("tensor_copy", dict(out=v_tok[:, jp * 2:jp * 2 + 2, h * DV:(h + 1) * DV],
                                                    in_=ps[:, b, :].rearrange("p (j v) -> p j v", j=2)),
                                r=ar(), w=aw(f"v_tok{jp * 2}", f"v_tok{jp * 2 + 1}"), x=[BK(b)])
                            bput(b)
                        if has_s:
                            if h % 2 == 0:
                                bvs_h[0] = bget()
                            bvs = bvs_h[0]
                            for kc in range(NKC):
                                mm(ps[0:NS, bvs, (h % 2) * WC:(h % 2 + 1) * WC], uT[:, kc, BT:TW], slot[:, kc, :], kc == 0, kc == NKC - 1, [wk, "uTs"], bvs)
                            if h % 2 == 1:
                                ACT("copy", dict(out=v_s[:, (h - 1) * DV:(h + 1) * DV], in_=ps[0:NS, bvs, :]),
                                    r=ar(), w=aw("v_s"), x=[BK(bvs)])
                                bput(bvs)
                        wrel(i)

                    def do_gg(wi):
                        if has_s and wi == 0:
                            bgs_h[0] = bget()
                        bgs = bgs_h[0]
                        i, slot, wk = wtake(("gg", g, l, wi))
                        for sub in range(2):
                            fc = wi * 2 + sub
                            b = bget()
                            for kc in range(NKC):
                                mm(ps[:, b, :], slot[:, kc, sub * 128:(sub + 1) * 128], uT[:, kc, 0:BT], kc == 0, kc == NKC - 1, [wk, ("uT", kc)], b)
                            ACT("activation", dict(out=sgt[:, fc, 0:BT], in_=ps[:, b, :], func=AF.Silu), r=ar(), w=aw(f"sg{fc}"), x=[BK(b)])
                            bput(b)
                            if has_s:
                                for kc in range(NKC):
                                    mm(ps[:, bgs, fc * NS:(fc + 1) * NS], slot[:, kc, sub * 128:(sub + 1) * 128], uT[:, kc, BT:TW],
                                       kc == 0, kc == NKC - 1, [wk, "uTs"], bgs)
                        wrel(i)
                        if has_s and wi == 3:
                            ACT("activation", dict(out=sgt[:, :, BT:TW], in_=ps[:, bgs, 0:NKC * NS].rearrange("p (c t) -> p c t", c=NKC), func=AF.Silu),
                                r=ar(), w=aw("sg_s"), x=[BK(bgs)])
                            bput(bgs)

                    bulk = [(do_v, h) for h in range(H)] + [(do_gg, wi) for wi in range(4)]
                    EQK = aw(*[f"E_q{h}" for h in range(H)])
                    EKK = aw(*[f"E_k{h}" for h in range(H)])
                    for j in range(GT):
                        jj = j % 2
                        b = bget()
                        mm(ps[:, b, :], alr_aug[0:17, j * 128:(j + 1) * 128], w_a2aug[0:17, l, :], True, True, ["alr_aug", "w_a2aug"], b)
                        ACT("activation", dict(out=e1, in_=ps[:, b, :], func=AF.Exp, scale=-1.0), r=ar(), w=aw("e1"), x=[BK(b)])
                        bput(b)
                        ACT("activation", dict(out=sp[:, jj, :], in_=e1, func=AF.Ln, bias=1.0), r=ar("e1"), w=aw("sp"))
                        f_, a_ = bulk.pop(0)
                        f_(a_)
                        bc = bget()
                        for h in range(H):
                            mm(ps[:, bc, h * 128:(h + 1) * 128], sp[:, jj, h * 128:(h + 1) * 128], tri[:], True, True, ["tri"] + ar("sp"), bc)
                        csv = ps[:, bc, :].rearrange("p (h t) -> p h t", h=H)
                        ACT("activation", dict(out=E_q[:, :, j * 128:(j + 1) * 128], in_=csv, func=AF.Exp, scale=-1.0 / 16), r=ar(), w=EQK, x=[BK(bc)])
                        ACT("activation", dict(out=E_k[:, :, j * 128:(j + 1) * 128], in_=csv, func=AF.Exp, scale=1.0 / 16), r=ar(), w=EKK, x=[BK(bc)])
                        bput(bc)
                    if has_s:
                        b = bget()
                        mm(ps[0:NS, b, :], alr_aug[0:17, BT:TW], w_a2aug[0:17, l, :], True, True, ["alr_aug_s", "w_a2aug"], b)
                        ACT("activation", dict(out=e1[0:NS, :], in_=ps[0:NS, b, :], func=AF.Exp, scale=-1.0), r=ar(), w=aw("e1"), x=[BK(b)])
                        bput(b)
                        ACT("activation", dict(out=sp[0:NS, 0, :], in_=e1[0:NS, :], func=AF.Ln, bias=1.0), r=ar("e1"), w=aw("sp"))
                        b = bget()
                        for h in range(H):
                            tp(ps[:, b, h * NS:(h + 1) * NS], sp[0:NS, 0, h * 128:(h + 1) * 128], ident_f[0:NS, 0:NS], ["ident_f"] + ar("sp"), b)
                        ACT("activation", dict(out=a_s[:].rearrange("p h s -> p (h s)"), in_=ps[:, b, 0:H * NS], func=AF.Exp, scale=-1.0 / 16),
                            r=ar(), w=aw("a_s"), x=[BK(b)])
                        bput(b)
                    while bulk:
                        f_, a_ = bulk.pop(0)
                        f_(a_)
                    for nm, Eb, dst in (("q", E_q, q_in), ("k", E_k, k_in)):
                        bs = bget() if (has_s and nm == "q") else None
                        bkt = bget() if (has_s and nm == "k") else None
                        for wi in range(2):
                            i, slot, wk = wtake((nm, g, l, wi))
                            for sub in range(2):
                                h = wi * 2 + sub
                                b = bget()
                                for kc in range(NKC):
                                    mm(ps[:, b, :], slot[:, kc, sub * 128:(sub + 1) * 128], uT[:, kc, 0:BT], kc == 0, kc == NKC - 1, [wk, ("uT", kc)], b)
                                if nm == "q":
                                    DVE("scalar_tensor_tensor", dict(out=q_in[:, h, :], in0=ps[:, b, :], scalar=QSCALE, in1=E_q[:, h, :],
                                                                                   op0=ALU.mult, op1=ALU.mult),
                                        r=ar(f"E_q{h}"), w=aw(f"q_in{h}"), x=[BK(b)])
                                else:
                                    DVE("tensor_tensor", dict(out=k_in[:, h, :], in0=ps[:, b, :], in1=E_k[:, h, :], op=ALU.mult),
                                        r=ar(f"E_k{h}"), w=aw(f"k_in{h}"), x=[BK(b)])
                                bput(b)
                                if has_s and nm == "q":
                                    for kc in range(NKC):
                                        mm(ps[:, bs, h * NS:(h + 1) * NS], slot[:, kc, sub * 128:(sub + 1) * 128], uT[:, kc, BT:TW],
                                           kc == 0, kc == NKC - 1, [wk, "uTs"], bs)
                            if has_s and nm == "k":
                                for kc in range(NKC):
                                    mm(ps[0:NS, bkt, wi * WC:(wi + 1) * WC], uT[:, kc, BT:TW], slot[:, kc, :], kc == 0, kc == NKC - 1, [wk, "uTs"], bkt)
                            wrel(i)
                        if has_s and nm == "q":
                            ACT("activation", dict(out=q_s[:].rearrange("p h s -> p (h s)"), in_=ps[:, bs, 0:H * NS], func=AF.Identity, scale=QSCALE),
                                r=ar(), w=aw("q_s"), x=[BK(bs)])
                        if has_s and nm == "k":
                            ACT("copy", dict(out=ktok_s, in_=ps[0:NS, bkt, :]), r=ar(), w=aw("e1"), x=[BK(bkt)])
                            bput(bkt)
                        if has_s and nm == "q":
                            bput(bs)
                    bos = bget() if has_s else None

                    def sample_load(s):
                        DMA("sp", "dma_start", dict(out=Sbuf[:, s % 3, :].rearrange("p (h v) -> p h v", h=H), in_=sgl[l, s].rearrange("h k v -> k h v")),
                            w=[("Sbuf", s % 3)])

                    if has_s:
                        for s_ in range(3):
                            sample_load(s_)

                    def stage_A(j):
                        jj = j % 2
                        jc0, jc1 = j * 128, (j + 1) * 128
                        b = bget()
                        psb = ps[:, b, :].bitcast(BF16)
                        for h in range(H):
                            tp(psb[:, h * 128:(h + 1) * 128], k_in[:, h, jc0:jc1], ident_b[:], ["ident_b"] + ar(f"k_in{h}"), b)
                        ACT("copy", dict(out=k_tok[:, jj, :], in_=psb[:, 0:512]), r=ar(), w=aw(f"k_tok{jj}"), x=[BK(b)])
                        bput(b)
                        b = bget()
                        for h in range(H):
                            mm(ps[:, b, h * 128:(h + 1) * 128], k_in[:, h, jc0:jc1], q_in[:, h, jc0:jc1], True, True, ar(f"k_in{h}", f"q_in{h}"), b)
                        DVE("tensor_tensor", dict(out=attm[:, jj, :].rearrange("p (h t) -> p h t", h=H), in0=ps[:, b, :].rearrange("p (h t) -> p h t", h=H),
                                                  in1=tri[:].unsqueeze(1).to_broadcast([128, H, 128]), op=ALU.mult),
                            r=["tri"] + ar(), w=aw(f"attm{jj}"), x=[BK(b)])
                        bput(b)

                    def samp_A(s):
                        kmv = km2[:, s % 2, :]
                        DVE("tensor_scalar", dict(out=kmv, in0=ktok_s, scalar1=ident_f[0:NS, s:s + 1], scalar2=None, op0=ALU.mult),
                            r=["ident_f"] + ar("e1"), w=aw(f"km{s % 2}"))
                        b2 = bget2()
                        for h in range(H):
                            mm(ps[:, b2 + h // 2, (h % 2) * DV:(h % 2 + 1) * DV], kmv[:, h * 128:(h + 1) * 128], v_s[:, h * DV:(h + 1) * DV], True, True,
                               ar(f"km{s % 2}", "v_s"), b2 + h // 2)
                        return b2

                    def samp_B(s, b2):
                        buf = Sbuf[:, s % 3, :]
                        bk = ("Sbuf", s % 3)
                        sbv = sbf[:, s % 2, :]
                        sk = ("sbf", s % 2)
                        for h in range(H):
                            ACT("activation", dict(out=buf[:, h * DV:(h + 1) * DV], in_=buf[:, h * DV:(h + 1) * DV], func=AF.Identity,
                                                   scale=a_s[:, h, s:s + 1]),
                                r=[bk] + ar("a_s"), w=[bk])
                        for half in range(2):
                            DVE("tensor_tensor", dict(out=buf[:, half * 512:(half + 1) * 512], in0=ps[:, b2 + half, :],
                                                      in1=buf[:, half * 512:(half + 1) * 512], op=ALU.add),
                                r=[bk], w=[bk], x=[BK(b2 + half)])
                        bput(b2, 2)
                        DMA("sp", "dma_start", dict(out=gs[l, s].rearrange("h k v -> k h v"), in_=buf.rearrange("p (h v) -> p h v", h=H)),
                            r=[bk], w=[("gs", l, s)])
                        DVE("tensor_copy", dict(out=sbv, in_=buf), r=[bk], w=[sk])
                        if s + 3 < NS:
                            sample_load(s + 3)
                        for h in range(H):
                            for vc in range(2):
                                idx = h * 2 + vc
                                mm(ps[:, bos, idx * NS + s: idx * NS + s + 1], sbv[:, idx * 128:(idx + 1) * 128], q_s[:, h, s:s + 1], True, True,
                                   [sk] + ar("q_s"), bos)

                    stage_A(0)
                    pend_norm = [None]
                    for j in range(GT):
                        jj = j % 2
                        jc0, jc1 = j * 128, (j + 1) * 128
                        bo = bget2()
                        for h in range(H):
                            for vc in range(2):
                                idx = h * 2 + vc
                                out = ps[:, bo + idx // 4, (idx % 4) * 128:(idx % 4 + 1) * 128]
                                mm(out, v_tok[:, j, idx * 128:(idx + 1) * 128], attm[:, jj, h * 128:(h + 1) * 128], True, False,
                                   ar(f"v_tok{j}", f"attm{jj}"), bo + idx // 4)
                                mm(out, S_bf[:, l, idx * 128:(idx + 1) * 128], q_in[:, h, jc0:jc1], False, True, [("S_bf", l, h)] + ar(f"q_in{h}"), bo + idx // 4)
                        b2 = bget2()
                        for h in range(H):
                            mm(ps[:, b2 + h // 2, (h % 2) * DV:(h % 2 + 1) * DV], k_tok[:, jj, h * 128:(h + 1) * 128], v_tok[:, j, h * DV:(h + 1) * DV], True, True,
                               ar(f"k_tok{jj}", f"v_tok{j}"), b2 + h // 2)
                        for half in range(2):
                            DVE("tensor_tensor", dict(out=Pst[:, half * 512:(half + 1) * 512], in0=ps[:, b2 + half, :],
                                                      in1=S[:, l, half * 512:(half + 1) * 512], op=ALU.add),
                                r=[("S", l, 2 * half), ("S", l, 2 * half + 1)] + ar(), w=aw(f"P{half}"), x=[BK(b2 + half)])
                        bput(b2, 2)
                        for h in range(H):
                            dcol = E_q[:, h, jc1 - 1:jc1]
                            ACT("activation", dict(out=S_bf[:, l, h * DV:(h + 1) * DV], in_=Pst[:, h * DV:(h + 1) * DV], func=AF.Identity, scale=dcol),
                                r=ar(f"E_q{h}", f"P{h // 2}"), w=[("S_bf", l, h)])
                        for h in range(H):
                            dcol = E_q[:, h, jc1 - 1:jc1]
                            DVE("tensor_scalar", dict(out=S[:, l, h * DV:(h + 1) * DV], in0=Pst[:, h * DV:(h + 1) * DV], scalar1=dcol, scalar2=None,
                                                      op0=ALU.mult),
                                r=ar(f"E_q{h}", f"P{h // 2}"), w=[("S", l, h)])
                        if j + 1 < GT:
                            stage_A(j + 1)
                        ov = ps[:, bo:bo + 2, :].rearrange("p b (c t) -> p (b c) t", c=4)
                        ACT("activation", dict(out=sq[:], in_=ov, func=AF.Square), r=ar(), w=aw("sq"), x=[BK(bo), BK(bo + 1)])
                        bss = bget()
                        for h in range(H):
                            for vc in range(2):
                                mm(ps[:, bss, h * 128:(h + 1) * 128], ones_b[:], sq[:, h * 2 + vc, :], vc == 0, vc == 1, ["ones_b"] + ar("sq"), bss)
                        ACT("activation", dict(out=lnv, in_=ps[:, bss, :], func=AF.Ln, bias=EPS, scale=1.0 / DV), r=ar(), w=aw("lnv"), x=[BK(bss)])
                        bput(bss)
                        ACT("activation", dict(out=rstd2[:, jj].rearrange("p h t -> p (h t)"), in_=lnv, func=AF.Exp, scale=-0.5), r=ar("lnv"), w=aw(f"rstd{jj}"))
                        if pend_norm[0] is not None:
                            pend_norm[0]()

                        def norm_dve(j=j, jj=jj, bo=bo, ov=ov, jc0=jc0, jc1=jc1):
                            for vc in range(2):
                                DVE("scalar_tensor_tensor", dict(out=t1[:, vc::2, :], in0=ov[:, vc::2, :], scalar=gnorm[:, l, vc:vc + 1], in1=rstd2[:, jj],
                                                                 op0=ALU.mult, op1=ALU.mult),
                                    r=["gnorm"] + ar(f"rstd{jj}"), w=aw("t1"), x=[BK(bo), BK(bo + 1)])
                            bput(bo, 2)
                            DVE("tensor_tensor", dict(out=og[:, :, jc0:jc1], in0=t1[:], in1=sgt[:, :, jc0:jc1], op=ALU.mult),
                                r=ar("t1", *[f"sg{fc}" for fc in range(8)]), w=[("og", j)])

                        pend_norm[0] = norm_dve
                        if has_s:
                            s0 = j * 4
                            b2n = samp_A(s0)
                            for s in range(s0, s0 + 4):
                                b2c = b2n
                                if s + 1 < s0 + 4:
                                    b2n = samp_A(s + 1)
                                samp_B(s, b2c)
                    pend_norm[0]()
                    pend_norm[0] = None
                    if has_s:
                        ovs = ps[:, bos, 0:NKC * NS].rearrange("p (c t) -> p c t", c=NKC)
                        ACT("activation", dict(out=sq_s[:], in_=ovs, func=AF.Square), r=ar(), w=aw("sq_s"), x=[BK(bos)])
                        bss = bget()
                        for h in range(H):
                            for vc in range(2):
                                mm(ps[:, bss, h * NS:(h + 1) * NS], ones_b[:], sq_s[:, h * 2 + vc, :], vc == 0, vc == 1, ["ones_b"] + ar("sq_s"), bss)
                        ACT("activation", dict(out=lnv_s, in_=ps[:, bss, 0:H * NS], func=AF.Ln, bias=EPS, scale=1.0 / DV), r=ar(), w=aw("lnv_s"), x=[BK(bss)])
                        bput(bss)
                        ACT("activation", dict(out=rstd_s[:].rearrange("p h t -> p (h t)"), in_=lnv_s, func=AF.Exp, scale=-0.5), r=ar("lnv_s"), w=aw("rstd_s"))
                        for vc in range(2):
                            DVE("scalar_tensor_tensor", dict(out=t1_s[:, vc::2, :], in0=ovs[:, vc::2, :], scalar=gnorm[:, l, vc:vc + 1], in1=rstd_s[:],
                                                                                 op0=ALU.mult, op1=ALU.mult),
                                r=["gnorm"] + ar("rstd_s"), w=aw("t1_s"), x=[BK(bos)])
                        bput(bos)
                        DVE("tensor_tensor", dict(out=og[:, :, BT:TW], in0=t1_s[:], in1=sgt[:, :, BT:TW], op=ALU.mult), r=ar("t1_s", "sg_s"), w=["og_s"])
                    if last_g:
                        DMA("sp", "dma_start", dict(out=gp[l].rearrange("h k v -> k h v"), in_=S[:, l, :].rearrange("p (h v) -> p h v", h=H)),
                            r=[("S", l, h_) for h_ in range(H)], w=[("gp", l)])

                    epoch_barrier()
                    DVE("tensor_copy", dict(out=pbuf[:, :, 0:2], in_=carry[:, l, :, :]), r=["carry"] + ar(), w=aw("pb0", "pb1", "pb2", "pb3"))
                    if has_s:
                        DMA("sp", "dma_start", dict(out=sc_tok, in_=scv[l].rearrange("s r d -> (s r) d")), r=[EP], w=aw("sc_tok"))
                        b = bget()
                        for c in range(4):
                            tp(ps[:, b, c * 2 * NS:(c + 1) * 2 * NS], sc_tok[:, c * 128:(c + 1) * 128], ident_f[0:2 * NS, 0:2 * NS], ["ident_f"] + ar("sc_tok"), b)
                        ACT("copy", dict(out=scT[:].rearrange("p c t -> p (c t)"), in_=ps[:, b, 0:4 * 2 * NS]), r=ar(), w=aw("scT"), x=[BK(b)])
                        bput(b)
                        DMA("sp", "dma_start", dict(out=cs[l, :, 0, :], in_=scv[l, :, 1, :]), w=[("cs0", l)])

                    def conv_mm(nm, post, post_s):
                        bs = bget() if has_s else None
                        for wi in range(2):
                            i, slot, wk = wtake((nm, g, l, wi))
                            for sub in range(2):
                                c = wi * 2 + sub
                                b = bget()
                                for kc in range(NKC):
                                    mm(ps[:, b, :], slot[:, kc, sub * 128:(sub + 1) * 128], uT[:, kc, 0:BT], kc == 0, kc == NKC - 1, [wk, ("uT", kc)], b)
                                post(c, b)
                                bput(b)
                                if has_s:
                                    for kc in range(NKC):
                                        mm(ps[:, bs, c * NS:(c + 1) * NS], slot[:, kc, sub * 128:(sub + 1) * 128], uT[:, kc, BT:TW],
                                           kc == 0, kc == NKC - 1, [wk, "uTs"], bs)
                            wrel(i)
                        if has_s:
                            post_s(ps[:, bs, 0:4 * NS].rearrange("p (c t) -> p c t", c=4), bs)
                            bput(bs)

                    conv_mm("cc",
                            lambda c, b: ACT("copy", dict(out=ccy[:, c, 0:BT], in_=ps[:, b, :]), r=ar(), w=aw(f"ccy{c}"), x=[BK(b)]),
                            lambda pv, bs: ACT("copy", dict(out=ccy[:, :, BT:TW], in_=pv), r=ar(), w=aw("ccy_s"), x=[BK(bs)]))

                    def post_ch(c, b):
                        DVE("tensor_tensor", dict(out=pbuf[:, c, 2:BT + 2], in0=ps[:, b, :], in1=ccy[:, c, 0:BT], op=ALU.mult),
                            r=ar(f"ccy{c}"), w=aw(f"pb{c}"), x=[BK(b)])
                        ACT("activation", dict(out=tmp1, in_=pbuf[:, c, 0:BT], func=AF.Identity, scale=convw[:, l, c, 0:1]),
                            r=["convw"] + ar(f"pb{c}"), w=aw("tmp1"))
                        DVE("scalar_tensor_tensor", dict(out=tmp2, in0=pbuf[:, c, 1:BT + 1], scalar=convw[:, l, c, 1:2], in1=tmp1, op0=ALU.mult, op1=ALU.add),
                            r=["convw"] + ar(f"pb{c}", "tmp1"), w=aw("tmp2"))
                        DVE("scalar_tensor_tensor", dict(out=ccy[:, c, 0:BT], in0=pbuf[:, c, 2:BT + 2], scalar=convw[:, l, c, 2:3], in1=tmp2, op0=ALU.mult, op1=ALU.add),
                            r=["convw"] + ar(f"pb{c}", "tmp2"), w=aw(f"ccy{c}"))

                    def post_ch_s(pv, bs):
                        DVE("tensor_tensor", dict(out=p_s[:], in0=pv, in1=ccy[:, :, BT:TW], op=ALU.mult), r=ar("ccy_s"), w=aw("p_s"), x=[BK(bs)])
                        for c in range(4):
                            ACT("activation", dict(out=tcs1, in_=scT[:, c, 0:2 * NS:2], func=AF.Identity, scale=convw[:, l, c, 0:1]),
                                r=["convw"] + ar("scT"), w=aw("tcs1"))
                            DVE("scalar_tensor_tensor", dict(out=tcs2, in0=scT[:, c, 1:2 * NS:2], scalar=convw[:, l, c, 1:2], in1=tcs1, op0=ALU.mult, op1=ALU.add),
                                r=["convw"] + ar("scT", "tcs1"), w=aw("tcs2"))
                            DVE("scalar_tensor_tensor", dict(out=ccy[:, c, BT:TW], in0=p_s[:, c, :], scalar=convw[:, l, c, 2:3], in1=tcs2, op0=ALU.mult, op1=ALU.add),
                                r=["convw"] + ar("p_s", "tcs2"), w=aw("ccy_s"))

                    conv_mm("ch", post_ch, post_ch_s)
                    DVE("tensor_copy", dict(out=carry[:, l, :, :], in_=pbuf[:, :, BT:BT + 2]), r=ar("pb0", "pb1", "pb2", "pb3"), w=["carry"])
                    if last_g:
                        b = bget()
                        for c in range(4):
                            tp(ps[0:2, b, c * 128:(c + 1) * 128], carry[:, l, c, :], ident_f[:], ["ident_f", "carry"], b)
                        ACT("copy", dict(out=cst, in_=ps[0:2, b, :]), r=ar(), w=aw("cst"), x=[BK(b)])
                        bput(b)
                        DMA("sp", "dma_start", dict(out=cp[l], in_=cst), r=ar("cst"), w=[("cp", l)])
                    if has_s:
                        b = bget()
                        for c in range(4):
                            tp(ps[0:NS, b, c * 128:(c + 1) * 128], p_s[:, c, :], ident_f[:], ["ident_f"] + ar("p_s"), b)
                        ACT("copy", dict(out=pst, in_=ps[0:NS, b, :]), r=ar(), w=aw("pst"), x=[BK(b)])
                        bput(b)
                        DMA("sp", "dma_start", dict(out=cs[l, :, 1, :], in_=pst), r=ar("pst"), w=[("cs1", l)])
                    conv_mm("cb",
                            lambda c, b: DVE("tensor_tensor", dict(out=ccy[:, c, 0:BT], in0=ps[:, b, :], in1=ccy[:, c, 0:BT], op=ALU.mult),
                                             r=ar(f"ccy{c}"), w=aw(f"ccy{c}"), x=[BK(b)]),
                            lambda pv, bs: DVE("tensor_tensor", dict(out=ccy[:, :, BT:TW], in0=pv, in1=ccy[:, :, BT:TW], op=ALU.mult),
                                               r=ar("ccy_s"), w=aw("ccy_s"), x=[BK(bs)]))

                    def post_gc(c, b):
                        ACT("activation", dict(out=tmp3, in_=ps[:, b, :], func=AF.Silu), r=ar(), w=aw("tmp3"), x=[BK(b)])
                        DVE("tensor_tensor", dict(out=yc[:, c, 0:BT], in0=ccy[:, c, 0:BT], in1=tmp3, op=ALU.mult), r=ar(f"ccy{c}", "tmp3"), w=[("yc", c)])

                    def post_gc_s(pv, bs):
                        ACT("activation", dict(out=tmp3s[:], in_=pv, func=AF.Silu), r=ar(), w=aw("tmp3s"), x=[BK(bs)])
                        DVE("tensor_tensor", dict(out=yc[:, :, BT:TW], in0=ccy[:, :, BT:TW], in1=tmp3s[:], op=ALU.mult), r=ar("ccy_s", "tmp3s"), w=["yc_s"])

                    conv_mm("gc", post_gc, post_gc_s)

                    ogr = [("og", j) for j in range(GT)]
                    ycr = [("yc", c) for c in range(4)]
                    for cpi in range(4):
                        ia, sa, ka = wtake(("pa", g, l, cpi))
                        ib, sbb, kb = wtake(("pb", g, l, cpi))
                        ig, sgw, kg = wtake(("mg", g, l, cpi))
                        ic, scw, kcw = wtake(("mc", g, l, cpi))
                        for sub in range(2):
                            c = cpi * 2 + sub
                            cs0, cs1 = sub * 128, (sub + 1) * 128
                            bA = bget()
                            for kc in range(NKC):
                                mm(ps[:, bA, :], sa[:, kc, cs0:cs1], og[:, kc, 0:BT], kc == 0, kc == NKC - 1, [ka] + ogr, bA)
                            bB = bget()
                            for kc in range(4):
                                mm(ps[:, bB, :], sbb[:, kc, cs0:cs1], yc[:, kc, 0:BT], kc == 0, kc == 3, [kb, ("yc", kc)], bB)
                            bG = bget()
                            for kc in range(NKC):
                                mm(ps[:, bG, :], sgw[:, kc, cs0:cs1], uT[:, kc, 0:BT], kc == 0, kc == NKC - 1, [kg, ("uT", kc)], bG)
                            bC = bget()
                            for kc in range(NKC):
                                mm(ps[:, bC, :], scw[:, kc, cs0:cs1], uT[:, kc, 0:BT], kc == 0, kc == NKC - 1, [kcw, ("uT", kc)], bC)
                            ACT("activation", dict(out=sgm, in_=ps[:, bG, :], func=AF.Sigmoid), r=ar(), w=aw("sgm"), x=[BK(bG)])
                            bput(bG)
                            ACT("activation", dict(out=scm, in_=ps[:, bC, :], func=AF.Sigmoid), r=ar(), w=aw("scm"), x=[BK(bC)])
                            bput(bC)
                            DVE("tensor_tensor", dict(out=tm1, in0=ps[:, bA, :], in1=sgm, op=ALU.mult), r=ar("sgm"), w=aw("tm1"), x=[BK(bA)])
                            bput(bA)
                            DVE("tensor_tensor", dict(out=tm2, in0=ps[:, bB, :], in1=scm, op=ALU.mult), r=ar("scm"), w=aw("tm2"), x=[BK(bB)])
                            bput(bB)
                            DVE("tensor_tensor", dict(out=mg[:, c, 0:BT], in0=tm1, in1=tm2, op=ALU.add), r=ar("tm1", "tm2"), w=aw(f"mg{c}"))
                            if has_s:
                                bS = bget()
                                for kc in range(NKC):
                                    mm(ps[:, bS, 0:NS], sa[:, kc, cs0:cs1], og[:, kc, BT:TW], kc == 0, kc == NKC - 1, [ka, "og_s"], bS)
                                for kc in range(4):
                                    mm(ps[:, bS, NS:2 * NS], sbb[:, kc, cs0:cs1], yc[:, kc, BT:TW], kc == 0, kc == 3, [kb, "yc_s"], bS)
                                for kc in range(NKC):
                                    mm(ps[:, bS, 2 * NS:3 * NS], sgw[:, kc, cs0:cs1], uT[:, kc, BT:TW], kc == 0, kc == NKC - 1, [kg, "uTs"], bS)
                                for kc in range(NKC):
                                    mm(ps[:, bS, 3 * NS:4 * NS], scw[:, kc, cs0:cs1], uT[:, kc, BT:TW], kc == 0, kc == NKC - 1, [kcw, "uTs"], bS)
                                ACT("activation", dict(out=sg_s, in_=ps[:, bS, 2 * NS:4 * NS], func=AF.Sigmoid), r=ar(), w=aw("sg_s2"), x=[BK(bS)])
                                DVE("tensor_tensor", dict(out=t_s, in0=ps[:, bS, 0:2 * NS], in1=sg_s, op=ALU.mult), r=ar("sg_s2"), w=aw("t_s"), x=[BK(bS)])
                                bput(bS)
                                DVE("tensor_tensor", dict(out=mg[:, c, BT:TW], in0=t_s[:, 0:NS], in1=t_s[:, NS:2 * NS], op=ALU.add), r=ar("t_s"), w=aw("mg_s"))
                        wrel(ia)
                        wrel(ib)
                        wrel(ig)
                        wrel(ic)

                    so = [wtake(("o", g, l, wi)) for wi in range(4)]
                    mgr = ar(*[f"mg{c}" for c in range(NKC)])

                    def o_s1(xv, gv, tov, stv, mvv, bo, npart, xkey, mk):
                        DVE("tensor_tensor", dict(out=tov.rearrange("p (b n) -> p b n", b=2), in0=ps[0:npart, bo:bo + 2, :],
                                                  in1=gv.rearrange("p (b n) -> p b n", b=2), op=ALU.mult),
                            r=["gate_bc", "gate_s"] + ar(), w=aw(mk + "to"), x=[BK(bo), BK(bo + 1)])
                        bput(bo, 2)
                        DVE("scalar_tensor_tensor", dict(out=xv, in0=xv, scalar=ALPHA, in1=tov, op0=ALU.mult, op1=ALU.add),
                            r=[xkey] + ar(mk + "to"), w=[xkey])
                        for hh in range(2):
                            DVE("bn_stats", dict(out=stv[:, hh, :], in_=xv[:, hh * 512:(hh + 1) * 512]), r=[xkey], w=[mk + "st"])
                        DVE("bn_aggr", dict(out=mvv[:, 0:2], in_=stv.rearrange("p a b -> p (a b)")), r=[mk + "st"], w=[mk + "mv", mk + "mean"])

                    def o_s2(mvv, mk):
                        ACT("activation", dict(out=mvv[:, 2:3], in_=mvv[:, 1:2], func=AF.Ln, bias=EPS), r=[mk + "mv"], w=[mk + "mv2"])
                        ACT("activation", dict(out=mvv[:, 3:4], in_=mvv[:, 2:3], func=AF.Exp, scale=-0.5), r=[mk + "mv2"], w=[mk + "mv3"])
                        DVE("tensor_scalar", dict(out=mvv[:, 4:5], in0=mvv[:, 0:1], scalar1=mvv[:, 3:4], scalar2=-1.0, op0=ALU.mult, op1=ALU.mult),
                            r=[mk + "mean", mk + "mv3"], w=[mk + "mv4"])

                    def o_s3(xv, mvv, lg, lb, xkey, mk):
                        ACT("activation", dict(out=xv, in_=xv, func=AF.Identity, bias=mvv[:, 4:5], scale=mvv[:, 3:4]),
                            r=[xkey, mk + "mv3", mk + "mv4"], w=[xkey])
                        DVE("tensor_tensor", dict(out=xv, in0=xv, in1=lg, op=ALU.mult), r=[xkey, "lng"], w=[xkey])
                        DVE("tensor_tensor", dict(out=xv, in0=xv, in1=lb, op=ALU.add), r=[xkey, "lnb"], w=[xkey])

                    if has_s:
                        bo = bget2()
                        for wi in range(4):
                            for kc in range(NKC):
                                mm(ps[0:NS, bo + wi // 2, (wi % 2) * WC:(wi % 2 + 1) * WC], mg[:, kc, BT:TW], so[wi][1][:, kc, :],
                                   kc == 0, kc == NKC - 1, [so[wi][2]] + ar("mg_s"), bo + wi // 2)
                        o_s1(xs_sb[:], gate_s[:, l, :], to_s, st_s[:], mv_s[:], bo, NS, "xs", "s")
                        o_s2(mv_s[:], "s")
                        o_s3(xs_sb[:], mv_s[:], lng_bc[0:NS, :], lnb_bc[0:NS, :], "xs", "s")

                    def o_mm_s1(j):
                        bo = bget2()
                        for wi in range(4):
                            for kc in range(NKC):
                                mm(ps[:, bo + wi // 2, (wi % 2) * WC:(wi % 2 + 1) * WC], mg[:, kc, j * 128:(j + 1) * 128], so[wi][1][:, kc, :],
                                   kc == 0, kc == NKC - 1, [so[wi][2]] + mgr, bo + wi // 2)
                        o_s1(x_sb[:, j, :], gate_bc[:, l, :], to2[:, j % 2, :], st2[:, j % 2], mv[:, (j % 2) * 8:(j % 2) * 8 + 8], bo, 128, ("x", j), f"p{j % 2}")

                    def o_2(j):
                        o_s2(mv[:, (j % 2) * 8:(j % 2) * 8 + 8], f"p{j % 2}")

                    def o_3(j):
                        o_s3(x_sb[:, j, :], mv[:, (j % 2) * 8:(j % 2) * 8 + 8], lng_bc[:], lnb_bc[:], ("x", j), f"p{j % 2}")
                        if l == nl_run - 1:
                            DMA("sp", "dma_start", dict(out=yp[g * BT + j * 128: g * BT + (j + 1) * 128, :], in_=x_sb[:, j, :]),
                                r=[("x", j)], w=[("yp", g, j)])

                    o_mm_s1(0)
                    o_mm_s1(1)
                    o_2(0)
                    o_3(0)
                    o_mm_s1(2)
                    o_2(1)
                    o_3(1)
                    o_mm_s1(3)
                    o_2(2)
                    o_3(2)
                    o_2(3)
                    o_3(3)
                    for wi in range(4):
                        wrel(so[wi][0])
                    if l == nl_run - 1:
                        pass
                        if has_s:
                            DMA("sp", "dma_start", dict(out=ys, in_=xs_sb[:]), r=["xs"], w=["ys"])

            assert wstate["taken"] == len(wlist), (wstate, len(wlist))
            return wlist

        wl = emit_all(Prog(), True, None)
        P = Prog()
        emit_all(P, False, wl)
        P.finalize()

        @block.tensor
        def _(e):
            P.run_engine("pe", e, sems, dma_sems)

        @block.scalar
        def _(e):
            P.run_engine("act", e, sems, dma_sems)

        @block.vector
        def _(e):
            P.run_engine("dve", e, sems, dma_sems)

        @block.gpsimd
        def _(e):
            P.run_engine("pool", e, sems, dma_sems)

        @block.sync
        def _(e):
            P.run_engine("sp", e, sems, dma_sems)
            P.final_wait(e, dma_sems)

    return nc


_NC_CACHE = {}


def kernel(x_prompt, x_sample, c_prompt, c_sample, state_gla, state_conv, w_ada, b_ada, w_in, w_a2, b_a,
           gla_norm_g, conv_w, w_pa, w_pb, w_o, ln_g, ln_b):
    f = lambda a: np.ascontiguousarray(np.asarray(a, dtype=np.float32))
    x_prompt, x_sample, c_prompt, c_sample = f(x_prompt), f(x_sample), f(c_prompt), f(c_sample)
    state_gla, state_conv = f(state_gla), f(state_conv)
    shared = dict(w_ada=f(w_ada), b_ada=f(b_ada), w_in=f(w_in), w_a2=f(w_a2), b_a=f(b_a), gng=f(gla_norm_g),
                  conv_w=f(conv_w), w_pa=f(w_pa), w_pb=f(w_pb), w_o=f(w_o), ln_g=f(ln_g), ln_b=f(ln_b))
    ncores = 8
    in_maps = []
    for i in range(ncores):
        sl = slice(i * NS, (i + 1) * NS)
        m = dict(shared)
        m["xp"] = x_prompt[i]
        m["xs"] = np.ascontiguousarray(x_sample[sl, 0, :])
        m["c17"] = np.ascontiguousarray(np.concatenate([c_prompt[i:i + 1], c_sample[sl]], axis=0))
        m["sgl"] = np.ascontiguousarray(state_gla[:, sl])
        m["scv"] = np.ascontiguousarray(state_conv[:, sl])
        in_maps.append(m)
    if "nc" not in _NC_CACHE:
        _NC_CACHE["nc"] = build_program()
    res = run_bass_kernel_spmd(_NC_CACHE["nc"], in_maps, core_ids=list(range(ncores)))
    R = res.results
    y_prompt = np.stack([R[i]["yp"] for i in range(ncores)], axis=0)
    y_sample = np.concatenate([R[i]["ys"] for i in range(ncores)], axis=0)[:, None, :]
    gla_p = np.stack([R[i]["gp"] for i in range(ncores)], axis=1)
    conv_p = np.stack([R[i]["cp"] for i in range(ncores)], axis=1)
    gla_s = np.concatenate([R[i]["gs"] for i in range(ncores)], axis=1)
    conv_s = np.concatenate([R[i]["cs"] for i in range(ncores)], axis=1)
    return (y_prompt.astype(np.float32), y_sample.astype(np.float32), gla_p.astype(np.float32),
            conv_p.astype(np.float32), gla_s.astype(np.float32), conv_s.astype(np.float32))
```
